# Optimizing a Trainium2 kernel written in Bass

```python
import math
import jax, jax.numpy as jnp
from jax import lax
import numpy as np

D_MODEL = 1024
BATCH = 2
SEQ = 8192
DEPTH = 1

GRID_W = 64
CTX_LEN = 256
RET_HEADS = 8
RET_QK_DIM = 64
RET_V_DIM = 128
RET_QK_WIDTH = RET_HEADS * RET_QK_DIM
RET_V_WIDTH = RET_HEADS * RET_V_DIM
CHUNK = 128
POOL_WINDOWS = (2, 4, 8, 16)
POOL_GROUPS = len(POOL_WINDOWS)
POOL_GROUP_DIM = 128
POOL_WIDTH = POOL_GROUPS * POOL_GROUP_DIM
D_FF = 2816
CONV_K = 3
N_MOD = 6
LN_EPS = 1e-6
DEEPNORM_ALPHA = (2.0 * DEPTH) ** 0.25
DEEPNORM_BETA = (8.0 * DEPTH) ** -0.25
IN_SIZES = (RET_QK_WIDTH, RET_V_WIDTH, RET_QK_WIDTH, RET_V_WIDTH, POOL_WIDTH, D_MODEL, D_MODEL)
IN_WIDTH = sum(IN_SIZES)
KV_COLS = RET_QK_WIDTH + RET_V_WIDTH

kernel_name = "hybrid_retention_pool_convffn_dit"


def _layer_norm(x):
    xf = x.astype(jnp.float32)
    mu = jnp.mean(xf, axis=-1, keepdims=True)
    var = jnp.mean(jnp.square(xf - mu), axis=-1, keepdims=True)
    return ((xf - mu) * lax.rsqrt(var + LN_EPS)).astype(x.dtype)


def _modulate(x, shift, scale):
    return _layer_norm(x) * (1.0 + scale) + shift


def _post_norm(z, g, b):
    return _layer_norm(z) * g + b


def _split_in(proj):
    offs = np.cumsum((0,) + IN_SIZES)
    return [proj[..., int(offs[i]):int(offs[i + 1])] for i in range(len(IN_SIZES))]


def _ret_kv(k, v):
    kh = k.reshape(k.shape[0], k.shape[1], RET_HEADS, RET_QK_DIM).astype(jnp.float32) * RET_QK_DIM ** -0.5
    vh = v.reshape(v.shape[0], v.shape[1], RET_HEADS, RET_V_DIM).astype(jnp.float32)
    return kh, vh


def _context_states(kh, vh, lg):
    Lc = kh.shape[1]
    pos = jnp.arange(Lc, dtype=jnp.float32)
    w_f = jnp.exp((Lc - 1 - pos)[:, None] * lg[0][None, :])
    w_b = jnp.exp(pos[:, None] * lg[1][None, :])
    s_f = jnp.einsum('bjhd,jh,bjhe->bhde', kh, w_f, vh)
    s_b = jnp.einsum('bjhd,jh,bjhe->bhde', kh, w_b, vh)
    return s_f, s_b


def _retention_scan(q, k, v, log_gamma, s0):
    B, L, H, dk = q.shape
    dv = v.shape[-1]
    n = L // CHUNK
    qc = q.reshape(B, n, CHUNK, H, dk)
    kc = k.reshape(B, n, CHUNK, H, dk)
    vc = v.reshape(B, n, CHUNK, H, dv)
    pos = jnp.arange(CHUNK, dtype=jnp.float32)
    diff = pos[:, None] - pos[None, :]
    decay_in = jnp.where(diff[None] >= 0,
                         jnp.exp(jnp.maximum(diff, 0.0)[None] * log_gamma[:, None, None]), 0.0)
    scores = jnp.einsum('bnihd,bnjhd->bnhij', qc, kc) * decay_in
    intra = jnp.einsum('bnhij,bnjhe->bnihe', scores, vc)
    q_dec = jnp.exp((pos + 1.0)[:, None] * log_gamma[None, :])
    k_dec = jnp.exp((CHUNK - 1.0 - pos)[:, None] * log_gamma[None, :])
    chunk_dec = jnp.exp(CHUNK * log_gamma)[None, :, None, None]
    kv = jnp.einsum('bnjhd,jh,bnjhe->nbhde', kc, k_dec, vc)

    def step(s, kv_n):
        return chunk_dec * s + kv_n, s

    _, s_prev = lax.scan(step, s0.astype(jnp.float32), kv)
    cross = jnp.einsum('bnihd,ih,nbhde->bnihe', qc, q_dec, s_prev)
    return (intra + cross).reshape(B, L, H, dv)


def _retention_branch(q, k, v, g, s_f, s_b, lg):
    B, L, _ = q.shape
    qh = q.reshape(B, L, RET_HEADS, RET_QK_DIM).astype(jnp.float32)
    kh, vh = _ret_kv(k, v)
    fwd = _retention_scan(qh, kh, vh, lg[0], s_f)
    bwd = jnp.flip(_retention_scan(jnp.flip(qh, 1), jnp.flip(kh, 1), jnp.flip(vh, 1), lg[1], s_b), 1)
    y = fwd + bwd
    mu = jnp.mean(y, axis=-1, keepdims=True)
    var = jnp.mean(jnp.square(y - mu), axis=-1, keepdims=True)
    y = ((y - mu) * lax.rsqrt(var + LN_EPS)).reshape(B, L, RET_V_WIDTH)
    return (y * jax.nn.silu(g.astype(jnp.float32))).astype(g.dtype)


def _pool_mixer(p, pool_w, pool_scale):
    B, L, _ = p.shape
    pf = p.astype(jnp.float32)
    csum = jnp.concatenate([jnp.zeros((B, 1, POOL_WIDTH), jnp.float32), jnp.cumsum(pf, axis=1)], axis=1)
    t = jnp.arange(L)
    outs = []
    for gi, w in enumerate(POOL_WINDOWS):
        lo = jnp.clip(t - w // 2, 0, L)
        hi = jnp.clip(t + w // 2, 0, L)
        cs = csum[:, :, gi * POOL_GROUP_DIM:(gi + 1) * POOL_GROUP_DIM]
        mean = (cs[:, hi] - cs[:, lo]) / (hi - lo).astype(jnp.float32)[None, :, None]
        diff = (mean - pf[:, :, gi * POOL_GROUP_DIM:(gi + 1) * POOL_GROUP_DIM]).astype(p.dtype)
        outs.append(diff @ pool_w[gi])
    return jnp.concatenate(outs, axis=-1) * pool_scale


def _token_mixer(proj, s_f, s_b, lg, pool_w, pool_scale, w_branch_ret, w_branch_pool, w_out):
    k, v, q, g, p_in, gate_a, gate_b = _split_in(proj)
    ret = _retention_branch(q, k, v, g, s_f, s_b, lg) @ w_branch_ret
    pool = _pool_mixer(p_in, pool_w, pool_scale) @ w_branch_pool
    merged = jax.nn.sigmoid(gate_a) * ret + jax.nn.sigmoid(gate_b) * pool
    return merged @ w_out


def _conv_ffn(u, rows, cols, w_up, conv_w, conv_b, w_down):
    B, L, _ = u.shape
    h = (u @ w_up).reshape(B, rows, cols, 2 * D_FF)
    h = lax.conv_general_dilated(h, conv_w[:, :, None, :], (1, 1), 'SAME',
                                 dimension_numbers=('NHWC', 'HWIO', 'NHWC'),
                                 feature_group_count=2 * D_FF) + conv_b
    a, b = jnp.split(h.reshape(B, L, 2 * D_FF), 2, axis=-1)
    return (jax.nn.gelu(a) * b) @ w_down


def _layer(x, ctx, c, c_ctx, w_ada, b_ada, w_in, ret_decay_logit, pool_w, pool_scale,
           w_branch_ret, w_branch_pool, w_out, ln1_g, ln1_b, w_up, conv_w, conv_b, w_down,
           ln2_g, ln2_b, update_ctx):
    B, L, _ = x.shape
    rows = L // GRID_W
    Lc = ctx.shape[1]
    mod = jax.nn.silu(c) @ w_ada + b_ada
    mod_c = jax.nn.silu(c_ctx) @ w_ada + b_ada
    sh1, sc1, g1, sh2, sc2, g2 = jnp.split(mod[:, None, :], N_MOD, axis=-1)
    sh1c, sc1c, g1c, sh2c, sc2c, g2c = jnp.split(mod_c, N_MOD, axis=-1)
    lg = jax.nn.log_sigmoid(ret_decay_logit.astype(jnp.float32))

    uc = _modulate(ctx, sh1c, sc1c)
    if update_ctx:
        projc = uc @ w_in
        kvc = projc[..., :KV_COLS]
    else:
        kvc = uc @ w_in[:, :KV_COLS]
    kc, vc = _ret_kv(kvc[..., :RET_QK_WIDTH], kvc[..., RET_QK_WIDTH:])
    s_f, s_b = _context_states(kc, vc, lg)

    u = _modulate(x, sh1, sc1)
    mix = _token_mixer(u @ w_in, s_f, s_b, lg, pool_w, pool_scale, w_branch_ret, w_branch_pool, w_out)
    x = _post_norm(DEEPNORM_ALPHA * x + g1 * mix, ln1_g, ln1_b)
    u2 = _modulate(x, sh2, sc2)
    x = _post_norm(DEEPNORM_ALPHA * x + g2 * _conv_ffn(u2, rows, GRID_W, w_up, conv_w, conv_b, w_down),
                   ln2_g, ln2_b)

    if update_ctx:
        zero_state = jnp.zeros((B, RET_HEADS, RET_QK_DIM, RET_V_DIM), jnp.float32)
        mix_c = _token_mixer(projc, zero_state, zero_state, lg, pool_w, pool_scale,
                             w_branch_ret, w_branch_pool, w_out)
        ctx = _post_norm(DEEPNORM_ALPHA * ctx + g1c * mix_c, ln1_g, ln1_b)
        u2c = _modulate(ctx, sh2c, sc2c)
        ctx = _post_norm(DEEPNORM_ALPHA * ctx + g2c * _conv_ffn(u2c, 1, Lc, w_up, conv_w, conv_b, w_down),
                         ln2_g, ln2_b)
    return x, ctx


def setup_inputs(seed: int = 0) -> dict:
    key = jax.random.key(seed)
    ks = jax.random.split(key, 24)
    D = D_MODEL
    nrm = lambda k, shape, s: jax.random.normal(k, shape, jnp.float32) * s
    base_logit = jnp.log(2.0 ** (5.0 + jnp.arange(RET_HEADS, dtype=jnp.float32)) - 1.0)
    return {
        "x": nrm(ks[0], (BATCH, SEQ, D), 1.0),
        "c": nrm(ks[1], (BATCH, D), 1.0),
        "ctx": nrm(ks[2], (BATCH, CTX_LEN, D), 1.0),
        "c_ctx": nrm(ks[3], (D,), 1.0),
        "w_ada": nrm(ks[4], (DEPTH, D, N_MOD * D), D ** -0.5),
        "b_ada": nrm(ks[5], (DEPTH, N_MOD * D), 0.02),
        "w_in": nrm(ks[6], (DEPTH, D, IN_WIDTH), D ** -0.5),
        "ret_decay_logit": base_logit[None, None, :] + nrm(ks[7], (DEPTH, 2, RET_HEADS), 0.05),
        "pool_w": nrm(ks[8], (DEPTH, POOL_GROUPS, POOL_GROUP_DIM, POOL_GROUP_DIM), POOL_GROUP_DIM ** -0.5),
        "pool_scale": 1.0 + nrm(ks[9], (DEPTH, POOL_WIDTH), 0.1),
        "w_branch_ret": nrm(ks[10], (DEPTH, RET_V_WIDTH, D), RET_V_WIDTH ** -0.5),
        "w_branch_pool": nrm(ks[11], (DEPTH, POOL_WIDTH, D), POOL_WIDTH ** -0.5),
        "w_out": nrm(ks[12], (DEPTH, D, D), D ** -0.5 * DEEPNORM_BETA),
        "ln1_g": 1.0 + nrm(ks[13], (DEPTH, D), 0.02),
        "ln1_b": nrm(ks[14], (DEPTH, D), 0.02),
        "w_up": nrm(ks[15], (DEPTH, D, 2 * D_FF), D ** -0.5),
        "conv_w": nrm(ks[16], (DEPTH, CONV_K, CONV_K, 2 * D_FF), 1.0 / CONV_K),
        "conv_b": nrm(ks[17], (DEPTH, 2 * D_FF), 0.02),
        "w_down": nrm(ks[18], (DEPTH, D_FF, D), D_FF ** -0.5 * DEEPNORM_BETA),
        "ln2_g": 1.0 + nrm(ks[19], (DEPTH, D), 0.02),
        "ln2_b": nrm(ks[20], (DEPTH, D), 0.02),
    }


def reference(x, c, ctx, c_ctx, w_ada, b_ada, w_in, ret_decay_logit, pool_w, pool_scale,
              w_branch_ret, w_branch_pool, w_out, ln1_g, ln1_b, w_up, conv_w, conv_b, w_down,
              ln2_g, ln2_b):
    for l in range(DEPTH):
        x, ctx = _layer(x, ctx, c, c_ctx, w_ada[l], b_ada[l], w_in[l], ret_decay_logit[l], pool_w[l],
                        pool_scale[l], w_branch_ret[l], w_branch_pool[l], w_out[l], ln1_g[l], ln1_b[l],
                        w_up[l], conv_w[l], conv_b[l], w_down[l], ln2_g[l], ln2_b[l],
                        update_ctx=(l < DEPTH - 1))
    return x
```

```python
import numpy as np
import concourse.bass as bass
import concourse.mybir as mybir
from concourse.bass_utils import run_bass_kernel_spmd

F32 = mybir.dt.float32
BF16 = mybir.dt.bfloat16
AF = mybir.ActivationFunctionType
ALU = mybir.AluOpType

STAGE = 3
NCORE = 8
import os
CUT = int(os.environ.get('KCUT', '99'))


class _Stop(Exception):
    pass

D = 1024
SEG = 2048
NCH = 16
EPS = 1e-6
ALPHA = 2.0 ** 0.25
NDS = 24
GROUPS = [[0, 1, 2, 3], [4, 5, 6, 7]]

_CT = {}
_off = 0
for _n, _w in [("IDENT", 128), ("D1", 128), ("D2", 128), ("EYE8", 128), ("IOTA1", 128), ("IOTA2", 128),
               ("ONES", 128), ("JREV", 1), ("JFWD", 1), ("CEXP", 17), ("NEXPF", 5), ("MASKF", 5),
               ("NEXPB", 5), ("MASKB", 5), ("PMASK", 2), ("PE", 1), ("PO", 1), ("EDGEL", 32), ("EDGER", 32), ("OH", 4), ("HM", 2)]:
    _CT[_n] = (_off, _w)
    _off += _w
NCT = _off


def _const_table(core):
    s = core % 4
    t = np.zeros((128, NCT), np.float32)

    def put(name, arr):
        o, w = _CT[name]
        t[:, o:o + w] = arr

    p = np.arange(128)[:, None].astype(np.float32)
    i = np.arange(128)[None, :].astype(np.float32)
    put("IDENT", np.eye(128, dtype=np.float32))
    put("D1", np.maximum(i - p, 0))
    put("D2", np.maximum(p - i, 0))
    put("EYE8", 0.125 * np.eye(128, dtype=np.float32))
    put("IOTA1", np.broadcast_to(i + 1, (128, 128)))
    put("IOTA2", np.broadcast_to(128 - i, (128, 128)))
    put("ONES", np.ones((128, 128), np.float32))
    put("JREV", 127 - p)
    put("JFWD", p)
    put("CEXP", np.broadcast_to(128.0 * np.arange(17)[None, :], (128, 17)))
    nf = np.zeros(5); mf = np.zeros(5); nb = np.zeros(5); mb = np.zeros(5)
    for r in range(4):
        if r < s:
            nf[r] = s - 1 - r; mf[r] = 1
        if r > s:
            nb[r] = r - s - 1; mb[r] = 1
    nf[4] = s; mf[4] = 1; nb[4] = 3 - s; mb[4] = 1
    put("NEXPF", np.broadcast_to(2048.0 * nf[None, :], (128, 5)))
    put("MASKF", np.broadcast_to(mf[None, :], (128, 5)))
    put("NEXPB", np.broadcast_to(2048.0 * nb[None, :], (128, 5)))
    put("MASKB", np.broadcast_to(mb[None, :], (128, 5)))
    put("PE", (p < 64).astype(np.float32))
    put("PO", (p >= 64).astype(np.float32))
    put("PMASK", np.broadcast_to(np.array([1.0 if s > 0 else 0.0, 1.0 if s < 3 else 0.0])[None, :], (128, 2)))
    L = 8192
    el = np.ones((4, 8)); er = np.ones((4, 8))
    for g, w in enumerate((2, 4, 8, 16)):
        for j in range(8):
            tpos = s * SEG + j
            cnt = min(tpos + w // 2, L) - max(tpos - w // 2, 0)
            el[g, j] = w / cnt
            tpos = s * SEG + SEG - 8 + j
            cnt = min(tpos + w // 2, L) - max(tpos - w // 2, 0)
            er[g, j] = w / cnt
    put("EDGEL", np.broadcast_to(el.reshape(1, 32), (128, 32)))
    put("EDGER", np.broadcast_to(er.reshape(1, 32), (128, 32)))
    oh = np.zeros((128, 4))
    if s > 0:
        oh[0:64, s - 1] = 1
    if s < 3:
        oh[64:128, s + 1] = 1
    put("OH", oh)
    put("HM", np.broadcast_to(np.array([1.0 if s > 0 else 0.0, 1.0 if s < 3 else 0.0])[None, :], (128, 2)))
    return t


class Rec:
    def __init__(self):
        self.ops = []

    def add(self, eng, fn, r=(), w=(), kind="c"):
        self.ops.append(dict(eng=eng, fn=fn, r=tuple(r), w=tuple(w), kind=kind))
        return len(self.ops) - 1

    def barrier(self, keep=()):
        self.ops.append(dict(eng="*", fn=None, r=(), w=(), kind="bar", keep=tuple(keep)))

    def emit(self, nc, block, sems, dsems, ccsem):
        ops = self.ops
        n = len(ops)
        deps = [set() for _ in range(n)]
        lastw = {}
        readers = {}
        last_eng = {}
        dma_since = []
        frontier = set()
        dma_slot_last = {}
        ndma = 0
        npool = 0
        slot_of = {}
        last_cc = []
        for i, o in enumerate(ops):
            if o["kind"] == "bar":
                keep = o.get("keep", ())
                frontier = set(last_eng.values())
                for q in dma_slot_last.values():
                    if keep and ops[q]["w"] and all(k in keep for k in ops[q]["w"]):
                        continue
                    frontier.add(q)
                if not keep:
                    frontier |= set(last_cc)
                kw_ = {k: lastw[k] for k in keep if k in lastw}
                kr_ = {k: readers[k] for k in keep if k in readers}
                lastw.clear(); readers.clear()
                lastw.update(kw_); readers.update(kr_)
                continue
            dp = deps[i]
            dp |= frontier
            for k in o["r"]:
                if k in lastw:
                    dp.update(lastw[k])
            for k in o["w"]:
                if k in lastw:
                    same_burst = (o["kind"] == "dma" and all(ops[q]["kind"] == "dma" for q in lastw[k])
                                  and not readers.get(k))
                    if not same_burst:
                        dp.update(lastw[k])
                for rr in readers.get(k, ()):
                    dp.add(rr)
            for k in o["w"]:
                if o["kind"] == "dma" and k in lastw and all(ops[q]["kind"] == "dma" for q in lastw[k]) and not readers.get(k):
                    lastw[k] = lastw[k] + [i]
                else:
                    lastw[k] = [i]
                readers[k] = []
            for k in o["r"]:
                lst = readers.setdefault(k, [])
                if o["kind"] == "c":
                    lst[:] = [q for q in lst if not (ops[q]["kind"] == "c" and ops[q]["eng"] == o["eng"])]
                lst.append(i)
            if o["kind"] == "dma":
                half = NDS // 2
                if o["eng"] == "pool":
                    sl = half + (npool % half)
                    npool += 1
                else:
                    sl = ndma % half
                    ndma += 1
                slot_of[i] = sl
                if sl in dma_slot_last:
                    dp.add(dma_slot_last[sl])
                dma_slot_last[sl] = i
            elif o["kind"] == "c":
                last_eng[o["eng"]] = i
            elif o["kind"] == "cc":
                last_cc = [i]
            dp.discard(i)
        hasdep = [False] * n
        for i in range(n):
            for d in deps[i]:
                hasdep[d] = True
        tok = [None] * n
        cnt = {e: 0 for e in sems}
        dcnt = [0] * NDS
        cccnt = 0
        pos_in_eng = [0] * n
        epos = {}
        for i, o in enumerate(ops):
            if o["kind"] == "bar":
                continue
            if o["kind"] == "dma":
                sl = slot_of[i]
                dcnt[sl] += 16
                tok[i] = (dsems[sl], dcnt[sl], 16)
            elif o["kind"] == "cc":
                cccnt += 1
                tok[i] = (ccsem, cccnt, 1)
            else:
                e = o["eng"]
                epos[e] = epos.get(e, 0) + 1
                pos_in_eng[i] = epos[e]
                if hasdep[i]:
                    cnt[e] += 1
                    tok[i] = (sems[e], cnt[e], 1)
        final_d = list(dcnt)

        def run_engine(ename, eh):
            waited = {}
            for i, o in enumerate(ops):
                if o["kind"] == "bar" or o["eng"] != ename:
                    continue
                for d in sorted(deps[i]):
                    od = ops[d]
                    t = tok[d]
                    if t is None:
                        continue
                    if od["kind"] == "c" and od["eng"] == ename:
                        if ename == "pe":
                            continue
                        if o["kind"] == "c" and pos_in_eng[i] - pos_in_eng[d] > 3:
                            continue
                    sem, val, _ = t
                    if waited.get(sem.num, 0) < val:
                        eh.wait_ge(sem, val)
                        waited[sem.num] = val
                ins = o["fn"](eh)
                t = tok[i]
                if t is not None:
                    ins.then_inc(t[0], t[2])
            if ename == "sp":
                for sl in range(NDS):
                    if final_d[sl] > 0:
                        eh.wait_ge(dsems[sl], final_d[sl])

        block.sync(lambda e: run_engine("sp", e))
        block.tensor(lambda e: run_engine("pe", e))
        block.scalar(lambda e: run_engine("act", e))
        block.vector(lambda e: run_engine("dve", e))
        block.gpsimd(lambda e: run_engine("pool", e))


def build():
    nc = bass.Bass("TRN2", target_bir_lowering=False)
    R = Rec()

    def din(name, shape, dt=F32):
        return nc.dram_tensor(name, list(shape), dt, kind="ExternalInput").ap()

    xs = din("xs", [SEG, D]); xh = din("xh", [16, D]); ctxs = din("ctxs", [256, D])
    cfm = din("cfm", [128, 16]); ct_d = din("ct", [128, NCT]); lgt = din("lgt", [1, 16])
    w_ada = din("w_ada", [D, 6 * D]); bada = din("bada", [128, 48])
    w_in = din("w_in", [D, 5632]); pool_w = din("pool_w", [4, 128, 128]); psc = din("psc", [128, 4])
    w_br = din("w_br", [D, D]); w_bp = din("w_bp", [512, D]); w_out = din("w_out", [D, D])
    ln1g = din("ln1g", [1, D]); ln1b = din("ln1b", [1, D]); ln2g = din("ln2g", [1, D]); ln2b = din("ln2b", [1, D])
    w_up = din("w_up", [D, 5632]); cwf = din("cwf", [128, 44 * 9]); cbf = din("cbf", [128, 44])
    w_down = din("w_down", [2816, D])
    out_d = nc.dram_tensor("out", [SEG, D], F32, kind="ExternalOutput").ap()
    x1d = nc.dram_tensor("x1d", [SEG, D], F32).ap()
    ccin = nc.dram_tensor("ccin", [128, 1024], F32)
    ccout = nc.dram_tensor("ccout", [4 * 128, 1024], F32)
    ffin = nc.dram_tensor("ffin", [128, 1024], F32)
    ffout = nc.dram_tensor("ffout", [4 * 128, 1024], F32)
    hxd = nc.dram_tensor("hxd", [128, 1024], F32)

    w_in_v = w_in.rearrange("(kc p) n -> p kc n", p=128)
    w_up_v = w_up.rearrange("(kc p) n -> p kc n", p=128)
    w_ada_v = w_ada.rearrange("(kc p) n -> p kc n", p=128)

    cur = [16640]

    def alloc(name, shape, dt):
        nb = int(np.prod(shape[1:])) * (4 if dt == F32 else 2)
        nb = (nb + 31) // 32 * 32
        assert cur[0] + nb <= 229120, (name, cur[0], nb)
        t = nc.alloc_sbuf_tensor_at(name, list(shape), dt, offset=cur[0])
        cur[0] += nb
        return t

    CT = alloc("CT", [128, NCT], F32)

    def ctc(name, a=0, b=None):
        o, w = _CT[name]
        return CT[:, o + a:o + (w if b is None else b)]

    IDB = alloc("IDB", [128, 128], BF16)
    LG = alloc("LG", [128, 16], F32)
    LGP = alloc("LGP", [128, 8], F32)
    SCR = alloc("SCR", [128, 6, 16], F32)
    MT = alloc("MT", [128, 8, 128], F32)
    QFM = alloc("QFM", [128, 4, 4, 128], F32)
    KF = alloc("KF", [128, 8], F32)
    KB = alloc("KB", [128, 8], F32)
    CDX = alloc("CDX", [128, 2, 4, 128], F32)
    CD = alloc("CD", [128, 8], F32)
    CDPOW = alloc("CDPOW", [128, 8, 17], F32)
    COEF = alloc("COEF", [128, 8, 5], F32)
    CFM = alloc("CFM", [128, 16], F32)
    SC = alloc("SC", [128, 16], BF16)
    BADA = alloc("BADA", [128, 48], F32)
    MODT = alloc("MODT", [128, 48, 2], F32)
    OPSC1 = alloc("OPSC1", [128, 8, 2], F32)
    OPSC2 = alloc("OPSC2", [128, 8], F32)
    PSC = alloc("PSC", [128, 4], F32)
    CW = alloc("CW", [128, 44, 9], F32)
    CB = alloc("CB", [128, 44], F32)
    SST = alloc("SST", [128, 2, 4, 128], F32)
    SRUN = alloc("SRUN", [128, 4, 128], F32)
    LNS = alloc("LNS", [128, 2, 16], F32)
    LNM = alloc("LNM", [128, 2, 4], F32)
    W_BASE = cur[0]
    W_SIZE = 34816
    cur[0] += W_SIZE
    BIG_BASE = cur[0]
    WB_BASE = 229120 - 45056

    def arena(base):
        cur[0] = base

    PS = nc.alloc_psum_tensor("PS", [128, 4096], F32)

    def bank(b, n=1):
        return PS[:, b * 512:(b + n) * 512]

    def bkeys(b, n=1):
        return ["ps%d" % k for k in range(b, b + n)]

    def A(eng, fn, r=(), w=()):
        return R.add(eng, fn, r, w)

    def DMA(eng, out, in_, r=(), w=()):
        return R.add(eng, lambda e, o=out, i=in_: e.dma_start(out=o, in_=i), r, w, kind="dma")

    def mm(out, lhsT, rhs, start, stop, r, w):
        return R.add("pe", lambda e, o=out, l=lhsT, rh=rhs, s=start, p=stop: e.matmul(o, l, rh, start=s, stop=p), r, w)

    def act(out, in_, func, r, w, bias=0.0, scale=1.0):
        return R.add("act", lambda e, o=out, i=in_, f=func, b=bias, s=scale: e.activation(out=o, in_=i, func=f, bias=b, scale=s), r, w)

    def ts(eng, out, in0, s1, s2, op0, op1, r, w):
        if s2 is None:
            return R.add(eng, lambda e, o=out, i=in0, a=s1, p0=op0: e.tensor_scalar(out=o, in0=i, scalar1=a, scalar2=None, op0=p0), r, w)
        return R.add(eng, lambda e, o=out, i=in0, a=s1, b=s2, p0=op0, p1=op1: e.tensor_scalar(out=o, in0=i, scalar1=a, scalar2=b, op0=p0, op1=p1), r, w)

    def tt(eng, out, in0, in1, op, r, w):
        return R.add(eng, lambda e, o=out, a=in0, b=in1, p=op: e.tensor_tensor(out=o, in0=a, in1=b, op=p), r, w)

    def stt(eng, out, in0, scalar, in1, op0, op1, r, w):
        return R.add(eng, lambda e, o=out, a=in0, s=scalar, b=in1, p0=op0, p1=op1: e.scalar_tensor_tensor(out=o, in0=a, scalar=s, in1=b, op0=p0, op1=p1), r, w)

    def cp(eng, out, in_, r, w):
        return R.add(eng, lambda e, o=out, i=in_: e.tensor_copy(out=o, in_=i), r, w)


    def cut(n):
        if CUT == n:
            raise _Stop()

    try:
        DMA("sp", CT[:, :], ct_d, w=["CT"])
        DMA("sp", LG[:, :], lgt.partition_broadcast(128).rearrange("p o n -> p (o n)"), w=["LG"])
        DMA("sp", CFM[:, :], cfm, w=["CFM"])
        DMA("sp", BADA[:, :], bada, w=["BADA"])
        DMA("sp", PSC[:, :], psc, w=["PSC"])
        DMA("sp", CW[:, :, :].rearrange("p a b -> p (a b)"), cwf, w=["CW"])
        DMA("sp", CB[:, :], cbf, w=["CB"])
        cp("dve", IDB[:, :], ctc("IDENT"), ["CT"], ["IDB"])
        T0 = SCR[:, 0, :]; T1 = SCR[:, 1, :]; T2 = SCR[:, 2, :]; T3 = SCR[:, 3, :]
        act(T0, LG[:, :], AF.Exp, ["LG"], ["T0"], scale=-1.0)
        ts("dve", T1, T0, 2.0, None, ALU.add, None, ["T0"], ["T1"])
        R.add("dve", lambda e: e.reciprocal(out=T1, in_=T1), ["T1"], ["T1"])
        tt("dve", T1, T0, T1, ALU.mult, ["T0", "T1"], ["T1"])
        tt("dve", T2, T1, T1, ALU.mult, ["T1"], ["T2"])
        ts("dve", T3, T2, 1.0 / 15, 1.0 / 13, ALU.mult, ALU.add, ["T2"], ["T3"])
        for cst in (1.0 / 11, 1.0 / 9, 1.0 / 7, 1.0 / 5, 1.0 / 3, 1.0):
            tt("dve", T3, T3, T2, ALU.mult, ["T3", "T2"], ["T3"])
            ts("dve", T3, T3, cst, None, ALU.add, None, ["T3"], ["T3"])
        tt("dve", T3, T3, T1, ALU.mult, ["T3", "T1"], ["T3"])
        ts("dve", LG[:, :], T3, -2.0, None, ALU.mult, None, ["T3"], ["LG"])
        if CUT == -3:
            DMA("sp", out_d[128:256, 0:16], LG[:, :], r=["LG"], w=["o"])
        cut(-3)
        LGv = LG[:, :].rearrange("p (d m r) -> p d m r", d=2, m=4, r=2)
        LGPv = LGP[:, :].rearrange("p (d m) -> p d m", d=2)
        cp("dve", LGPv[0:64], LGv[0:64, :, :, 0], ["LG"], ["LGP"])
        cp("dve", LGPv[64:128], LGv[64:128, :, :, 1], ["LG"], ["LGP"])
        arena(BIG_BASE)
        UT = alloc("UT", [128, 8, 2064], BF16)
        VR = alloc("VR", [128, 16, 1024], BF16)
        AFB = alloc("AFB", [128, 16, 4, 128], BF16)
        KVB = alloc("KVB", [128, 17, 4, 128], BF16)
        XB = [alloc("XB%d" % i, [128, 1024], F32) for i in range(2)]
        XN = [alloc("XN%d" % i, [128, 1024], BF16) for i in range(2)]
        KW = [alloc("KW%d" % i, [128, 2, 512], BF16) for i in range(2)]
        UCS = [alloc("UC%d" % i, [128, 8, 128], BF16) for i in range(2)]
        UH = alloc("UH", [128, 8, 16], BF16)
        VC = alloc("VC", [128, 1024], BF16)
        PACC = alloc("PACC", [128, 2, 4, 128], F32)
        SCTX = alloc("SCTX", [128, 2, 4, 128], F32)
        KVS = alloc("KVS", [128, 2, 4, 128], F32)
        PGB = [alloc("PGB%d" % i, [128, 2, 4, 128], F32) for i in range(2)]
        P1_END = cur[0]
        arena(W_BASE)
        WKV = alloc("WKV", [128, 8, 1536], BF16)
        WAD = [AFB[:, :, :, :].rearrange("p a b c -> p (a b c)").rearrange("p (k n) -> p k n", k=8),
               KVB[:, 0:16, :, :].rearrange("p a b c -> p (a b c)").rearrange("p (k n) -> p k n", k=8)]
        MSC = XB[0][:, 0:128]
        MSC2 = XB[0][:, 128:256]
        for h in range(8):
            ts("dve", MSC, ctc("D1"), LG[:, h:h + 1], None, ALU.mult, None, ["CT", "LG"], ["MSC", "XB0"])
            stt("dve", MSC2, ctc("D2"), LG[:, 8 + h:9 + h], MSC, ALU.mult, ALU.add, ["CT", "LG", "MSC"], ["MSC2", "XB0"])
            act(MSC2, MSC2, AF.Exp, ["MSC2"], ["MSC2", "XB0"])
            stt("dve", MT[:, h, :], MSC2, 0.125, ctc("EYE8"), ALU.mult, ALU.add, ["MSC2", "CT", "XB0"], ["MT"])
        for m in range(4):
            act(QFM[:, 0, m, :], ctc("IOTA1"), AF.Exp, ["CT", "LGP"], ["QF"], scale=LGP[:, m:m + 1])
            act(QFM[:, 2, m, :], ctc("IOTA2"), AF.Exp, ["CT", "LGP"], ["QF"], scale=LGP[:, 4 + m:5 + m])
        cp("dve", QFM[:, 1, :, :], QFM[:, 0, :, :], ["QF"], ["QF"])
        cp("dve", QFM[:, 3, :, :], QFM[:, 2, :, :], ["QF"], ["QF"])
        for q_ in range(4):
            lo = 64 if q_ % 2 == 0 else 0
            R.add("dve", lambda e, t=QFM[lo:lo + 64, q_, :, :]: e.memset(t, 0.0), ["QF"], ["QF"])
        act(KF[:, :], LG[:, 0:8], AF.Exp, ["LG", "CT"], ["KF"], scale=ctc("JREV"))
        act(KB[:, :], LG[:, 8:16], AF.Exp, ["LG", "CT"], ["KB"], scale=ctc("JFWD"))
        ts("dve", KF[:, :], KF[:, :], 0.125, None, ALU.mult, None, ["KF"], ["KF"])
        ts("dve", KB[:, :], KB[:, :], 0.125, None, ALU.mult, None, ["KB"], ["KB"])
        act(CD[:, :], LGP[:, :], AF.Exp, ["LGP"], ["CD"], scale=128.0)
        for j in range(8):
            act(CDPOW[:, j, :], ctc("CEXP"), AF.Exp, ["CT", "LGP"], ["CDPOW"], scale=LGP[:, j:j + 1])
            act(COEF[:, j, :], ctc("NEXPF" if j < 4 else "NEXPB"), AF.Exp, ["CT", "LGP"], ["COEF"], scale=LGP[:, j:j + 1])
            tt("dve", COEF[:, j, :], COEF[:, j, :], ctc("MASKF" if j < 4 else "MASKB"), ALU.mult, ["COEF", "CT"], ["COEF"])
            d_, m_ = j // 4, j % 4
            ts("dve", CDX[:, d_, m_, :], ctc("ONES"), CD[:, j:j + 1], None, ALU.mult, None, ["CT", "CD"], ["CDX"])
        if CUT == -2:
            DMA("sp", out_d[128:256, 0:16], LG[:, :], r=["LG"], w=["o"])
            DMA("sp", out_d[256:384, 0:1024], MT[:, :, :].rearrange("p a b -> p (a b)"), r=["MT"], w=["o"])
        cut(-2)
        act(SC[:, :], CFM[:, :], AF.Silu, ["CFM"], ["SC"])
        SCv = SC[:, :].rearrange("p (v k) -> p k v", v=2)
        def mod_dma(v):
            DMA("pool", WAD[v % 2], w_ada_v[:, :, v * 1024:(v + 1) * 1024], w=["WAD%d" % (v % 2)])

        def mod_mm(v):
            wb = WAD[v % 2]
            for j in range(8):
                col = (v * 8 + j) * 2
                for kc in range(8):
                    mm(PS[:, col:col + 2], wb[:, kc, j * 128:(j + 1) * 128], SCv[:, kc, :], kc == 0, kc == 7,
                       ["WAD%d" % (v % 2), "SC"], ["ps0"])

        mod_dma(0)
        mod_dma(1)
        mod_mm(0)
        mod_dma(2)
        mod_mm(1)
        mod_dma(3)
        tt("dve", MODT[:, 0:16, :], PS[:, 0:32].rearrange("p (a b) -> p a b", b=2),
           BADA[:, 0:16].unsqueeze(2).to_broadcast([128, 16, 2]), ALU.add, ["ps0", "BADA"], ["MODT"])
        ts("dve", OPSC1[:, :, :], MODT[:, 8:16, :], 1.0, None, ALU.add, None, ["MODT"], ["OPSC"])

        lnctr = [0]

        def ln_stats(xt, ntok, rk):
            b = lnctr[0] % 2
            lnctr[0] += 1
            st = LNS[0:ntok, b, :]
            mv = LNM[0:ntok, b, :]
            k = "LN%d" % b
            R.add("dve", lambda e, o=st[:, 0:6], i=xt[:, 0:512]: e.bn_stats(out=o, in_=i), rk, [k])
            R.add("dve", lambda e, o=st[:, 6:12], i=xt[:, 512:1024]: e.bn_stats(out=o, in_=i), rk, [k])
            R.add("dve", lambda e, o=mv[:, 0:2], i=st[:, 0:12]: e.bn_aggr(out=o, in_=i), [k], [k])
            act(mv[:, 2:3], mv[:, 1:2], AF.Sqrt, [k], [k], bias=EPS)
            R.add("dve", lambda e, o=mv[:, 2:3]: e.reciprocal(out=o, in_=o), [k], [k])
            return mv[:, 0:1], mv[:, 2:3], k

        def ln_to_T(xt, ntok, rk, XN, xnk, dst_fn, dstk, sh_fn, osc_fn, pb, part=0):
            if part in (0, 1):
                mean, rstd, k = ln_stats(xt, ntok, rk)
                ts("dve", XN[0:ntok, :], xt, mean, rstd, ALU.subtract, ALU.mult, rk + [k], [xnk])
            if part == 1:
                return
            pbf = bank(pb).bitcast(BF16)
            for kc in range(8):
                R.add("pe", lambda e, o=pbf[:, kc * 128:kc * 128 + ntok], i=XN[0:ntok, kc * 128:(kc + 1) * 128],
                      idn=IDB[0:ntok, 0:ntok]: e.transpose(o, i, idn), [xnk, "IDB"], ["ps%d" % pb])
            for kc in range(8):
                act(dst_fn(kc), pbf[:, kc * 128:kc * 128 + ntok], AF.Identity, ["ps%d" % pb, "MODT", "OPSC"], [dstk],
                    bias=sh_fn(kc), scale=osc_fn(kc))

        for kc in range(8):
            DMA("pool", WKV[:, kc, :], w_in_v[:, kc, 0:1536], w=["W"])
        R.add("dve", lambda e: e.memset(PACC[:, :, :, :], 0.0), [], ["PACCf", "PACCb"])
        R.add("dve", lambda e: e.memset(SCTX[:, :, :, :], 0.0), [], ["SCTXf", "SCTXb"])

        def kv_chunk(u_fn, uk, vdst, vk, accF, accFk, accB, accBk, cidx, bi, afdst=None, kvbdst=None):
            kw = KW[bi % 2]
            kwk = "KW%d" % (bi % 2)
            for kc in range(8):
                mm(bank(1), u_fn(kc), WKV[:, kc, 0:512], kc == 0, kc == 7, [uk, "W"], ["ps1"])
            for nb in range(2):
                for kc in range(8):
                    mm(bank(2 + nb), u_fn(kc), WKV[:, kc, 512 + nb * 512:1024 + nb * 512], kc == 0, kc == 7, [uk, "W"], ["ps%d" % (2 + nb)])
            k3 = bank(1).rearrange("p (h d) -> p h d", d=64)
            tt("dve", kw[:, 0, :].rearrange("p (h d) -> p h d", d=64), k3, KF[:, :].unsqueeze(2).to_broadcast([128, 8, 64]),
               ALU.mult, ["ps1", "KF"], [kwk])
            tt("dve", kw[:, 1, :].rearrange("p (h d) -> p h d", d=64), k3, KB[:, :].unsqueeze(2).to_broadcast([128, 8, 64]),
               ALU.mult, ["ps1", "KB"], [kwk])
            act(vdst, bank(2, 2), AF.Copy, ["ps2", "ps3"], [vk])
            for d_ in range(2):
                for m in range(4):
                    b0 = 4 + 2 * d_ + m // 2
                    mm(PS[:, b0 * 512 + (m % 2) * 256: b0 * 512 + (m % 2) * 256 + 256], kw[:, d_, m * 128:(m + 1) * 128],
                       vdst[:, m * 256:(m + 1) * 256], True, True, [kwk, vk], ["ps%d" % b0])
            for d_ in range(2):
                src = bank(4 + 2 * d_, 2).rearrange("p (m x) -> p m x", x=256)
                act(KVS[0:64, d_, :, :], src[0:64, :, 0:128], AF.Copy, ["ps%d" % (4 + 2 * d_), "ps%d" % (5 + 2 * d_)], ["KVS%d" % d_])
                act(KVS[64:128, d_, :, :], src[64:128, :, 128:256], AF.Copy, ["ps%d" % (4 + 2 * d_), "ps%d" % (5 + 2 * d_)], ["KVS%d" % d_])
            if afdst is not None:
                act(afdst, accF, AF.Copy, [accFk], ["AFB", "WAD0"])
            tt("dve", accF, accF, CDX[:, 0, :, :], ALU.mult, [accFk, "CDX"], [accFk])
            tt("dve", accF, accF, KVS[:, 0, :, :], ALU.add, [accFk, "KVS0"], [accFk])
            for m in range(4):
                stt("dve", accB[:, m, :], KVS[:, 1, m, :], CDPOW[:, 4 + m, cidx:cidx + 1], accB[:, m, :], ALU.mult, ALU.add,
                    ["KVS1", "CDPOW", accBk], [accBk])
            if kvbdst is not None:
                act(kvbdst, KVS[:, 1, :, :], AF.Copy, ["KVS1"], ["KVB", "WAD1"])

        for cc_ in range(2):
            xb = XB[cc_ % 2]
            DMA("sp", xb[:, :], ctxs[cc_ * 128:(cc_ + 1) * 128, :], w=["XB%d" % (cc_ % 2)])
            ln_to_T(xb[:, :], 128, ["XB%d" % (cc_ % 2)], XN[cc_ % 2], "XN%d" % (cc_ % 2),
                    lambda kc, cc_=cc_: UCS[cc_][:, kc, :], "UC%d" % cc_, lambda kc: MODT[:, kc, 1:2], lambda kc: OPSC1[:, kc, 1:2], 7)
        DMA("sp", XB[0][0:16, :], xh, w=["XB0"])
        ln_to_T(XB[0][0:16, :], 16, ["XB0"], XN[0], "XN0", lambda kc: UH[:, kc, 0:16], "UH",
                lambda kc: MODT[:, kc, 0:1], lambda kc: OPSC1[:, kc, 0:1], 7)
        ts("dve", UT[:, :, 0:8], UH[:, :, 0:8], ctc("PMASK", 0, 1), None, ALU.mult, None, ["UH", "CT"], ["UTh"])
        ts("dve", UT[:, :, 2056:2064], UH[:, :, 8:16], ctc("PMASK", 1, 2), None, ALU.mult, None, ["UH", "CT"], ["UTh"])
        def p1_a(c):
            xb = XB[c % 2]
            DMA("sp", xb[:, :], xs[c * 128:(c + 1) * 128, :], w=["XB%d" % (c % 2)])
            ln_to_T(xb[:, :], 128, ["XB%d" % (c % 2)], XN[c % 2], "XN%d" % (c % 2),
                    lambda kc, c=c: UT[:, kc, 8 + c * 128:8 + (c + 1) * 128], "UT%d" % c,
                    lambda kc: MODT[:, kc, 0:1], lambda kc: OPSC1[:, kc, 0:1], 7)

        def p1_b(c):
            kv_chunk(lambda kc, c=c: UT[:, kc, 8 + c * 128:8 + (c + 1) * 128], "UT%d" % c, VR[:, c, :], "VR%d" % c,
                     PACC[:, 0, :, :], "PACCf", PACC[:, 1, :, :], "PACCb", c, c,
                     afdst=AFB[:, c, :, :], kvbdst=KVB[:, c, :, :])

        for c in range(NCH):
            p1_a(c)
            if c in (1, 5, 9, 13):
                v_ = 2 + (c - 1) // 4
                mod_mm(v_)
                if v_ + 2 < 6:
                    mod_dma(v_ + 2)
        tt("dve", MODT[:, 16:48, :], PS[:, 32:96].rearrange("p (a b) -> p a b", b=2),
           BADA[:, 16:48].unsqueeze(2).to_broadcast([128, 32, 2]), ALU.add, ["ps0", "BADA"], ["MODT2"])
        ts("dve", OPSC2[:, :], MODT[:, 32:40, 0], 1.0, None, ALU.add, None, ["MODT2"], ["OPSC2"])
        for cc_ in range(2):
            kv_chunk(lambda kc, cc_=cc_: UCS[cc_][:, kc, :], "UC%d" % cc_, VC[:, :], "VC", SCTX[:, 0, :, :], "SCTXf",
                     SCTX[:, 1, :, :], "SCTXb", cc_, cc_)
        for c in range(NCH):
            p1_b(c)
        DMA("sp", ccin.ap(), PACC[:, :, :, :].rearrange("p a b c -> p (a b c)"), r=["PACCf", "PACCb"], w=["ccin"])
        R.add("pool", lambda e: e.collective_compute("AllGather", ALU.bypass, replica_groups=GROUPS,
                                                     ins=[ccin.ap().opt()], outs=[ccout.ap().opt()]),
              ["ccin"], ["ccout"], kind="cc")
        arena(W_BASE)
        WQ = alloc("WQ", [128, 8, 512], BF16)
        WK = alloc("WK", [128, 8, 512], BF16)
        WG = alloc("WG", [128, 8, 1024], BF16)
        for kc in range(8):
            DMA("pool", WK[:, kc, :], w_in_v[:, kc, 0:512], r=[], w=["W"])
            DMA("pool", WQ[:, kc, :], w_in_v[:, kc, 1536:2048], w=["W"])
            DMA("pool", WG[:, kc, :], w_in_v[:, kc, 2048:3072], w=["W"])
        for d_ in range(2):
            for m in range(4):
                ts("dve", SST[:, d_, m, :], SCTX[:, d_, m, :], COEF[:, d_ * 4 + m, 4:5], None, ALU.mult, None,
                   ["SCTXf", "SCTXb", "COEF"], ["SST"])
        for r_ in range(4):
            pg = PGB[r_ % 2]
            DMA("sp", pg[:, :, :, :].rearrange("p a b c -> p (a b c)"), ccout.ap()[r_ * 128:(r_ + 1) * 128, :],
                r=["ccout"], w=["PGB%d" % (r_ % 2)])
            for d_ in range(2):
                for m in range(4):
                    stt("dve", SST[:, d_, m, :], pg[:, d_, m, :], COEF[:, d_ * 4 + m, r_:r_ + 1], SST[:, d_, m, :], ALU.mult, ALU.add,
                        ["PGB%d" % (r_ % 2), "COEF", "SST"], ["SST"])
        cp("dve", SRUN[:, :, :], SST[:, 1, :, :], ["SST"], ["SRUN"])
        act(KVB[:, 16, :, :], SRUN[:, :, :], AF.Copy, ["SRUN"], ["SB16"])
        R.barrier()
        if CUT == 1:
            DMA("sp", out_d[0:128, 0:1024], SST[:, :, :, :].rearrange("p a b c -> p (a b c)"), r=["SST"], w=["o"])
            DMA("sp", out_d[128:256, 0:1024], SCTX[:, :, :, :].rearrange("p a b c -> p (a b c)"), r=["SCTXf", "SCTXb"], w=["o"])
            DMA("sp", out_d[256:384, 0:1024], PACC[:, :, :, :].rearrange("p a b c -> p (a b c)"), r=["PACCf", "PACCb"], w=["o"])
        cut(1)

        arena(BIG_BASE + 8 * 2064 * 2 + 16 * 1024 * 2 + 16 * 512 * 2 + 17 * 512 * 2)
        QK = [alloc("QK%d" % i, [128, 7, 512], BF16) for i in range(3)]
        STt = [alloc("ST%d" % i, [128, 8, 128], BF16) for i in range(2)]
        SG = [alloc("SG%d" % i, [128, 1024], BF16) for i in range(3)]
        YN = alloc("YN", [128, 1024], F32)
        RGT = [alloc("RGT%d" % i, [128, 1024], BF16) for i in range(2)]
        QKT = alloc("QKT", [128, 1024], BF16)
        YST = alloc("YST", [128, 8, 6], F32)
        YMV = alloc("YMV", [128, 8, 4], F32)
        def p2a_a(c):
            for m in range(4):
                stt("dve", AFB[:, c, m, :], SST[:, 0, m, :], CDPOW[:, m, c:c + 1], AFB[:, c, m, :],
                    ALU.mult, ALU.add, ["SST", "CDPOW", "AFB"], ["SF%d" % c])
            if c < NCH - 1:
                tt("dve", SRUN[:, :, :], SRUN[:, :, :], CDX[:, 1, :, :], ALU.mult, ["SRUN", "CDX"], ["SRUN"])
                tt("dve", SRUN[:, :, :], SRUN[:, :, :], KVB[:, c + 1, :, :], ALU.add, ["SRUN", "KVB", "SB%d" % (c + 1)], ["SRUN"])
                act(KVB[:, c + 1, :, :], SRUN[:, :, :], AF.Copy, ["SRUN"], ["SB%d" % (c + 1), "KVB"])
            qk = QK[c % 3]; qkk = "QK%d" % (c % 3)
            st = STt[c % 2]; stk = "ST%d" % (c % 2)
            ucols = slice(8 + c * 128, 8 + (c + 1) * 128)
            sg = SG[c % 3]; sgk_ = "SG%d" % (c % 3)
            for kc in range(8):
                mm(bank(0), UT[:, kc, ucols], WQ[:, kc, :], kc == 0, kc == 7, ["W", "UT%d" % c], ["ps0"])
            for kc in range(8):
                mm(bank(1), UT[:, kc, ucols], WK[:, kc, :], kc == 0, kc == 7, ["W", "UT%d" % c], ["ps1"])
            for nb in range(2):
                for kc in range(8):
                    mm(bank(2 + nb), UT[:, kc, ucols], WG[:, kc, nb * 512:(nb + 1) * 512], kc == 0, kc == 7,
                       ["W", "UT%d" % c], ["ps%d" % (2 + nb)])
            act(QKT[:, :], bank(0, 2), AF.Copy, ["ps0", "ps1"], ["QKT"])
            pbq = bank(0).bitcast(BF16)
            for j in range(8):
                R.add("pe", lambda e, o=pbq[:, j * 128:(j + 1) * 128], i=QKT[:, j * 128:(j + 1) * 128]: e.transpose(o, i, IDB[:, :]),
                      ["QKT", "IDB"], ["ps0"])
            for q_ in range(4):
                tt("dve", qk[:, q_, :], pbq[:, 0:512], QFM[:, q_, :, :].rearrange("p a b -> p (a b)"), ALU.mult, ["ps0", "QF"], [qkk])
            act(qk[:, 4, :], pbq[:, 0:512], AF.Copy, ["ps0"], [qkk])
            act(qk[:, 5, :], pbq[:, 512:1024], AF.Copy, ["ps0", "CT"], [qkk], scale=ctc("PE"))
            act(qk[:, 6, :], pbq[:, 512:1024], AF.Copy, ["ps0", "CT"], [qkk], scale=ctc("PO"))
            act(sg[:, :], bank(2, 2), AF.Silu, ["ps2", "ps3"], [sgk_])

        def p2a_b(c):
            qk = QK[c % 3]; qkk = "QK%d" % (c % 3)
            st = STt[c % 2]; stk = "ST%d" % (c % 2)
            ucols = slice(8 + c * 128, 8 + (c + 1) * 128)
            sg = SG[c % 3]; sgk_ = "SG%d" % (c % 3)
            for h in range(8):
                m, par = h // 2, h % 2
                pr = slice(par * 64, par * 64 + 64)
                b0 = 4 + h // 4
                mm(PS[:, b0 * 512 + (h % 4) * 128: b0 * 512 + (h % 4 + 1) * 128], qk[:, 5 + par, m * 128:(m + 1) * 128],
                   qk[:, 4, m * 128:(m + 1) * 128], True, True, [qkk], ["ps%d" % b0])
            for hb in range(2):
                tt("dve", st[:, hb * 4:(hb + 1) * 4, :], bank(4 + hb).rearrange("p (a b) -> p a b", b=128), MT[:, hb * 4:(hb + 1) * 4, :],
                   ALU.mult, ["ps%d" % (4 + hb), "MT"], [stk])
            for h in range(8):
                m, par = h // 2, h % 2
                pr = slice(par * 64, par * 64 + 64)
                b0 = 6 + h // 4
                o = PS[:, b0 * 512 + (h % 4) * 128: b0 * 512 + (h % 4 + 1) * 128]
                mm(o, st[:, h, :], VR[:, c, h * 128:(h + 1) * 128], True, False, [stk, "VR%d" % c], ["ps%d" % b0])
                mm(o, qk[:, 0 + par, m * 128:(m + 1) * 128], AFB[:, c, m, :], False, False, [qkk, "SF%d" % c], ["ps%d" % b0])
                mm(o, qk[:, 2 + par, m * 128:(m + 1) * 128], KVB[:, c + 1, m, :], False, True, [qkk, "SB%d" % (c + 1)], ["ps%d" % b0])
            for h in range(8):
                b0 = 6 + h // 4
                yh = PS[:, b0 * 512 + (h % 4) * 128: b0 * 512 + (h % 4 + 1) * 128]
                R.add("dve", lambda e, o=YST[:, h, :], i=yh: e.bn_stats(out=o, in_=i), ["ps%d" % b0], ["YST"])
            for h in range(8):
                R.add("dve", lambda e, o=YMV[:, h, 0:2], i=YST[:, h, :]: e.bn_aggr(out=o, in_=i), ["YST"], ["YMV"])
            act(YMV[:, :, 2], YMV[:, :, 1], AF.Sqrt, ["YMV"], ["YMV"], bias=EPS)
            R.add("dve", lambda e: e.reciprocal(out=YMV[:, :, 2], in_=YMV[:, :, 2]), ["YMV"], ["YMV"])
            stt("dve", YMV[:, :, 3], YMV[:, :, 0], -1.0, YMV[:, :, 2], ALU.mult, ALU.mult, ["YMV"], ["YMV"])
            for h in range(8):
                b0 = 6 + h // 4
                yh = PS[:, b0 * 512 + (h % 4) * 128: b0 * 512 + (h % 4 + 1) * 128]
                act(YN[:, h * 128:(h + 1) * 128], yh, AF.Identity, ["ps%d" % b0, "YMV"], ["YN"], bias=YMV[:, h, 3:4], scale=YMV[:, h, 2:3])
            rgt = RGT[c % 2]; rgk = "RGT%d" % (c % 2)
            tt("dve", rgt[:, :], YN[:, :], sg[:, :], ALU.mult, ["YN", sgk_], [rgk])

        def p2a_b2(c):
            rgt = RGT[c % 2]; rgk = "RGT%d" % (c % 2)
            pbf = bank(4).bitcast(BF16)
            for kc in range(8):
                R.add("pe", lambda e, o=pbf[:, kc * 128:(kc + 1) * 128], i=rgt[:, kc * 128:(kc + 1) * 128]: e.transpose(o, i, IDB[:, :]),
                      [rgk, "IDB"], ["ps4"])
            act(VR[:, c, :], pbf, AF.Copy, ["ps4"], ["VR%d" % c])

        w_br_v = w_br.rearrange("(kc p) n -> p kc n", p=128)

        def prefetch_2bi():
            arena(W_BASE)
            wbr = alloc("WBR", [128, 8, 1024], BF16)
            wga = alloc("WGA", [128, 8, 1024], BF16)
            for kc in range(8):
                DMA("pool", wbr[:, kc, :], w_br_v[:, kc, :], w=["W"])
                DMA("pool", wga[:, kc, :], w_in_v[:, kc, 3584:4608], w=["W"])
            return wbr, wga

        order2a = list(range(NCH - 1, -1, -1))
        p2a_a(order2a[0])
        p2a_a(order2a[1])
        for i_, c in enumerate(order2a):
            if i_ + 2 < NCH:
                p2a_a(order2a[i_ + 2])
            if i_ == NCH - 3:
                WBR, WGA = prefetch_2bi()
            p2a_b(c)
            if i_ >= 1:
                p2a_b2(order2a[i_ - 1])
        p2a_b2(order2a[-1])
        R.barrier()
        if CUT == 2:
            for c_ in range(16):
                DMA("sp", out_d[c_ * 128:(c_ + 1) * 128, :], VR[:, c_, :].bitcast(F32) if False else XB[0][:, :], r=["o"], w=["o"]) if False else None
        cut(2)

        arena(BIG_BASE + 8 * 2064 * 2 + 16 * 1024 * 2)
        MG = alloc("MG", [128, 8, 2048], BF16)
        SGA = [alloc("SGA%d" % i, [128, 512], BF16) for i in range(2)]
        P2B_END = cur[0]
        assert P2B_END <= WB_BASE, P2B_END
        assert cur[0] <= WB_BASE, cur[0]
        arena(WB_BASE)
        WP = alloc("WP", [128, 8, 512], BF16)
        WGB = alloc("WGB", [128, 8, 1024], BF16)
        WBP = alloc("WBP", [128, 4, 1024], BF16)
        PLW = alloc("PLW", [128, 4, 128], BF16)
        w_bp_v = w_bp.rearrange("(kc p) n -> p kc n", p=128)
        for kc in range(8):
            DMA("pool", WP[:, kc, :], w_in_v[:, kc, 3072:3584], w=["WB"])
            DMA("pool", WGB[:, kc, :], w_in_v[:, kc, 4608:5632], w=["WB"])
        for kc in range(4):
            DMA("pool", WBP[:, kc, :], w_bp_v[:, kc, :], w=["WB"])
            DMA("pool", PLW[:, kc, :], pool_w[kc], w=["WB"])
        it = 0
        for t4 in range(4):
            for dc in range(8):
                bx, by = 2 * (it % 4), 2 * (it % 4) + 1
                sga = SGA[it % 2]; sgk = "SGA%d" % (it % 2)
                for kc in range(8):
                    mm(bank(bx), WBR[:, kc, dc * 128:(dc + 1) * 128], VR[:, 4 * t4:4 * t4 + 4, kc * 128:(kc + 1) * 128],
                       kc == 0, kc == 7, ["W"] + ["VR%d" % (4 * t4 + q) for q in range(4)], ["ps%d" % bx])
                for kc in range(8):
                    mm(bank(by), WGA[:, kc, dc * 128:(dc + 1) * 128], UT[:, kc, 8 + t4 * 512:8 + (t4 + 1) * 512],
                       kc == 0, kc == 7, ["W", "UTall"], ["ps%d" % by])
                act(sga[:, :], bank(by), AF.Sigmoid, ["ps%d" % by], [sgk])
                tt("dve", MG[:, dc, t4 * 512:(t4 + 1) * 512], bank(bx), sga[:, :], ALU.mult, ["ps%d" % bx, sgk], ["MG"])
                it += 1
        R.barrier()
        cut(3)

        arena(W_BASE)
        WO = alloc("WO", [128, 8, 1024], BF16)
        TMPM = [alloc("TMPM%d" % i, [128, 512], F32) for i in range(1)]
        w_out_v = w_out.rearrange("(kc p) n -> p kc n", p=128)
        for kc in range(8):
            DMA("pool", WO[:, kc, :], w_out_v[:, kc, :], w=["W"])
        arena(BIG_BASE + 8 * 2064 * 2)
        PT = alloc("PT", [128, 4, 528], F32)
        TA = alloc("TA", [128, 4, 528], F32)
        TB = alloc("TB", [128, 3, 528], F32)
        DT = alloc("DT", [128, 4, 512], BF16)
        PM = alloc("PM", [128, 4, 512], BF16)
        assert cur[0] <= BIG_BASE + 8 * 2064 * 2 + 16 * 1024 * 2, cur[0]
        def p2b_P(t4):
            for g in range(4):
                for kc in range(8):
                    mm(bank(g), WP[:, kc, g * 128:(g + 1) * 128], UT[:, kc, t4 * 512:t4 * 512 + 512], kc == 0, kc == 7,
                       ["WB", "UTall"], ["ps%d" % g])
            for g in range(4):
                for kc in range(8):
                    mm(PS[:, 2048 + g * 16:2048 + (g + 1) * 16], WP[:, kc, g * 128:(g + 1) * 128],
                       UT[:, kc, t4 * 512 + 512:t4 * 512 + 528], kc == 0, kc == 7, ["WB", "UTall"], ["ps4"])
            act(PT[:, :, 0:512], PS[:, 0:2048].rearrange("p (g x) -> p g x", x=512), AF.Copy, bkeys(0, 4), ["PT"])
            act(PT[:, :, 512:528], PS[:, 2048:2112].rearrange("p (g x) -> p g x", x=16), AF.Copy, ["ps4"], ["PT"])
            tt("dve", TA[:, 0:4, 1:528], PT[:, 0:4, 0:527], PT[:, 0:4, 1:528], ALU.add, ["PT"], ["TA"])
            tt("dve", TB[:, 0:3, 2:527], TA[:, 1:4, 1:526], TA[:, 1:4, 3:528], ALU.add, ["TA"], ["TB"])
            tt("dve", TA[:, 2:4, 4:525], TB[:, 1:3, 2:523], TB[:, 1:3, 6:527], ALU.add, ["TB", "TA"], ["TA2"])
            tt("dve", TB[:, 2:3, 8:520], TA[:, 3:4, 4:516], TA[:, 3:4, 12:524], ALU.add, ["TA2", "TB"], ["TB2"])
            srcs = [TA[:, 0, 8:520], TB[:, 0, 8:520], TA[:, 2, 8:520], TB[:, 2, 8:520]]
            if t4 == 0 or t4 == 3:
                o_, w_ = _CT["EDGEL" if t4 == 0 else "EDGER"]
                for g in range(4):
                    sl = srcs[g][:, 0:8] if t4 == 0 else srcs[g][:, 504:512]
                    tt("dve", sl, sl, CT[:, o_ + g * 8:o_ + g * 8 + 8], ALU.mult, ["TA", "TB", "TA2", "TB2", "CT"], ["TA", "TB", "TA2", "TB2"])
            for g, wdw in enumerate((2, 4, 8, 16)):
                stt("dve", DT[:, g, :], srcs[g], 1.0 / wdw, PT[:, g, 8:520], ALU.mult, ALU.subtract,
                    ["TA", "TB", "TA2", "TB2", "PT"], ["DT"])

        def p2b_pre(t4):
            for g in range(4):
                mm(bank(5 + (g % 2)), PLW[:, g, :], DT[:, g, :], True, True, ["WB", "DT"], ["ps%d" % (5 + g % 2)])
                act(PM[:, g, :], bank(5 + (g % 2)), AF.Identity, ["ps%d" % (5 + g % 2), "PSC"], ["PM"], scale=PSC[:, g:g + 1])

        def p2b_dc(t4):
            for dc in range(8):
                bx, by = (0, 1) if dc % 2 == 0 else (2, 3)
                sga = SGA[dc % 2]; sgk = "SGA%d" % (dc % 2)
                tm = TMPM[0]; tmk = "TMPM0"
                for g in range(4):
                    mm(bank(bx), WBP[:, g, dc * 128:(dc + 1) * 128], PM[:, g, :], g == 0, g == 3, ["WB", "PM"], ["ps%d" % bx])
                for kc in range(8):
                    mm(bank(by), WGB[:, kc, dc * 128:(dc + 1) * 128], UT[:, kc, 8 + t4 * 512:8 + (t4 + 1) * 512],
                       kc == 0, kc == 7, ["WB", "UTall"], ["ps%d" % by])
                act(sga[:, :], bank(by), AF.Sigmoid, ["ps%d" % by], [sgk])
                tt("dve", tm[:, :], bank(bx), sga[:, :], ALU.mult, ["ps%d" % bx, sgk], [tmk])
                mgs = MG[:, dc, t4 * 512:(t4 + 1) * 512]
                tt("dve", mgs, mgs, tm[:, :], ALU.add, ["MG", tmk], ["MG"])

        p2b_P(0)
        for t4 in range(4):
            p2b_pre(t4)
            if t4 + 1 < 4:
                p2b_P(t4 + 1)
            p2b_dc(t4)
        R.barrier()
        cut(4)

        def bcast_rows(BC, gcol0, lng, lnb):
            DMA("sp", BC[:, 1, :], lng.partition_broadcast(128).rearrange("p o n -> p (o n)"), w=["BC"])
            DMA("sp", BC[:, 2, :], lnb.partition_broadcast(128).rearrange("p o n -> p (o n)"), w=["BC"])
            for kc in range(8):
                ts("dve", BC[:, 0, kc * 128:(kc + 1) * 128], ctc("IDENT"), MODT[:, gcol0 + kc, 0:1], None, ALU.mult, None,
                   ["CT", "MODT"], ["BC"])
            for kc in range(8):
                mm(PS[:, (6 + kc // 4) * 512 + (kc % 4) * 128:(6 + kc // 4) * 512 + (kc % 4 + 1) * 128],
                   ctc("ONES"), BC[:, 0, kc * 128:(kc + 1) * 128], True, True, ["CT", "BC"], ["ps%d" % (6 + kc // 4)])
            act(BC[:, 0, :], bank(6, 2), AF.Copy, ["ps6", "ps7"], ["BC"])

        def epilogue(pb, xb, xbk, BC, Z, outb, outk, zk="Z", folded=True):
            if folded:
                stt("dve", Z[:, :], xb[:, :], ALPHA, bank(pb, 2), ALU.mult, ALU.add, [xbk, "ps%d" % pb, "ps%d" % (pb + 1)], [zk])
            else:
                tt("dve", Z[:, :], bank(pb, 2), BC[:, 0, :], ALU.mult, ["ps%d" % pb, "ps%d" % (pb + 1), "BC"], [zk])
                stt("dve", Z[:, :], xb[:, :], ALPHA, Z[:, :], ALU.mult, ALU.add, [xbk, zk], [zk])
            mean, rstd, k = ln_stats(Z[:, :], 128, [zk])
            ts("dve", Z[:, :], Z[:, :], mean, rstd, ALU.subtract, ALU.mult, [zk, k], [zk])
            tt("dve", Z[:, :], Z[:, :], BC[:, 1, :], ALU.mult, [zk, "BC"], [zk])
            tt("dve", outb[:, :], Z[:, :], BC[:, 2, :], ALU.add, [zk, "BC"], [outk])

        arena(BIG_BASE)
        BC = alloc("BC", [128, 3, 1024], F32)
        Z = [alloc("Z%d" % i, [128, 1024], F32) for i in range(2)]
        XC = [alloc("XC%d" % i, [128, 1024], F32) for i in range(4)]
        XO = [alloc("XO%d" % i, [128, 1024], F32) for i in range(2)]
        assert cur[0] <= WB_BASE, cur[0]
        arena(WB_BASE)
        WD = alloc("WD", [128, 22, 1024], BF16)
        w_down_v = w_down.rearrange("(kc p) n -> p kc n", p=128)
        bcast_rows(BC, 16, ln1g, ln1b)
        for kc in range(8):
            tt("dve", WO[:, kc, :], WO[:, kc, :], BC[:, 0, :], ALU.mult, ["W", "BC"], ["W"])
        x1_target = out_d if STAGE == 2 else x1d
        order = [0, NCH - 1] + list(range(1, NCH - 1))
        for i0 in range(3):
            c0 = order[i0]
            DMA("sp", XC[i0 % 4][:, :], xs[c0 * 128:(c0 + 1) * 128, :], w=["XC%d" % (i0 % 4)])
        for it_, c in enumerate(order):
            xc = XC[it_ % 4]; xck = "XC%d" % (it_ % 4)
            xo = XO[it_ % 2]; xok = "XO%d" % (it_ % 2)
            if it_ + 3 < NCH:
                cn = order[it_ + 3]
                DMA("sp", XC[(it_ + 3) % 4][:, :], xs[cn * 128:(cn + 1) * 128, :], w=["XC%d" % ((it_ + 3) % 4)])
            if STAGE >= 3 and it_ < 11:
                for kc in (2 * it_, 2 * it_ + 1):
                    DMA("pool", WD[:, kc, :], w_down_v[:, kc, :], w=["WD%d" % kc])
            pb = 2 * (it_ % 2)
            for nb in range(2):
                for kc in range(8):
                    mm(bank(pb + nb), MG[:, kc, c * 128:(c + 1) * 128], WO[:, kc, nb * 512:(nb + 1) * 512], kc == 0, kc == 7,
                       ["W", "MG"], ["ps%d" % (pb + nb)])
            epilogue(pb, xc, xck, BC, Z[it_ % 2], xo, xok, zk="Z%d" % (it_ % 2))
            DMA("sp", x1_target[c * 128:(c + 1) * 128, :], xo[:, :], r=[xok], w=["x1d%d" % c])
            if c == 0:
                DMA("sp", ffin.ap()[0:64, :], xo[0:64, :], r=[xok], w=["ffin"])
            if c == NCH - 1:
                DMA("sp", ffin.ap()[64:128, :], xo[64:128, :], r=[xok], w=["ffin"])
                if STAGE >= 3:
                    R.add("pool", lambda e: e.collective_compute("AllGather", ALU.bypass, replica_groups=GROUPS,
                                                                 ins=[ffin.ap().opt()], outs=[ffout.ap().opt()]),
                          ["ffin"], ["ffout"], kind="cc")
        R.barrier(keep=["ffout"] + ["WD%d" % kc for kc in range(22)])

        if STAGE >= 3:
            arena(W_BASE)
            U2 = alloc("U2", [128, 8, 1152], BF16)
            GT = alloc("GT", [128, 22, 1024], BF16)
            WUP = [alloc("WUP%d" % i, [128, 8, 256], BF16) for i in range(2)]
            HA = [alloc("HA%d" % i, [128, 3, 18, 64], BF16) for i in range(1)]
            HBP = alloc("HBP", [128, 18, 66], BF16)
            ACC = alloc("ACC", [128, 1024], F32)
            ACCB = ACC[:, 0:512].bitcast(BF16)
            GA = [alloc("GA%d" % i, [128, 1024], F32) for i in range(1)]
            DG = [alloc("DG%d" % i, [128, 1, 9, 128], BF16) for i in range(1)]
            BC2 = alloc("BC2", [128, 3, 1024], F32)
            Z2 = alloc("Z2", [128, 1024], F32)
            X2 = [alloc("X2%d" % i, [128, 1024], F32) for i in range(2)]
            XN2 = [alloc("XN2%d" % i, [128, 1024], BF16) for i in range(1)]
            XO2 = [alloc("XO2%d" % i, [128, 1024], F32) for i in range(2)]
            HX = alloc("HX", [128, 1024], F32)
            X2U = [alloc("X2U%d" % i, [128, 1024], F32) for i in range(1)]
            assert cur[0] <= WB_BASE, cur[0]
            bcast_rows(BC2, 40, ln2g, ln2b)
            for i in range(1):
                R.add("pool", lambda e, t=HA[i]: e.memset(t[:, :, :, :], 0.0), [], ["HA%d" % i])
                R.add("pool", lambda e: e.memset(HBP[:, :, :], 0.0), [], ["HBP"])
            ci = 0
            pi_glob = 0
            u2ctr = [0]

            def u2_buf(hf, j):
                if hf == 0:
                    bufs = [(X2U[0], "X2U0"), (X2[0], "X20"), (X2[1], "X21")]
                else:
                    bufs = [(X2U[0], "X2U0"), (HX, "HX")]
                return bufs[j % len(bufs)]

            def u2_load(hf, j):
                x2, x2k = u2_buf(hf, j)
                row0 = 16 * hf + 2 * j
                if row0 == 0:
                    cp("pool", x2[0:64, :], HX[0:64, :], ["HX"], [x2k])
                    DMA("sp", x2[64:128, :], x1d[0:64, :], r=["x1d0"], w=[x2k])
                elif row0 == 32:
                    DMA("sp", x2[0:64, :], x1d[31 * 64:32 * 64, :], r=["x1d15"], w=[x2k])
                    DMA("sp", x2[64:128, :], hxd.ap()[64:128, :], r=["hxd"], w=[x2k])
                else:
                    t0 = (row0 - 1) * 64
                    DMA("sp", x2[:, :], x1d[t0:t0 + 128, :], r=["x1d%d" % (t0 // 128), "x1d%d" % ((t0 + 127) // 128)], w=[x2k])

            def u2_chunk(hf, j, part=0, load=True):
                x2, x2k = u2_buf(hf, j)
                row0 = 16 * hf + 2 * j
                if load and part != 2:
                    u2_load(hf, j)
                par_ = j % 2
                xnb, xnk_ = (XN2[0], "XN20") if par_ == 0 else (ACCB, "ACC")
                ln_to_T(x2[:, :], 128, [x2k], xnb, xnk_,
                        lambda kc, j=j: U2[:, kc, j * 128:(j + 1) * 128], "U2",
                        lambda kc: MODT[:, 24 + kc, 0:1], lambda kc: OPSC2[:, kc:kc + 1], 6 + par_, part=part)
                if part == 1:
                    return
                if row0 == 0:
                    ts("dve", U2[:, :, 0:64], U2[:, :, 0:64], ctc("HM", 0, 1), None, ALU.mult, None, ["U2", "CT"], ["U2"])
                if row0 == 32:
                    ts("dve", U2[:, :, 1088:1152], U2[:, :, 1088:1152], ctc("HM", 1, 2), None, ALU.mult, None, ["U2", "CT"], ["U2"])

            for hf in range(2):
                if hf == 0:
                    for j in [1, 2, 3, 4, 5, 6, 7, 8]:
                        u2_chunk(0, j)
                    R.add("dve", lambda e: e.memset(HX[:, :], 0.0), [], ["HX"])
                    ghs = [(XO2[0], "XO20"), (XO2[1], "XO21"), (Z2, "Z"), (GA[0], "GA0")]
                    for r_ in range(4):
                        gh, ghk = ghs[r_]
                        DMA("sp", gh[0:64, :], ffout.ap()[r_ * 128 + 64:r_ * 128 + 128, :], r=["ffout"], w=[ghk])
                        DMA("sp", gh[64:128, :], ffout.ap()[r_ * 128:r_ * 128 + 64, :], r=["ffout"], w=[ghk])
                    for r_ in range(4):
                        gh, ghk = ghs[r_]
                        stt("dve", HX[:, :], gh[:, :], ctc("OH", r_, r_ + 1), HX[:, :], ALU.mult, ALU.add, [ghk, "CT", "HX"], ["HX"])
                    DMA("sp", hxd.ap()[64:128, :], HX[64:128, :], r=["HX"], w=["hxd"])
                    u2_chunk(0, 0)
                for pi in range(22):
                    wu = WUP[pi_glob % 2]; wuk = "WUP%d" % (pi_glob % 2)
                    dg = DG[0]; dgk = "DG0"
                    ha = HA[0]; hak = "HA0"
                    ga = GA[0]; gak = "GA0"
                    DMA("pool", wu[:, :, 0:128], w_up_v[:, :, pi * 128:(pi + 1) * 128], w=[wuk])
                    DMA("pool", wu[:, :, 128:256], w_up_v[:, :, 2816 + pi * 128:2816 + (pi + 1) * 128], w=[wuk])
                    for t in range(9):
                        act(dg[:, 0, t, :], IDB[:, :], AF.Copy, ["IDB", "CW"], [dgk], scale=CW[:, pi, t:t + 1])
                    for ab in range(2):
                        for nb, (n0, n1) in enumerate(((0, 512), (512, 1024), (1024, 1152))):
                            bb = 3 * ab + nb
                            for kc in range(8):
                                mm(PS[:, bb * 512:bb * 512 + (n1 - n0)], wu[:, kc, ab * 128:(ab + 1) * 128], U2[:, kc, n0:n1],
                                   kc == 0, kc == 7, [wuk, "U2"], ["ps%d" % bb])
                    pa = PS[:, 0:1152].rearrange("p (r c) -> p r c", c=64)
                    pb_ = PS[:, 1536:2688].rearrange("p (r c) -> p r c", c=64)
                    act(ha[:, 1, :, :], pa, AF.Copy, bkeys(0, 3), [hak])
                    act(ha[:, 0, :, 1:64], pa[:, :, 0:63], AF.Copy, bkeys(0, 3), [hak])
                    act(ha[:, 2, :, 0:63], pa[:, :, 1:64], AF.Copy, bkeys(0, 3), [hak])
                    act(HBP[:, :, 1:65], pb_, AF.Copy, bkeys(3, 3), ["HBP"])
                    for blk in range(2):
                        bb = 6 + blk
                        for t in range(9):
                            dr, dc_ = t // 3 - 1, t % 3 - 1
                            mm(bank(bb), dg[:, 0, t, :], ha[:, dc_ + 1, 1 + 8 * blk + dr:9 + 8 * blk + dr, :],
                               t == 0, t == 8, [dgk, hak], ["ps%d" % bb])
                    chb = 22 + pi
                    accv = ACC[:, :].rearrange("p (r c) -> p r c", c=64)
                    for t in range(9):
                        dr, dc_ = t // 3 - 1, t % 3 - 1
                        win = HBP[:, 1 + dr:17 + dr, 1 + dc_:65 + dc_]
                        if t == 0:
                            act(accv, win, AF.Identity, ["HBP", "CW", "CB"], ["ACC"], bias=CB[:, chb:chb + 1], scale=CW[:, chb, 0:1])
                        else:
                            stt("dve", accv, win, CW[:, chb, t:t + 1], accv, ALU.mult, ALU.add, ["HBP", "CW", "ACC"], ["ACC"])
                    act(ga[:, :], bank(6, 2), AF.Gelu_apprx_tanh, ["ps6", "ps7"], [gak], bias=CB[:, pi:pi + 1])
                    tt("dve", GT[:, pi, :], ACC[:, :], ga[:, :], ALU.mult, ["ACC", gak], ["GT"])
                    pi_glob += 1
                if hf == 0:
                    u2_load(1, 0)
                DMA("sp", X2[ci % 2][:, :], x1d[hf * 8 * 128:(hf * 8 + 1) * 128, :], r=["x1d%d" % (hf * 8)], w=["X2%d" % (ci % 2)])
                for c in range(8):
                    gc = hf * 8 + c
                    xc = X2[ci % 2]; xck = "X2%d" % (ci % 2)
                    xo = XO2[ci % 2]; xok = "XO2%d" % (ci % 2)
                    ci += 1
                    if c + 1 < 8:
                        DMA("sp", X2[ci % 2][:, :], x1d[(gc + 1) * 128:(gc + 2) * 128, :], r=["x1d%d" % (gc + 1)], w=["X2%d" % (ci % 2)])
                    pb = 2 + 2 * (c % 2)
                    if hf == 0:
                        u2_load(1, c + 1)
                        u2_chunk(1, c, part=1, load=False)
                    for nb in range(2):
                        for kc in range(22):
                            mm(bank(pb + nb), GT[:, kc, c * 128:(c + 1) * 128], WD[:, kc, nb * 512:(nb + 1) * 512], kc == 0, kc == 21,
                               ["WD%d" % kc, "GT"], ["ps%d" % (pb + nb)])
                    if hf == 0:
                        u2_chunk(1, c, part=2)
                    epilogue(pb, xc, xck, BC2, Z2, xo, xok, folded=False)
                    DMA("sp", out_d[gc * 128:(gc + 1) * 128, :], xo[:, :], r=[xok], w=["out%d" % gc])
                if hf == 0:
                    u2_chunk(1, 8, load=False)

    except _Stop:
        pass

    if STAGE == 1:
        DMA("sp", out_d[0:128, 0:1024], SST[:, :, :, :].rearrange("p a b c -> p (a b c)"), r=["SST"], w=["o"])
        DMA("sp", out_d[128:256, 0:1024], SCTX[:, :, :, :].rearrange("p a b c -> p (a b c)"), r=["SCTXf", "SCTXb"], w=["o"])
        DMA("sp", out_d[256:384, 0:96], MODT[:, :, :].rearrange("p a b -> p (a b)"), r=["MODT"], w=["o"])
        DMA("sp", out_d[384:512, 0:16], LG[:, :], r=["LG"], w=["o"])

    with (nc.semaphore("s_pe") as s_pe, nc.semaphore("s_act") as s_act, nc.semaphore("s_dve") as s_dve,
          nc.semaphore("s_pool") as s_pool, nc.semaphore("s_sp") as s_sp, nc.semaphore("s_cc") as s_cc):
        import contextlib
        with contextlib.ExitStack() as es:
            dsems = [es.enter_context(nc.semaphore("s_d%d" % i)) for i in range(NDS)]
            with nc.Block() as block:
                R.emit(nc, block, dict(pe=s_pe, act=s_act, dve=s_dve, pool=s_pool, sp=s_sp), dsems, s_cc)
    return nc


_NC_CACHE = {}


def kernel(x, c, ctx, c_ctx, w_ada, b_ada, w_in, ret_decay_logit, pool_w, pool_scale,
           w_branch_ret, w_branch_pool, w_out, ln1_g, ln1_b, w_up, conv_w, conv_b, w_down, ln2_g, ln2_b):
    f = lambda a: np.ascontiguousarray(np.asarray(a, dtype=np.float32))
    x = f(x); ctx = f(ctx); c = f(c); c_ctx = f(c_ctx)
    if "nc" not in _NC_CACHE:
        _NC_CACHE["nc"] = build()
    nc = _NC_CACHE["nc"]
    shared = dict(
        w_ada=f(w_ada[0]), bada=f(b_ada[0].reshape(48, 128).T), w_in=f(w_in[0]), lgt=f(ret_decay_logit[0].reshape(1, 16)),
        pool_w=f(pool_w[0]), psc=f(pool_scale[0].reshape(4, 128).T), w_br=f(w_branch_ret[0]), w_bp=f(w_branch_pool[0]),
        w_out=f(w_out[0]), ln1g=f(ln1_g[0].reshape(1, D)), ln1b=f(ln1_b[0].reshape(1, D)),
        ln2g=f(ln2_g[0].reshape(1, D)), ln2b=f(ln2_b[0].reshape(1, D)), w_up=f(w_up[0]),
        cwf=f(conv_w[0].reshape(9, 44, 128).transpose(2, 1, 0).reshape(128, 44 * 9)),
        cbf=f(conv_b[0].reshape(44, 128).T), w_down=f(w_down[0]),
    )
    in_maps = []
    for core in range(NCORE):
        b, s = core // 4, core % 4
        xh = np.zeros((16, D), np.float32)
        if s > 0:
            xh[0:8] = x[b, s * SEG - 8:s * SEG]
        if s < 3:
            xh[8:16] = x[b, (s + 1) * SEG:(s + 1) * SEG + 8]
        cfm = np.concatenate([c[b].reshape(8, 128).T, c_ctx.reshape(8, 128).T], axis=1)
        m = dict(shared)
        m.update(xs=f(x[b, s * SEG:(s + 1) * SEG]), xh=xh, ctxs=f(ctx[b]), cfm=f(cfm), ct=_const_table(core))
        in_maps.append(m)
    res = run_bass_kernel_spmd(nc, in_maps, core_ids=list(range(NCORE)))
    out = np.zeros((2, 8192, D), np.float32)
    for core in range(NCORE):
        b, s = core // 4, core % 4
        out[b, s * SEG:(s + 1) * SEG] = np.asarray(res.results[core]["out"])
    return out
```

```python
import numpy as np
import concourse.bass as bass
import concourse.mybir as mybir
from concourse.bass_utils import run_bass_kernel_spmd

F32 = mybir.dt.float32
BF16 = mybir.dt.bfloat16
AF = mybir.ActivationFunctionType
ALU = mybir.AluOpType

STAGE = 3
NCORE = 8
import os
CUT = int(os.environ.get('KCUT', '99'))


class _Stop(Exception):
    pass

D = 1024
SEG = 2048
NCH = 16
EPS = 1e-6
ALPHA = 2.0 ** 0.25
NDS = 24
GROUPS = [[0, 1, 2, 3], [4, 5, 6, 7]]

_CT = {}
_off = 0
for _n, _w in [("IDENT", 128), ("D1", 128), ("D2", 128), ("EYE8", 128), ("IOTA1", 128), ("IOTA2", 128),
               ("ONES", 128), ("JREV", 1), ("JFWD", 1), ("CEXP", 17), ("NEXPF", 5), ("MASKF", 5),
               ("NEXPB", 5), ("MASKB", 5), ("PMASK", 2), ("PE", 1), ("PO", 1), ("EDGEL", 32), ("EDGER", 32), ("OH", 4), ("HM", 2)]:
    _CT[_n] = (_off, _w)
    _off += _w
NCT = _off


def _const_table(core):
    s = core % 4
    t = np.zeros((128, NCT), np.float32)

    def put(name, arr):
        o, w = _CT[name]
        t[:, o:o + w] = arr

    p = np.arange(128)[:, None].astype(np.float32)
    i = np.arange(128)[None, :].astype(np.float32)
    put("IDENT", np.eye(128, dtype=np.float32))
    put("D1", np.maximum(i - p, 0))
    put("D2", np.maximum(p - i, 0))
    put("EYE8", 0.125 * np.eye(128, dtype=np.float32))
    put("IOTA1", np.broadcast_to(i + 1, (128, 128)))
    put("IOTA2", np.broadcast_to(128 - i, (128, 128)))
    put("ONES", np.ones((128, 128), np.float32))
    put("JREV", 127 - p)
    put("JFWD", p)
    put("CEXP", np.broadcast_to(128.0 * np.arange(17)[None, :], (128, 17)))
    nf = np.zeros(5); mf = np.zeros(5); nb = np.zeros(5); mb = np.zeros(5)
    for r in range(4):
        if r < s:
            nf[r] = s - 1 - r; mf[r] = 1
        if r > s:
            nb[r] = r - s - 1; mb[r] = 1
    nf[4] = s; mf[4] = 1; nb[4] = 3 - s; mb[4] = 1
    put("NEXPF", np.broadcast_to(2048.0 * nf[None, :], (128, 5)))
    put("MASKF", np.broadcast_to(mf[None, :], (128, 5)))
    put("NEXPB", np.broadcast_to(2048.0 * nb[None, :], (128, 5)))
    put("MASKB", np.broadcast_to(mb[None, :], (128, 5)))
    put("PE", (p < 64).astype(np.float32))
    put("PO", (p >= 64).astype(np.float32))
    put("PMASK", np.broadcast_to(np.array([1.0 if s > 0 else 0.0, 1.0 if s < 3 else 0.0])[None, :], (128, 2)))
    L = 8192
    el = np.ones((4, 8)); er = np.ones((4, 8))
    for g, w in enumerate((2, 4, 8, 16)):
        for j in range(8):
            tpos = s * SEG + j
            cnt = min(tpos + w // 2, L) - max(tpos - w // 2, 0)
            el[g, j] = w / cnt
            tpos = s * SEG + SEG - 8 + j
            cnt = min(tpos + w // 2, L) - max(tpos - w // 2, 0)
            er[g, j] = w / cnt
    put("EDGEL", np.broadcast_to(el.reshape(1, 32), (128, 32)))
    put("EDGER", np.broadcast_to(er.reshape(1, 32), (128, 32)))
    oh = np.zeros((128, 4))
    if s > 0:
        oh[0:64, s - 1] = 1
    if s < 3:
        oh[64:128, s + 1] = 1
    put("OH", oh)
    put("HM", np.broadcast_to(np.array([1.0 if s > 0 else 0.0, 1.0 if s < 3 else 0.0])[None, :], (128, 2)))
    return t


class Rec:
    def __init__(self):
        self.ops = []

    def add(self, eng, fn, r=(), w=(), kind="c"):
        self.ops.append(dict(eng=eng, fn=fn, r=tuple(r), w=tuple(w), kind=kind))
        return len(self.ops) - 1

    def barrier(self, keep=()):
        self.ops.append(dict(eng="*", fn=None, r=(), w=(), kind="bar", keep=tuple(keep)))

    def emit(self, nc, block, sems, dsems, ccsem):
        ops = self.ops
        n = len(ops)
        deps = [set() for _ in range(n)]
        lastw = {}
        readers = {}
        last_eng = {}
        dma_since = []
        frontier = set()
        dma_slot_last = {}
        ndma = 0
        npool = 0
        slot_of = {}
        last_cc = []
        for i, o in enumerate(ops):
            if o["kind"] == "bar":
                keep = o.get("keep", ())
                frontier = set(last_eng.values())
                for q in dma_slot_last.values():
                    if keep and ops[q]["w"] and all(k in keep for k in ops[q]["w"]):
                        continue
                    frontier.add(q)
                if not keep:
                    frontier |= set(last_cc)
                kw_ = {k: lastw[k] for k in keep if k in lastw}
                kr_ = {k: readers[k] for k in keep if k in readers}
                lastw.clear(); readers.clear()
                lastw.update(kw_); readers.update(kr_)
                continue
            dp = deps[i]
            dp |= frontier
            for k in o["r"]:
                if k in lastw:
                    dp.update(lastw[k])
            for k in o["w"]:
                if k in lastw:
                    same_burst = (o["kind"] == "dma" and all(ops[q]["kind"] == "dma" for q in lastw[k])
                                  and not readers.get(k))
                    if not same_burst:
                        dp.update(lastw[k])
                for rr in readers.get(k, ()):
                    dp.add(rr)
            for k in o["w"]:
                if o["kind"] == "dma" and k in lastw and all(ops[q]["kind"] == "dma" for q in lastw[k]) and not readers.get(k):
                    lastw[k] = lastw[k] + [i]
                else:
                    lastw[k] = [i]
                readers[k] = []
            for k in o["r"]:
                lst = readers.setdefault(k, [])
                if o["kind"] == "c":
                    lst[:] = [q for q in lst if not (ops[q]["kind"] == "c" and ops[q]["eng"] == o["eng"])]
                lst.append(i)
            if o["kind"] == "dma":
                half = NDS // 2
                if o["eng"] == "pool":
                    sl = half + (npool % half)
                    npool += 1
                else:
                    sl = ndma % half
                    ndma += 1
                slot_of[i] = sl
                if sl in dma_slot_last:
                    dp.add(dma_slot_last[sl])
                dma_slot_last[sl] = i
            elif o["kind"] == "c":
                last_eng[o["eng"]] = i
            elif o["kind"] == "cc":
                last_cc = [i]
            dp.discard(i)
        hasdep = [False] * n
        for i in range(n):
            for d in deps[i]:
                hasdep[d] = True
        tok = [None] * n
        cnt = {e: 0 for e in sems}
        dcnt = [0] * NDS
        cccnt = 0
        pos_in_eng = [0] * n
        epos = {}
        for i, o in enumerate(ops):
            if o["kind"] == "bar":
                continue
            if o["kind"] == "dma":
                sl = slot_of[i]
                dcnt[sl] += 16
                tok[i] = (dsems[sl], dcnt[sl], 16)
            elif o["kind"] == "cc":
                cccnt += 1
                tok[i] = (ccsem, cccnt, 1)
            else:
                e = o["eng"]
                epos[e] = epos.get(e, 0) + 1
                pos_in_eng[i] = epos[e]
                if hasdep[i]:
                    cnt[e] += 1
                    tok[i] = (sems[e], cnt[e], 1)
        final_d = list(dcnt)

        def run_engine(ename, eh):
            waited = {}
            for i, o in enumerate(ops):
                if o["kind"] == "bar" or o["eng"] != ename:
                    continue
                for d in sorted(deps[i]):
                    od = ops[d]
                    t = tok[d]
                    if t is None:
                        continue
                    if od["kind"] == "c" and od["eng"] == ename:
                        if ename == "pe":
                            continue
                        if o["kind"] == "c" and pos_in_eng[i] - pos_in_eng[d] > 3:
                            continue
                    sem, val, _ = t
                    if waited.get(sem.num, 0) < val:
                        eh.wait_ge(sem, val)
                        waited[sem.num] = val
                ins = o["fn"](eh)
                t = tok[i]
                if t is not None:
                    ins.then_inc(t[0], t[2])
            if ename == "sp":
                for sl in range(NDS):
                    if final_d[sl] > 0:
                        eh.wait_ge(dsems[sl], final_d[sl])

        block.sync(lambda e: run_engine("sp", e))
        block.tensor(lambda e: run_engine("pe", e))
        block.scalar(lambda e: run_engine("act", e))
        block.vector(lambda e: run_engine("dve", e))
        block.gpsimd(lambda e: run_engine("pool", e))


def build():
    nc = bass.Bass("TRN2", target_bir_lowering=False)
    R = Rec()

    def din(name, shape, dt=F32):
        return nc.dram_tensor(name, list(shape), dt, kind="ExternalInput").ap()

    xs = din("xs", [SEG, D]); xh = din("xh", [16, D]); ctxs = din("ctxs", [256, D])
    cfm = din("cfm", [128, 16]); ct_d = din("ct", [128, NCT]); lgt = din("lgt", [1, 16])
    w_ada = din("w_ada", [D, 6 * D]); bada = din("bada", [128, 48])
    w_in = din("w_in", [D, 5632]); pool_w = din("pool_w", [4, 128, 128]); psc = din("psc", [128, 4])
    w_br = din("w_br", [D, D]); w_bp = din("w_bp", [512, D]); w_out = din("w_out", [D, D])
    ln1g = din("ln1g", [1, D]); ln1b = din("ln1b", [1, D]); ln2g = din("ln2g", [1, D]); ln2b = din("ln2b", [1, D])
    w_up = din("w_up", [D, 5632]); cwf = din("cwf", [128, 44 * 9]); cbf = din("cbf", [128, 44])
    w_down = din("w_down", [2816, D])
    out_d = nc.dram_tensor("out", [SEG, D], F32, kind="ExternalOutput").ap()
    x1d = nc.dram_tensor("x1d", [SEG, D], F32).ap()
    ccin = nc.dram_tensor("ccin", [128, 1024], F32)
    ccout = nc.dram_tensor("ccout", [4 * 128, 1024], F32)
    ffin = nc.dram_tensor("ffin", [128, 1024], F32)
    ffout = nc.dram_tensor("ffout", [4 * 128, 1024], F32)
    hxd = nc.dram_tensor("hxd", [128, 1024], F32)

    w_in_v = w_in.rearrange("(kc p) n -> p kc n", p=128)
    w_up_v = w_up.rearrange("(kc p) n -> p kc n", p=128)
    w_ada_v = w_ada.rearrange("(kc p) n -> p kc n", p=128)

    cur = [16640]

    def alloc(name, shape, dt):
        nb = int(np.prod(shape[1:])) * (4 if dt == F32 else 2)
        nb = (nb + 31) // 32 * 32
        assert cur[0] + nb <= 229120, (name, cur[0], nb)
        t = nc.alloc_sbuf_tensor_at(name, list(shape), dt, offset=cur[0])
        cur[0] += nb
        return t

    CT = alloc("CT", [128, NCT], F32)

    def ctc(name, a=0, b=None):
        o, w = _CT[name]
        return CT[:, o + a:o + (w if b is None else b)]

    IDB = alloc("IDB", [128, 128], BF16)
    LG = alloc("LG", [128, 16], F32)
    LGP = alloc("LGP", [128, 8], F32)
    SCR = alloc("SCR", [128, 6, 16], F32)
    MT = alloc("MT", [128, 8, 128], F32)
    QFM = alloc("QFM", [128, 4, 4, 128], F32)
    KF = alloc("KF", [128, 8], F32)
    KB = alloc("KB", [128, 8], F32)
    CDX = alloc("CDX", [128, 2, 4, 128], F32)
    CD = alloc("CD", [128, 8], F32)
    CDPOW = alloc("CDPOW", [128, 8, 17], F32)
    COEF = alloc("COEF", [128, 8, 5], F32)
    CFM = alloc("CFM", [128, 16], F32)
    SC = alloc("SC", [128, 16], BF16)
    BADA = alloc("BADA", [128, 48], F32)
    MODT = alloc("MODT", [128, 48, 2], F32)
    OPSC1 = alloc("OPSC1", [128, 8, 2], F32)
    OPSC2 = alloc("OPSC2", [128, 8], F32)
    PSC = alloc("PSC", [128, 4], F32)
    CW = alloc("CW", [128, 44, 9], F32)
    CB = alloc("CB", [128, 44], F32)
    SST = alloc("SST", [128, 2, 4, 128], F32)
    SRUN = alloc("SRUN", [128, 4, 128], F32)
    LNS = alloc("LNS", [128, 2, 16], F32)
    LNM = alloc("LNM", [128, 2, 4], F32)
    W_BASE = cur[0]
    W_SIZE = 34816
    cur[0] += W_SIZE
    BIG_BASE = cur[0]
    WB_BASE = 229120 - 45056

    def arena(base):
        cur[0] = base

    PS = nc.alloc_psum_tensor("PS", [128, 4096], F32)

    def bank(b, n=1):
        return PS[:, b * 512:(b + n) * 512]

    def bkeys(b, n=1):
        return ["ps%d" % k for k in range(b, b + n)]

    def A(eng, fn, r=(), w=()):
        return R.add(eng, fn, r, w)

    def DMA(eng, out, in_, r=(), w=()):
        return R.add(eng, lambda e, o=out, i=in_: e.dma_start(out=o, in_=i), r, w, kind="dma")

    def mm(out, lhsT, rhs, start, stop, r, w):
        return R.add("pe", lambda e, o=out, l=lhsT, rh=rhs, s=start, p=stop: e.matmul(o, l, rh, start=s, stop=p), r, w)

    def act(out, in_, func, r, w, bias=0.0, scale=1.0):
        return R.add("act", lambda e, o=out, i=in_, f=func, b=bias, s=scale: e.activation(out=o, in_=i, func=f, bias=b, scale=s), r, w)

    def ts(eng, out, in0, s1, s2, op0, op1, r, w):
        if s2 is None:
            return R.add(eng, lambda e, o=out, i=in0, a=s1, p0=op0: e.tensor_scalar(out=o, in0=i, scalar1=a, scalar2=None, op0=p0), r, w)
        return R.add(eng, lambda e, o=out, i=in0, a=s1, b=s2, p0=op0, p1=op1: e.tensor_scalar(out=o, in0=i, scalar1=a, scalar2=b, op0=p0, op1=p1), r, w)

    def tt(eng, out, in0, in1, op, r, w):
        return R.add(eng, lambda e, o=out, a=in0, b=in1, p=op: e.tensor_tensor(out=o, in0=a, in1=b, op=p), r, w)

    def stt(eng, out, in0, scalar, in1, op0, op1, r, w):
        return R.add(eng, lambda e, o=out, a=in0, s=scalar, b=in1, p0=op0, p1=op1: e.scalar_tensor_tensor(out=o, in0=a, scalar=s, in1=b, op0=p0, op1=p1), r, w)

    def cp(eng, out, in_, r, w):
        return R.add(eng, lambda e, o=out, i=in_: e.tensor_copy(out=o, in_=i), r, w)


    def cut(n):
        if CUT == n:
            raise _Stop()

    try:
        DMA("sp", CT[:, :], ct_d, w=["CT"])
        DMA("sp", LG[:, :], lgt.partition_broadcast(128).rearrange("p o n -> p (o n)"), w=["LG"])
        DMA("sp", CFM[:, :], cfm, w=["CFM"])
        DMA("sp", BADA[:, :], bada, w=["BADA"])
        DMA("sp", PSC[:, :], psc, w=["PSC"])
        DMA("sp", CW[:, :, :].rearrange("p a b -> p (a b)"), cwf, w=["CW"])
        DMA("sp", CB[:, :], cbf, w=["CB"])
        cp("dve", IDB[:, :], ctc("IDENT"), ["CT"], ["IDB"])
        T0 = SCR[:, 0, :]; T1 = SCR[:, 1, :]; T2 = SCR[:, 2, :]; T3 = SCR[:, 3, :]
        act(T0, LG[:, :], AF.Exp, ["LG"], ["T0"], scale=-1.0)
        ts("dve", T1, T0, 2.0, None, ALU.add, None, ["T0"], ["T1"])
        R.add("dve", lambda e: e.reciprocal(out=T1, in_=T1), ["T1"], ["T1"])
        tt("dve", T1, T0, T1, ALU.mult, ["T0", "T1"], ["T1"])
        tt("dve", T2, T1, T1, ALU.mult, ["T1"], ["T2"])
        ts("dve", T3, T2, 1.0 / 15, 1.0 / 13, ALU.mult, ALU.add, ["T2"], ["T3"])
        for cst in (1.0 / 11, 1.0 / 9, 1.0 / 7, 1.0 / 5, 1.0 / 3, 1.0):
            tt("dve", T3, T3, T2, ALU.mult, ["T3", "T2"], ["T3"])
            ts("dve", T3, T3, cst, None, ALU.add, None, ["T3"], ["T3"])
        tt("dve", T3, T3, T1, ALU.mult, ["T3", "T1"], ["T3"])
        ts("dve", LG[:, :], T3, -2.0, None, ALU.mult, None, ["T3"], ["LG"])
        if CUT == -3:
            DMA("sp", out_d[128:256, 0:16], LG[:, :], r=["LG"], w=["o"])
        cut(-3)
        LGv = LG[:, :].rearrange("p (d m r) -> p d m r", d=2, m=4, r=2)
        LGPv = LGP[:, :].rearrange("p (d m) -> p d m", d=2)
        cp("dve", LGPv[0:64], LGv[0:64, :, :, 0], ["LG"], ["LGP"])
        cp("dve", LGPv[64:128], LGv[64:128, :, :, 1], ["LG"], ["LGP"])
        arena(BIG_BASE)
        UT = alloc("UT", [128, 8, 2064], BF16)
        VR = alloc("VR", [128, 16, 1024], BF16)
        AFB = alloc("AFB", [128, 16, 4, 128], BF16)
        KVB = alloc("KVB", [128, 17, 4, 128], BF16)
        XB = [alloc("XB%d" % i, [128, 1024], F32) for i in range(2)]
        XN = [alloc("XN%d" % i, [128, 1024], BF16) for i in range(2)]
        KW = [alloc("KW%d" % i, [128, 2, 512], BF16) for i in range(2)]
        UCS = [alloc("UC%d" % i, [128, 8, 128], BF16) for i in range(2)]
        UH = alloc("UH", [128, 8, 16], BF16)
        VC = alloc("VC", [128, 1024], BF16)
        PACC = alloc("PACC", [128, 2, 4, 128], F32)
        SCTX = alloc("SCTX", [128, 2, 4, 128], F32)
        KVS = alloc("KVS", [128, 2, 4, 128], F32)
        PGB = [alloc("PGB%d" % i, [128, 2, 4, 128], F32) for i in range(2)]
        P1_END = cur[0]
        arena(W_BASE)
        WKV = alloc("WKV", [128, 8, 1536], BF16)
        WAD = [AFB[:, :, :, :].rearrange("p a b c -> p (a b c)").rearrange("p (k n) -> p k n", k=8),
               KVB[:, 0:16, :, :].rearrange("p a b c -> p (a b c)").rearrange("p (k n) -> p k n", k=8)]
        MSC = XB[0][:, 0:128]
        MSC2 = XB[0][:, 128:256]
        for h in range(8):
            ts("dve", MSC, ctc("D1"), LG[:, h:h + 1], None, ALU.mult, None, ["CT", "LG"], ["MSC", "XB0"])
            stt("dve", MSC2, ctc("D2"), LG[:, 8 + h:9 + h], MSC, ALU.mult, ALU.add, ["CT", "LG", "MSC"], ["MSC2", "XB0"])
            act(MSC2, MSC2, AF.Exp, ["MSC2"], ["MSC2", "XB0"])
            stt("dve", MT[:, h, :], MSC2, 0.125, ctc("EYE8"), ALU.mult, ALU.add, ["MSC2", "CT", "XB0"], ["MT"])
        for m in range(4):
            act(QFM[:, 0, m, :], ctc("IOTA1"), AF.Exp, ["CT", "LGP"], ["QF"], scale=LGP[:, m:m + 1])
            act(QFM[:, 2, m, :], ctc("IOTA2"), AF.Exp, ["CT", "LGP"], ["QF"], scale=LGP[:, 4 + m:5 + m])
        cp("dve", QFM[:, 1, :, :], QFM[:, 0, :, :], ["QF"], ["QF"])
        cp("dve", QFM[:, 3, :, :], QFM[:, 2, :, :], ["QF"], ["QF"])
        for q_ in range(4):
            lo = 64 if q_ % 2 == 0 else 0
            R.add("dve", lambda e, t=QFM[lo:lo + 64, q_, :, :]: e.memset(t, 0.0), ["QF"], ["QF"])
        act(KF[:, :], LG[:, 0:8], AF.Exp, ["LG", "CT"], ["KF"], scale=ctc("JREV"))
        act(KB[:, :], LG[:, 8:16], AF.Exp, ["LG", "CT"], ["KB"], scale=ctc("JFWD"))
        ts("dve", KF[:, :], KF[:, :], 0.125, None, ALU.mult, None, ["KF"], ["KF"])
        ts("dve", KB[:, :], KB[:, :], 0.125, None, ALU.mult, None, ["KB"], ["KB"])
        act(CD[:, :], LGP[:, :], AF.Exp, ["LGP"], ["CD"], scale=128.0)
        for j in range(8):
            act(CDPOW[:, j, :], ctc("CEXP"), AF.Exp, ["CT", "LGP"], ["CDPOW"], scale=LGP[:, j:j + 1])
            act(COEF[:, j, :], ctc("NEXPF" if j < 4 else "NEXPB"), AF.Exp, ["CT", "LGP"], ["COEF"], scale=LGP[:, j:j + 1])
            tt("dve", COEF[:, j, :], COEF[:, j, :], ctc("MASKF" if j < 4 else "MASKB"), ALU.mult, ["COEF", "CT"], ["COEF"])
            d_, m_ = j // 4, j % 4
            ts("dve", CDX[:, d_, m_, :], ctc("ONES"), CD[:, j:j + 1], None, ALU.mult, None, ["CT", "CD"], ["CDX"])
        if CUT == -2:
            DMA("sp", out_d[128:256, 0:16], LG[:, :], r=["LG"], w=["o"])
            DMA("sp", out_d[256:384, 0:1024], MT[:, :, :].rearrange("p a b -> p (a b)"), r=["MT"], w=["o"])
        cut(-2)
        act(SC[:, :], CFM[:, :], AF.Silu, ["CFM"], ["SC"])
        SCv = SC[:, :].rearrange("p (v k) -> p k v", v=2)
        def mod_dma(v):
            DMA("pool", WAD[v % 2], w_ada_v[:, :, v * 1024:(v + 1) * 1024], w=["WAD%d" % (v % 2)])

        def mod_mm(v):
            wb = WAD[v % 2]
            for j in range(8):
                col = (v * 8 + j) * 2
                for kc in range(8):
                    mm(PS[:, col:col + 2], wb[:, kc, j * 128:(j + 1) * 128], SCv[:, kc, :], kc == 0, kc == 7,
                       ["WAD%d" % (v % 2), "SC"], ["ps0"])

        mod_dma(0)
        mod_dma(1)
        mod_mm(0)
        mod_dma(2)
        mod_mm(1)
        mod_dma(3)
        tt("dve", MODT[:, 0:16, :], PS[:, 0:32].rearrange("p (a b) -> p a b", b=2),
           BADA[:, 0:16].unsqueeze(2).to_broadcast([128, 16, 2]), ALU.add, ["ps0", "BADA"], ["MODT"])
        ts("dve", OPSC1[:, :, :], MODT[:, 8:16, :], 1.0, None, ALU.add, None, ["MODT"], ["OPSC"])

        lnctr = [0]

        def ln_stats(xt, ntok, rk):
            b = lnctr[0] % 2
            lnctr[0] += 1
            st = LNS[0:ntok, b, :]
            mv = LNM[0:ntok, b, :]
            k = "LN%d" % b
            R.add("dve", lambda e, o=st[:, 0:6], i=xt[:, 0:512]: e.bn_stats(out=o, in_=i), rk, [k])
            R.add("dve", lambda e, o=st[:, 6:12], i=xt[:, 512:1024]: e.bn_stats(out=o, in_=i), rk, [k])
            R.add("dve", lambda e, o=mv[:, 0:2], i=st[:, 0:12]: e.bn_aggr(out=o, in_=i), [k], [k])
            act(mv[:, 2:3], mv[:, 1:2], AF.Sqrt, [k], [k], bias=EPS)
            R.add("dve", lambda e, o=mv[:, 2:3]: e.reciprocal(out=o, in_=o), [k], [k])
            return mv[:, 0:1], mv[:, 2:3], k

        def ln_to_T(xt, ntok, rk, XN, xnk, dst_fn, dstk, sh_fn, osc_fn, pb, part=0):
            if part in (0, 1):
                mean, rstd, k = ln_stats(xt, ntok, rk)
                ts("dve", XN[0:ntok, :], xt, mean, rstd, ALU.subtract, ALU.mult, rk + [k], [xnk])
            if part == 1:
                return
            pbf = bank(pb).bitcast(BF16)
            for kc in range(8):
                R.add("pe", lambda e, o=pbf[:, kc * 128:kc * 128 + ntok], i=XN[0:ntok, kc * 128:(kc + 1) * 128],
                      idn=IDB[0:ntok, 0:ntok]: e.transpose(o, i, idn), [xnk, "IDB"], ["ps%d" % pb])
            for kc in range(8):
                act(dst_fn(kc), pbf[:, kc * 128:kc * 128 + ntok], AF.Identity, ["ps%d" % pb, "MODT", "OPSC"], [dstk],
                    bias=sh_fn(kc), scale=osc_fn(kc))

        for kc in range(8):
            DMA("pool", WKV[:, kc, :], w_in_v[:, kc, 0:1536], w=["W"])
        R.add("dve", lambda e: e.memset(PACC[:, :, :, :], 0.0), [], ["PACCf", "PACCb"])
        R.add("dve", lambda e: e.memset(SCTX[:, :, :, :], 0.0), [], ["SCTXf", "SCTXb"])

        def kv_chunk(u_fn, uk, vdst, vk, accF, accFk, accB, accBk, cidx, bi, afdst=None, kvbdst=None):
            kw = KW[bi % 2]
            kwk = "KW%d" % (bi % 2)
            for kc in range(8):
                mm(bank(1), u_fn(kc), WKV[:, kc, 0:512], kc == 0, kc == 7, [uk, "W"], ["ps1"])
            for nb in range(2):
                for kc in range(8):
                    mm(bank(2 + nb), u_fn(kc), WKV[:, kc, 512 + nb * 512:1024 + nb * 512], kc == 0, kc == 7, [uk, "W"], ["ps%d" % (2 + nb)])
            k3 = bank(1).rearrange("p (h d) -> p h d", d=64)
            tt("dve", kw[:, 0, :].rearrange("p (h d) -> p h d", d=64), k3, KF[:, :].unsqueeze(2).to_broadcast([128, 8, 64]),
               ALU.mult, ["ps1", "KF"], [kwk])
            tt("dve", kw[:, 1, :].rearrange("p (h d) -> p h d", d=64), k3, KB[:, :].unsqueeze(2).to_broadcast([128, 8, 64]),
               ALU.mult, ["ps1", "KB"], [kwk])
            act(vdst, bank(2, 2), AF.Copy, ["ps2", "ps3"], [vk])
            for d_ in range(2):
                for m in range(4):
                    b0 = 4 + 2 * d_ + m // 2
                    mm(PS[:, b0 * 512 + (m % 2) * 256: b0 * 512 + (m % 2) * 256 + 256], kw[:, d_, m * 128:(m + 1) * 128],
                       vdst[:, m * 256:(m + 1) * 256], True, True, [kwk, vk], ["ps%d" % b0])
            for d_ in range(2):
                src = bank(4 + 2 * d_, 2).rearrange("p (m x) -> p m x", x=256)
                act(KVS[0:64, d_, :, :], src[0:64, :, 0:128], AF.Copy, ["ps%d" % (4 + 2 * d_), "ps%d" % (5 + 2 * d_)], ["KVS%d" % d_])
                act(KVS[64:128, d_, :, :], src[64:128, :, 128:256], AF.Copy, ["ps%d" % (4 + 2 * d_), "ps%d" % (5 + 2 * d_)], ["KVS%d" % d_])
            if afdst is not None:
                act(afdst, accF, AF.Copy, [accFk], ["AFB", "WAD0"])
            tt("dve", accF, accF, CDX[:, 0, :, :], ALU.mult, [accFk, "CDX"], [accFk])
            tt("dve", accF, accF, KVS[:, 0, :, :], ALU.add, [accFk, "KVS0"], [accFk])
            for m in range(4):
                stt("dve", accB[:, m, :], KVS[:, 1, m, :], CDPOW[:, 4 + m, cidx:cidx + 1], accB[:, m, :], ALU.mult, ALU.add,
                    ["KVS1", "CDPOW", accBk], [accBk])
            if kvbdst is not None:
                act(kvbdst, KVS[:, 1, :, :], AF.Copy, ["KVS1"], ["KVB", "WAD1"])

        for cc_ in range(2):
            xb = XB[cc_ % 2]
            DMA("sp", xb[:, :], ctxs[cc_ * 128:(cc_ + 1) * 128, :], w=["XB%d" % (cc_ % 2)])
            ln_to_T(xb[:, :], 128, ["XB%d" % (cc_ % 2)], XN[cc_ % 2], "XN%d" % (cc_ % 2),
                    lambda kc, cc_=cc_: UCS[cc_][:, kc, :], "UC%d" % cc_, lambda kc: MODT[:, kc, 1:2], lambda kc: OPSC1[:, kc, 1:2], 7)
        DMA("sp", XB[0][0:16, :], xh, w=["XB0"])
        ln_to_T(XB[0][0:16, :], 16, ["XB0"], XN[0], "XN0", lambda kc: UH[:, kc, 0:16], "UH",
                lambda kc: MODT[:, kc, 0:1], lambda kc: OPSC1[:, kc, 0:1], 7)
        ts("dve", UT[:, :, 0:8], UH[:, :, 0:8], ctc("PMASK", 0, 1), None, ALU.mult, None, ["UH", "CT"], ["UTh"])
        ts("dve", UT[:, :, 2056:2064], UH[:, :, 8:16], ctc("PMASK", 1, 2), None, ALU.mult, None, ["UH", "CT"], ["UTh"])
        def p1_a(c):
            xb = XB[c % 2]
            DMA("sp", xb[:, :], xs[c * 128:(c + 1) * 128, :], w=["XB%d" % (c % 2)])
            ln_to_T(xb[:, :], 128, ["XB%d" % (c % 2)], XN[c % 2], "XN%d" % (c % 2),
                    lambda kc, c=c: UT[:, kc, 8 + c * 128:8 + (c + 1) * 128], "UT%d" % c,
                    lambda kc: MODT[:, kc, 0:1], lambda kc: OPSC1[:, kc, 0:1], 7)

        def p1_b(c):
            kv_chunk(lambda kc, c=c: UT[:, kc, 8 + c * 128:8 + (c + 1) * 128], "UT%d" % c, VR[:, c, :], "VR%d" % c,
                     PACC[:, 0, :, :], "PACCf", PACC[:, 1, :, :], "PACCb", c, c,
                     afdst=AFB[:, c, :, :], kvbdst=KVB[:, c, :, :])

        for c in range(NCH):
            p1_a(c)
            if c in (1, 5, 9, 13):
                v_ = 2 + (c - 1) // 4
                mod_mm(v_)
                if v_ + 2 < 6:
                    mod_dma(v_ + 2)
        tt("dve", MODT[:, 16:48, :], PS[:, 32:96].rearrange("p (a b) -> p a b", b=2),
           BADA[:, 16:48].unsqueeze(2).to_broadcast([128, 32, 2]), ALU.add, ["ps0", "BADA"], ["MODT2"])
        ts("dve", OPSC2[:, :], MODT[:, 32:40, 0], 1.0, None, ALU.add, None, ["MODT2"], ["OPSC2"])
        for cc_ in range(2):
            kv_chunk(lambda kc, cc_=cc_: UCS[cc_][:, kc, :], "UC%d" % cc_, VC[:, :], "VC", SCTX[:, 0, :, :], "SCTXf",
                     SCTX[:, 1, :, :], "SCTXb", cc_, cc_)
        for c in range(NCH):
            p1_b(c)
        DMA("sp", ccin.ap(), PACC[:, :, :, :].rearrange("p a b c -> p (a b c)"), r=["PACCf", "PACCb"], w=["ccin"])
        R.add("pool", lambda e: e.collective_compute("AllGather", ALU.bypass, replica_groups=GROUPS,
                                                     ins=[ccin.ap().opt()], outs=[ccout.ap().opt()]),
              ["ccin"], ["ccout"], kind="cc")
        arena(W_BASE)
        WQ = alloc("WQ", [128, 8, 512], BF16)
        WK = alloc("WK", [128, 8, 512], BF16)
        WG = alloc("WG", [128, 8, 1024], BF16)
        for kc in range(8):
            DMA("pool", WK[:, kc, :], w_in_v[:, kc, 0:512], r=[], w=["W"])
            DMA("pool", WQ[:, kc, :], w_in_v[:, kc, 1536:2048], w=["W"])
            DMA("pool", WG[:, kc, :], w_in_v[:, kc, 2048:3072], w=["W"])
        for d_ in range(2):
            for m in range(4):
                ts("dve", SST[:, d_, m, :], SCTX[:, d_, m, :], COEF[:, d_ * 4 + m, 4:5], None, ALU.mult, None,
                   ["SCTXf", "SCTXb", "COEF"], ["SST"])
        for r_ in range(4):
            pg = PGB[r_ % 2]
            DMA("sp", pg[:, :, :, :].rearrange("p a b c -> p (a b c)"), ccout.ap()[r_ * 128:(r_ + 1) * 128, :],
                r=["ccout"], w=["PGB%d" % (r_ % 2)])
            for d_ in range(2):
                for m in range(4):
                    stt("dve", SST[:, d_, m, :], pg[:, d_, m, :], COEF[:, d_ * 4 + m, r_:r_ + 1], SST[:, d_, m, :], ALU.mult, ALU.add,
                        ["PGB%d" % (r_ % 2), "COEF", "SST"], ["SST"])
        cp("dve", SRUN[:, :, :], SST[:, 1, :, :], ["SST"], ["SRUN"])
        act(KVB[:, 16, :, :], SRUN[:, :, :], AF.Copy, ["SRUN"], ["SB16"])
        R.barrier()
        if CUT == 1:
            DMA("sp", out_d[0:128, 0:1024], SST[:, :, :, :].rearrange("p a b c -> p (a b c)"), r=["SST"], w=["o"])
            DMA("sp", out_d[128:256, 0:1024], SCTX[:, :, :, :].rearrange("p a b c -> p (a b c)"), r=["SCTXf", "SCTXb"], w=["o"])
            DMA("sp", out_d[256:384, 0:1024], PACC[:, :, :, :].rearrange("p a b c -> p (a b c)"), r=["PACCf", "PACCb"], w=["o"])
        cut(1)

        arena(BIG_BASE + 8 * 2064 * 2 + 16 * 1024 * 2 + 16 * 512 * 2 + 17 * 512 * 2)
        QK = [alloc("QK%d" % i, [128, 7, 512], BF16) for i in range(2)]
        STt = [alloc("ST%d" % i, [128, 8, 128], BF16) for i in range(2)]
        SG = [alloc("SG%d" % i, [128, 1024], F32) for i in range(2)]
        YN = alloc("YN", [128, 1024], F32)
        RGT = [alloc("RGT%d" % i, [128, 1024], BF16) for i in range(2)]
        YST = alloc("YST", [128, 8, 6], F32)
        YMV = alloc("YMV", [128, 8, 4], F32)
        def p2a_a(c):
            for m in range(4):
                stt("dve", AFB[:, c, m, :], SST[:, 0, m, :], CDPOW[:, m, c:c + 1], AFB[:, c, m, :],
                    ALU.mult, ALU.add, ["SST", "CDPOW", "AFB"], ["SF%d" % c])
            if c < NCH - 1:
                tt("dve", SRUN[:, :, :], SRUN[:, :, :], CDX[:, 1, :, :], ALU.mult, ["SRUN", "CDX"], ["SRUN"])
                tt("dve", SRUN[:, :, :], SRUN[:, :, :], KVB[:, c + 1, :, :], ALU.add, ["SRUN", "KVB", "SB%d" % (c + 1)], ["SRUN"])
                act(KVB[:, c + 1, :, :], SRUN[:, :, :], AF.Copy, ["SRUN"], ["SB%d" % (c + 1), "KVB"])
            qk = QK[c % 2]; qkk = "QK%d" % (c % 2)
            st = STt[c % 2]; stk = "ST%d" % (c % 2)
            ucols = slice(8 + c * 128, 8 + (c + 1) * 128)
            sg = SG[c % 2]; sgk_ = "SG%d" % (c % 2)
            for m in range(4):
                for kc in range(8):
                    mm(PS[:, m * 128:(m + 1) * 128], WQ[:, kc, m * 128:(m + 1) * 128], UT[:, kc, ucols], kc == 0, kc == 7,
                       ["W", "UT%d" % c], ["ps0"])
            for m in range(4):
                for kc in range(8):
                    mm(PS[:, 512 + m * 128:512 + (m + 1) * 128], WK[:, kc, m * 128:(m + 1) * 128], UT[:, kc, ucols], kc == 0, kc == 7,
                       ["W", "UT%d" % c], ["ps1"])
            for nb in range(2):
                for kc in range(8):
                    mm(bank(2 + nb), UT[:, kc, ucols], WG[:, kc, nb * 512:(nb + 1) * 512], kc == 0, kc == 7,
                       ["W", "UT%d" % c], ["ps%d" % (2 + nb)])
            for q_ in range(4):
                tt("dve", qk[:, q_, :], bank(0), QFM[:, q_, :, :].rearrange("p a b -> p (a b)"), ALU.mult, ["ps0", "QF"], [qkk])
            act(qk[:, 4, :], bank(0), AF.Copy, ["ps0"], [qkk])
            act(qk[:, 5, :], bank(1), AF.Copy, ["ps1", "CT"], [qkk], scale=ctc("PE"))
            act(qk[:, 6, :], bank(1), AF.Copy, ["ps1", "CT"], [qkk], scale=ctc("PO"))
            act(sg[:, :], bank(2, 2), AF.Silu, ["ps2", "ps3"], [sgk_])

        def p2a_b(c):
            qk = QK[c % 2]; qkk = "QK%d" % (c % 2)
            st = STt[c % 2]; stk = "ST%d" % (c % 2)
            ucols = slice(8 + c * 128, 8 + (c + 1) * 128)
            sg = SG[c % 2]; sgk_ = "SG%d" % (c % 2)
            for h in range(8):
                m, par = h // 2, h % 2
                pr = slice(par * 64, par * 64 + 64)
                b0 = 4 + h // 4
                mm(PS[:, b0 * 512 + (h % 4) * 128: b0 * 512 + (h % 4 + 1) * 128], qk[:, 5 + par, m * 128:(m + 1) * 128],
                   qk[:, 4, m * 128:(m + 1) * 128], True, True, [qkk], ["ps%d" % b0])
            for hb in range(2):
                tt("dve", st[:, hb * 4:(hb + 1) * 4, :], bank(4 + hb).rearrange("p (a b) -> p a b", b=128), MT[:, hb * 4:(hb + 1) * 4, :],
                   ALU.mult, ["ps%d" % (4 + hb), "MT"], [stk])
            for h in range(8):
                m, par = h // 2, h % 2
                pr = slice(par * 64, par * 64 + 64)
                b0 = 6 + h // 4
                o = PS[:, b0 * 512 + (h % 4) * 128: b0 * 512 + (h % 4 + 1) * 128]
                mm(o, st[:, h, :], VR[:, c, h * 128:(h + 1) * 128], True, False, [stk, "VR%d" % c], ["ps%d" % b0])
                mm(o, qk[:, 0 + par, m * 128:(m + 1) * 128], AFB[:, c, m, :], False, False, [qkk, "SF%d" % c], ["ps%d" % b0])
                mm(o, qk[:, 2 + par, m * 128:(m + 1) * 128], KVB[:, c + 1, m, :], False, True, [qkk, "SB%d" % (c + 1)], ["ps%d" % b0])
            for h in range(8):
                b0 = 6 + h // 4
                yh = PS[:, b0 * 512 + (h % 4) * 128: b0 * 512 + (h % 4 + 1) * 128]
                R.add("dve", lambda e, o=YST[:, h, :], i=yh: e.bn_stats(out=o, in_=i), ["ps%d" % b0], ["YST"])
            for h in range(8):
                R.add("dve", lambda e, o=YMV[:, h, 0:2], i=YST[:, h, :]: e.bn_aggr(out=o, in_=i), ["YST"], ["YMV"])
            act(YMV[:, :, 2], YMV[:, :, 1], AF.Sqrt, ["YMV"], ["YMV"], bias=EPS)
            R.add("dve", lambda e: e.reciprocal(out=YMV[:, :, 2], in_=YMV[:, :, 2]), ["YMV"], ["YMV"])
            stt("dve", YMV[:, :, 3], YMV[:, :, 0], -1.0, YMV[:, :, 2], ALU.mult, ALU.mult, ["YMV"], ["YMV"])
            for h in range(8):
                b0 = 6 + h // 4
                yh = PS[:, b0 * 512 + (h % 4) * 128: b0 * 512 + (h % 4 + 1) * 128]
                act(YN[:, h * 128:(h + 1) * 128], yh, AF.Identity, ["ps%d" % b0, "YMV"], ["YN"], bias=YMV[:, h, 3:4], scale=YMV[:, h, 2:3])
            rgt = RGT[c % 2]; rgk = "RGT%d" % (c % 2)
            tt("dve", rgt[:, :], YN[:, :], sg[:, :], ALU.mult, ["YN", sgk_], [rgk])

        def p2a_b2(c):
            rgt = RGT[c % 2]; rgk = "RGT%d" % (c % 2)
            pbf = bank(4).bitcast(BF16)
            for kc in range(8):
                R.add("pe", lambda e, o=pbf[:, kc * 128:(kc + 1) * 128], i=rgt[:, kc * 128:(kc + 1) * 128]: e.transpose(o, i, IDB[:, :]),
                      [rgk, "IDB"], ["ps4"])
            act(VR[:, c, :], pbf, AF.Copy, ["ps4"], ["VR%d" % c])

        w_br_v = w_br.rearrange("(kc p) n -> p kc n", p=128)

        def prefetch_2bi():
            arena(W_BASE)
            wbr = alloc("WBR", [128, 8, 1024], BF16)
            wga = alloc("WGA", [128, 8, 1024], BF16)
            for kc in range(8):
                DMA("pool", wbr[:, kc, :], w_br_v[:, kc, :], w=["W"])
                DMA("pool", wga[:, kc, :], w_in_v[:, kc, 3584:4608], w=["W"])
            return wbr, wga

        order2a = list(range(NCH - 1, -1, -1))
        p2a_a(order2a[0])
        for i_, c in enumerate(order2a):
            if i_ + 1 < NCH:
                p2a_a(order2a[i_ + 1])
            if i_ == NCH - 2:
                WBR, WGA = prefetch_2bi()
            p2a_b(c)
            if i_ >= 1:
                p2a_b2(order2a[i_ - 1])
        p2a_b2(order2a[-1])
        R.barrier()
        if CUT == 2:
            for c_ in range(16):
                DMA("sp", out_d[c_ * 128:(c_ + 1) * 128, :], VR[:, c_, :].bitcast(F32) if False else XB[0][:, :], r=["o"], w=["o"]) if False else None
        cut(2)

        arena(BIG_BASE + 8 * 2064 * 2 + 16 * 1024 * 2)
        MG = alloc("MG", [128, 8, 2048], BF16)
        SGA = [alloc("SGA%d" % i, [128, 512], BF16) for i in range(2)]
        P2B_END = cur[0]
        assert P2B_END <= WB_BASE, P2B_END
        assert cur[0] <= WB_BASE, cur[0]
        arena(WB_BASE)
        WP = alloc("WP", [128, 8, 512], BF16)
        WGB = alloc("WGB", [128, 8, 1024], BF16)
        WBP = alloc("WBP", [128, 4, 1024], BF16)
        PLW = alloc("PLW", [128, 4, 128], BF16)
        w_bp_v = w_bp.rearrange("(kc p) n -> p kc n", p=128)
        for kc in range(8):
            DMA("pool", WP[:, kc, :], w_in_v[:, kc, 3072:3584], w=["WB"])
            DMA("pool", WGB[:, kc, :], w_in_v[:, kc, 4608:5632], w=["WB"])
        for kc in range(4):
            DMA("pool", WBP[:, kc, :], w_bp_v[:, kc, :], w=["WB"])
            DMA("pool", PLW[:, kc, :], pool_w[kc], w=["WB"])
        it = 0
        for t4 in range(4):
            for dc in range(8):
                bx, by = 2 * (it % 4), 2 * (it % 4) + 1
                sga = SGA[it % 2]; sgk = "SGA%d" % (it % 2)
                for kc in range(8):
                    mm(bank(bx), WBR[:, kc, dc * 128:(dc + 1) * 128], VR[:, 4 * t4:4 * t4 + 4, kc * 128:(kc + 1) * 128],
                       kc == 0, kc == 7, ["W"] + ["VR%d" % (4 * t4 + q) for q in range(4)], ["ps%d" % bx])
                for kc in range(8):
                    mm(bank(by), WGA[:, kc, dc * 128:(dc + 1) * 128], UT[:, kc, 8 + t4 * 512:8 + (t4 + 1) * 512],
                       kc == 0, kc == 7, ["W", "UTall"], ["ps%d" % by])
                act(sga[:, :], bank(by), AF.Sigmoid, ["ps%d" % by], [sgk])
                tt("dve", MG[:, dc, t4 * 512:(t4 + 1) * 512], bank(bx), sga[:, :], ALU.mult, ["ps%d" % bx, sgk], ["MG"])
                it += 1
        R.barrier()
        cut(3)

        arena(W_BASE)
        WO = alloc("WO", [128, 8, 1024], BF16)
        TMPM = [alloc("TMPM%d" % i, [128, 512], F32) for i in range(1)]
        w_out_v = w_out.rearrange("(kc p) n -> p kc n", p=128)
        for kc in range(8):
            DMA("pool", WO[:, kc, :], w_out_v[:, kc, :], w=["W"])
        arena(BIG_BASE + 8 * 2064 * 2)
        PT = alloc("PT", [128, 4, 528], F32)
        TA = alloc("TA", [128, 4, 528], F32)
        TB = alloc("TB", [128, 3, 528], F32)
        DT = alloc("DT", [128, 4, 512], BF16)
        PM = alloc("PM", [128, 4, 512], BF16)
        assert cur[0] <= BIG_BASE + 8 * 2064 * 2 + 16 * 1024 * 2, cur[0]
        def p2b_P(t4):
            for g in range(4):
                for kc in range(8):
                    mm(bank(g), WP[:, kc, g * 128:(g + 1) * 128], UT[:, kc, t4 * 512:t4 * 512 + 512], kc == 0, kc == 7,
                       ["WB", "UTall"], ["ps%d" % g])
            for g in range(4):
                for kc in range(8):
                    mm(PS[:, 2048 + g * 16:2048 + (g + 1) * 16], WP[:, kc, g * 128:(g + 1) * 128],
                       UT[:, kc, t4 * 512 + 512:t4 * 512 + 528], kc == 0, kc == 7, ["WB", "UTall"], ["ps4"])
            act(PT[:, :, 0:512], PS[:, 0:2048].rearrange("p (g x) -> p g x", x=512), AF.Copy, bkeys(0, 4), ["PT"])
            act(PT[:, :, 512:528], PS[:, 2048:2112].rearrange("p (g x) -> p g x", x=16), AF.Copy, ["ps4"], ["PT"])
            tt("dve", TA[:, 0:4, 1:528], PT[:, 0:4, 0:527], PT[:, 0:4, 1:528], ALU.add, ["PT"], ["TA"])
            tt("dve", TB[:, 0:3, 2:527], TA[:, 1:4, 1:526], TA[:, 1:4, 3:528], ALU.add, ["TA"], ["TB"])
            tt("dve", TA[:, 2:4, 4:525], TB[:, 1:3, 2:523], TB[:, 1:3, 6:527], ALU.add, ["TB", "TA"], ["TA2"])
            tt("dve", TB[:, 2:3, 8:520], TA[:, 3:4, 4:516], TA[:, 3:4, 12:524], ALU.add, ["TA2", "TB"], ["TB2"])
            srcs = [TA[:, 0, 8:520], TB[:, 0, 8:520], TA[:, 2, 8:520], TB[:, 2, 8:520]]
            if t4 == 0 or t4 == 3:
                o_, w_ = _CT["EDGEL" if t4 == 0 else "EDGER"]
                for g in range(4):
                    sl = srcs[g][:, 0:8] if t4 == 0 else srcs[g][:, 504:512]
                    tt("dve", sl, sl, CT[:, o_ + g * 8:o_ + g * 8 + 8], ALU.mult, ["TA", "TB", "TA2", "TB2", "CT"], ["TA", "TB", "TA2", "TB2"])
            for g, wdw in enumerate((2, 4, 8, 16)):
                stt("dve", DT[:, g, :], srcs[g], 1.0 / wdw, PT[:, g, 8:520], ALU.mult, ALU.subtract,
                    ["TA", "TB", "TA2", "TB2", "PT"], ["DT"])

        def p2b_pre(t4):
            for g in range(4):
                mm(bank(5 + (g % 2)), PLW[:, g, :], DT[:, g, :], True, True, ["WB", "DT"], ["ps%d" % (5 + g % 2)])
                act(PM[:, g, :], bank(5 + (g % 2)), AF.Identity, ["ps%d" % (5 + g % 2), "PSC"], ["PM"], scale=PSC[:, g:g + 1])

        def p2b_dc(t4):
            for dc in range(8):
                bx, by = (0, 1) if dc % 2 == 0 else (2, 3)
                sga = SGA[dc % 2]; sgk = "SGA%d" % (dc % 2)
                tm = TMPM[0]; tmk = "TMPM0"
                for g in range(4):
                    mm(bank(bx), WBP[:, g, dc * 128:(dc + 1) * 128], PM[:, g, :], g == 0, g == 3, ["WB", "PM"], ["ps%d" % bx])
                for kc in range(8):
                    mm(bank(by), WGB[:, kc, dc * 128:(dc + 1) * 128], UT[:, kc, 8 + t4 * 512:8 + (t4 + 1) * 512],
                       kc == 0, kc == 7, ["WB", "UTall"], ["ps%d" % by])
                act(sga[:, :], bank(by), AF.Sigmoid, ["ps%d" % by], [sgk])
                tt("dve", tm[:, :], bank(bx), sga[:, :], ALU.mult, ["ps%d" % bx, sgk], [tmk])
                mgs = MG[:, dc, t4 * 512:(t4 + 1) * 512]
                tt("dve", mgs, mgs, tm[:, :], ALU.add, ["MG", tmk], ["MG"])

        p2b_P(0)
        for t4 in range(4):
            p2b_pre(t4)
            if t4 + 1 < 4:
                p2b_P(t4 + 1)
            p2b_dc(t4)
        R.barrier()
        cut(4)

        def bcast_rows(BC, gcol0, lng, lnb):
            DMA("sp", BC[:, 1, :], lng.partition_broadcast(128).rearrange("p o n -> p (o n)"), w=["BC"])
            DMA("sp", BC[:, 2, :], lnb.partition_broadcast(128).rearrange("p o n -> p (o n)"), w=["BC"])
            for kc in range(8):
                ts("dve", BC[:, 0, kc * 128:(kc + 1) * 128], ctc("IDENT"), MODT[:, gcol0 + kc, 0:1], None, ALU.mult, None,
                   ["CT", "MODT"], ["BC"])
            for kc in range(8):
                mm(PS[:, (6 + kc // 4) * 512 + (kc % 4) * 128:(6 + kc // 4) * 512 + (kc % 4 + 1) * 128],
                   ctc("ONES"), BC[:, 0, kc * 128:(kc + 1) * 128], True, True, ["CT", "BC"], ["ps%d" % (6 + kc // 4)])
            act(BC[:, 0, :], bank(6, 2), AF.Copy, ["ps6", "ps7"], ["BC"])

        def epilogue(pb, xb, xbk, BC, Z, outb, outk, zk="Z", folded=True):
            if folded:
                stt("dve", Z[:, :], xb[:, :], ALPHA, bank(pb, 2), ALU.mult, ALU.add, [xbk, "ps%d" % pb, "ps%d" % (pb + 1)], [zk])
            else:
                tt("dve", Z[:, :], bank(pb, 2), BC[:, 0, :], ALU.mult, ["ps%d" % pb, "ps%d" % (pb + 1), "BC"], [zk])
                stt("dve", Z[:, :], xb[:, :], ALPHA, Z[:, :], ALU.mult, ALU.add, [xbk, zk], [zk])
            mean, rstd, k = ln_stats(Z[:, :], 128, [zk])
            if folded:
                nmr = LNM[:, (lnctr[0] - 1) % 2, 3:4]
                stt("dve", nmr, mean, -1.0, rstd, ALU.mult, ALU.mult, [k], [k])
                act(Z[:, :], Z[:, :], AF.Identity, [zk, k], [zk], bias=nmr, scale=rstd)
            else:
                ts("dve", Z[:, :], Z[:, :], mean, rstd, ALU.subtract, ALU.mult, [zk, k], [zk])
            tt("dve", Z[:, :], Z[:, :], BC[:, 1, :], ALU.mult, [zk, "BC"], [zk])
            tt("dve", outb[:, :], Z[:, :], BC[:, 2, :], ALU.add, [zk, "BC"], [outk])

        arena(BIG_BASE)
        BC = alloc("BC", [128, 3, 1024], F32)
        Z = [alloc("Z%d" % i, [128, 1024], F32) for i in range(2)]
        XC = [alloc("XC%d" % i, [128, 1024], F32) for i in range(4)]
        XO = [alloc("XO%d" % i, [128, 1024], F32) for i in range(2)]
        assert cur[0] <= WB_BASE, cur[0]
        arena(WB_BASE)
        WD = alloc("WD", [128, 22, 1024], BF16)
        w_down_v = w_down.rearrange("(kc p) n -> p kc n", p=128)
        bcast_rows(BC, 16, ln1g, ln1b)
        for kc in range(8):
            tt("dve", WO[:, kc, :], WO[:, kc, :], BC[:, 0, :], ALU.mult, ["W", "BC"], ["W"])
        x1_target = out_d if STAGE == 2 else x1d
        order = [0, NCH - 1] + list(range(1, NCH - 1))
        for i0 in range(3):
            c0 = order[i0]
            DMA("sp", XC[i0 % 4][:, :], xs[c0 * 128:(c0 + 1) * 128, :], w=["XC%d" % (i0 % 4)])
        for it_, c in enumerate(order):
            xc = XC[it_ % 4]; xck = "XC%d" % (it_ % 4)
            xo = XO[it_ % 2]; xok = "XO%d" % (it_ % 2)
            if it_ + 3 < NCH:
                cn = order[it_ + 3]
                DMA("sp", XC[(it_ + 3) % 4][:, :], xs[cn * 128:(cn + 1) * 128, :], w=["XC%d" % ((it_ + 3) % 4)])
            if STAGE >= 3 and it_ < 11:
                for kc in (2 * it_, 2 * it_ + 1):
                    DMA("pool", WD[:, kc, :], w_down_v[:, kc, :], w=["WD%d" % kc])
            pb = 2 * (it_ % 2)
            for nb in range(2):
                for kc in range(8):
                    mm(bank(pb + nb), MG[:, kc, c * 128:(c + 1) * 128], WO[:, kc, nb * 512:(nb + 1) * 512], kc == 0, kc == 7,
                       ["W", "MG"], ["ps%d" % (pb + nb)])
            epilogue(pb, xc, xck, BC, Z[it_ % 2], xo, xok, zk="Z%d" % (it_ % 2))
            DMA("sp", x1_target[c * 128:(c + 1) * 128, :], xo[:, :], r=[xok], w=["x1d%d" % c])
            if c == 0:
                DMA("sp", ffin.ap()[0:64, :], xo[0:64, :], r=[xok], w=["ffin"])
            if c == NCH - 1:
                DMA("sp", ffin.ap()[64:128, :], xo[64:128, :], r=[xok], w=["ffin"])
                if STAGE >= 3:
                    R.add("pool", lambda e: e.collective_compute("AllGather", ALU.bypass, replica_groups=GROUPS,
                                                                 ins=[ffin.ap().opt()], outs=[ffout.ap().opt()]),
                          ["ffin"], ["ffout"], kind="cc")
        R.barrier(keep=["ffout"] + ["WD%d" % kc for kc in range(22)])

        if STAGE >= 3:
            arena(W_BASE)
            U2 = alloc("U2", [128, 8, 1152], BF16)
            GT = alloc("GT", [128, 22, 1024], BF16)
            WUP = [alloc("WUP%d" % i, [128, 8, 256], BF16) for i in range(2)]
            HA = [alloc("HA%d" % i, [128, 3, 18, 64], BF16) for i in range(1)]
            HBP = alloc("HBP", [128, 18, 66], BF16)
            ACC = alloc("ACC", [128, 1024], F32)
            ACCB = ACC[:, 0:512].bitcast(BF16)
            GA = [alloc("GA%d" % i, [128, 1024], F32) for i in range(1)]
            DG = [alloc("DG%d" % i, [128, 1, 9, 128], BF16) for i in range(1)]
            BC2 = alloc("BC2", [128, 3, 1024], F32)
            Z2 = alloc("Z2", [128, 1024], F32)
            X2 = [alloc("X2%d" % i, [128, 1024], F32) for i in range(2)]
            XN2 = [alloc("XN2%d" % i, [128, 1024], BF16) for i in range(1)]
            XO2 = [alloc("XO2%d" % i, [128, 1024], F32) for i in range(2)]
            HX = alloc("HX", [128, 1024], F32)
            X2U = [alloc("X2U%d" % i, [128, 1024], F32) for i in range(1)]
            assert cur[0] <= WB_BASE, cur[0]
            bcast_rows(BC2, 40, ln2g, ln2b)
            for i in range(1):
                R.add("pool", lambda e, t=HA[i]: e.memset(t[:, :, :, :], 0.0), [], ["HA%d" % i])
                R.add("pool", lambda e: e.memset(HBP[:, :, :], 0.0), [], ["HBP"])
            ci = 0
            pi_glob = 0
            u2ctr = [0]

            def u2_buf(hf, j):
                if hf == 0:
                    bufs = [(X2U[0], "X2U0"), (X2[0], "X20"), (X2[1], "X21")]
                else:
                    bufs = [(X2U[0], "X2U0"), (HX, "HX")]
                return bufs[j % len(bufs)]

            def u2_load(hf, j):
                x2, x2k = u2_buf(hf, j)
                row0 = 16 * hf + 2 * j
                if row0 == 0:
                    cp("pool", x2[0:64, :], HX[0:64, :], ["HX"], [x2k])
                    DMA("sp", x2[64:128, :], x1d[0:64, :], r=["x1d0"], w=[x2k])
                elif row0 == 32:
                    DMA("sp", x2[0:64, :], x1d[31 * 64:32 * 64, :], r=["x1d15"], w=[x2k])
                    DMA("sp", x2[64:128, :], hxd.ap()[64:128, :], r=["hxd"], w=[x2k])
                else:
                    t0 = (row0 - 1) * 64
                    DMA("sp", x2[:, :], x1d[t0:t0 + 128, :], r=["x1d%d" % (t0 // 128), "x1d%d" % ((t0 + 127) // 128)], w=[x2k])

            def u2_chunk(hf, j, part=0, load=True):
                x2, x2k = u2_buf(hf, j)
                row0 = 16 * hf + 2 * j
                if load and part != 2:
                    u2_load(hf, j)
                par_ = j % 2
                xnb, xnk_ = (XN2[0], "XN20") if par_ == 0 else (ACCB, "ACC")
                ln_to_T(x2[:, :], 128, [x2k], xnb, xnk_,
                        lambda kc, j=j: U2[:, kc, j * 128:(j + 1) * 128], "U2",
                        lambda kc: MODT[:, 24 + kc, 0:1], lambda kc: OPSC2[:, kc:kc + 1], 6 + par_, part=part)
                if part == 1:
                    return
                if row0 == 0:
                    ts("dve", U2[:, :, 0:64], U2[:, :, 0:64], ctc("HM", 0, 1), None, ALU.mult, None, ["U2", "CT"], ["U2"])
                if row0 == 32:
                    ts("dve", U2[:, :, 1088:1152], U2[:, :, 1088:1152], ctc("HM", 1, 2), None, ALU.mult, None, ["U2", "CT"], ["U2"])

            for hf in range(2):
                if hf == 0:
                    for j in [1, 2, 3, 4, 5, 6, 7, 8]:
                        u2_chunk(0, j)
                    R.add("dve", lambda e: e.memset(HX[:, :], 0.0), [], ["HX"])
                    ghs = [(XO2[0], "XO20"), (XO2[1], "XO21"), (Z2, "Z"), (GA[0], "GA0")]
                    for r_ in range(4):
                        gh, ghk = ghs[r_]
                        DMA("sp", gh[0:64, :], ffout.ap()[r_ * 128 + 64:r_ * 128 + 128, :], r=["ffout"], w=[ghk])
                        DMA("sp", gh[64:128, :], ffout.ap()[r_ * 128:r_ * 128 + 64, :], r=["ffout"], w=[ghk])
                    for r_ in range(4):
                        gh, ghk = ghs[r_]
                        stt("dve", HX[:, :], gh[:, :], ctc("OH", r_, r_ + 1), HX[:, :], ALU.mult, ALU.add, [ghk, "CT", "HX"], ["HX"])
                    DMA("sp", hxd.ap()[64:128, :], HX[64:128, :], r=["HX"], w=["hxd"])
                    u2_chunk(0, 0)
                for pi in range(22):
                    wu = WUP[pi_glob % 2]; wuk = "WUP%d" % (pi_glob % 2)
                    dg = DG[0]; dgk = "DG0"
                    ha = HA[0]; hak = "HA0"
                    ga = GA[0]; gak = "GA0"
                    DMA("pool", wu[:, :, 0:128], w_up_v[:, :, pi * 128:(pi + 1) * 128], w=[wuk])
                    DMA("pool", wu[:, :, 128:256], w_up_v[:, :, 2816 + pi * 128:2816 + (pi + 1) * 128], w=[wuk])
                    for t in range(9):
                        act(dg[:, 0, t, :], IDB[:, :], AF.Copy, ["IDB", "CW"], [dgk], scale=CW[:, pi, t:t + 1])
                    for ab in range(2):
                        for nb, (n0, n1) in enumerate(((0, 512), (512, 1024), (1024, 1152))):
                            bb = 3 * ab + nb
                            for kc in range(8):
                                mm(PS[:, bb * 512:bb * 512 + (n1 - n0)], wu[:, kc, ab * 128:(ab + 1) * 128], U2[:, kc, n0:n1],
                                   kc == 0, kc == 7, [wuk, "U2"], ["ps%d" % bb])
                    pa = PS[:, 0:1152].rearrange("p (r c) -> p r c", c=64)
                    pb_ = PS[:, 1536:2688].rearrange("p (r c) -> p r c", c=64)
                    act(ha[:, 1, :, :], pa, AF.Copy, bkeys(0, 3), [hak])
                    act(ha[:, 0, :, 1:64], pa[:, :, 0:63], AF.Copy, bkeys(0, 3), [hak])
                    act(ha[:, 2, :, 0:63], pa[:, :, 1:64], AF.Copy, bkeys(0, 3), [hak])
                    act(HBP[:, :, 1:65], pb_, AF.Copy, bkeys(3, 3), ["HBP"])
                    for blk in range(2):
                        bb = 6 + blk
                        for t in range(9):
                            dr, dc_ = t // 3 - 1, t % 3 - 1
                            mm(bank(bb), dg[:, 0, t, :], ha[:, dc_ + 1, 1 + 8 * blk + dr:9 + 8 * blk + dr, :],
                               t == 0, t == 8, [dgk, hak], ["ps%d" % bb])
                    chb = 22 + pi
                    accv = ACC[:, :].rearrange("p (r c) -> p r c", c=64)
                    for t in range(9):
                        dr, dc_ = t // 3 - 1, t % 3 - 1
                        win = HBP[:, 1 + dr:17 + dr, 1 + dc_:65 + dc_]
                        if t == 0:
                            act(accv, win, AF.Identity, ["HBP", "CW", "CB"], ["ACC"], bias=CB[:, chb:chb + 1], scale=CW[:, chb, 0:1])
                        else:
                            stt("dve", accv, win, CW[:, chb, t:t + 1], accv, ALU.mult, ALU.add, ["HBP", "CW", "ACC"], ["ACC"])
                    act(ga[:, :], bank(6, 2), AF.Gelu_apprx_tanh, ["ps6", "ps7"], [gak], bias=CB[:, pi:pi + 1])
                    tt("dve", GT[:, pi, :], ACC[:, :], ga[:, :], ALU.mult, ["ACC", gak], ["GT"])
                    pi_glob += 1
                if hf == 0:
                    u2_load(1, 0)
                DMA("sp", X2[ci % 2][:, :], x1d[hf * 8 * 128:(hf * 8 + 1) * 128, :], r=["x1d%d" % (hf * 8)], w=["X2%d" % (ci % 2)])
                for c in range(8):
                    gc = hf * 8 + c
                    xc = X2[ci % 2]; xck = "X2%d" % (ci % 2)
                    xo = XO2[ci % 2]; xok = "XO2%d" % (ci % 2)
                    ci += 1
                    if c + 1 < 8:
                        DMA("sp", X2[ci % 2][:, :], x1d[(gc + 1) * 128:(gc + 2) * 128, :], r=["x1d%d" % (gc + 1)], w=["X2%d" % (ci % 2)])
                    pb = 2 + 2 * (c % 2)
                    if hf == 0:
                        u2_load(1, c + 1)
                        u2_chunk(1, c, part=1, load=False)
                    for nb in range(2):
                        for kc in range(22):
                            mm(bank(pb + nb), GT[:, kc, c * 128:(c + 1) * 128], WD[:, kc, nb * 512:(nb + 1) * 512], kc == 0, kc == 21,
                               ["WD%d" % kc, "GT"], ["ps%d" % (pb + nb)])
                    if hf == 0:
                        u2_chunk(1, c, part=2)
                    epilogue(pb, xc, xck, BC2, Z2, xo, xok, folded=False)
                    DMA("sp", out_d[gc * 128:(gc + 1) * 128, :], xo[:, :], r=[xok], w=["out%d" % gc])
                if hf == 0:
                    u2_chunk(1, 8, load=False)

    except _Stop:
        pass

    if STAGE == 1:
        DMA("sp", out_d[0:128, 0:1024], SST[:, :, :, :].rearrange("p a b c -> p (a b c)"), r=["SST"], w=["o"])
        DMA("sp", out_d[128:256, 0:1024], SCTX[:, :, :, :].rearrange("p a b c -> p (a b c)"), r=["SCTXf", "SCTXb"], w=["o"])
        DMA("sp", out_d[256:384, 0:96], MODT[:, :, :].rearrange("p a b -> p (a b)"), r=["MODT"], w=["o"])
        DMA("sp", out_d[384:512, 0:16], LG[:, :], r=["LG"], w=["o"])

    with (nc.semaphore("s_pe") as s_pe, nc.semaphore("s_act") as s_act, nc.semaphore("s_dve") as s_dve,
          nc.semaphore("s_pool") as s_pool, nc.semaphore("s_sp") as s_sp, nc.semaphore("s_cc") as s_cc):
        import contextlib
        with contextlib.ExitStack() as es:
            dsems = [es.enter_context(nc.semaphore("s_d%d" % i)) for i in range(NDS)]
            with nc.Block() as block:
                R.emit(nc, block, dict(pe=s_pe, act=s_act, dve=s_dve, pool=s_pool, sp=s_sp), dsems, s_cc)
    return nc


_NC_CACHE = {}


def kernel(x, c, ctx, c_ctx, w_ada, b_ada, w_in, ret_decay_logit, pool_w, pool_scale,
           w_branch_ret, w_branch_pool, w_out, ln1_g, ln1_b, w_up, conv_w, conv_b, w_down, ln2_g, ln2_b):
    f = lambda a: np.ascontiguousarray(np.asarray(a, dtype=np.float32))
    x = f(x); ctx = f(ctx); c = f(c); c_ctx = f(c_ctx)
    if "nc" not in _NC_CACHE:
        _NC_CACHE["nc"] = build()
    nc = _NC_CACHE["nc"]
    shared = dict(
        w_ada=f(w_ada[0]), bada=f(b_ada[0].reshape(48, 128).T), w_in=f(w_in[0]), lgt=f(ret_decay_logit[0].reshape(1, 16)),
        pool_w=f(pool_w[0]), psc=f(pool_scale[0].reshape(4, 128).T), w_br=f(w_branch_ret[0]), w_bp=f(w_branch_pool[0]),
        w_out=f(w_out[0]), ln1g=f(ln1_g[0].reshape(1, D)), ln1b=f(ln1_b[0].reshape(1, D)),
        ln2g=f(ln2_g[0].reshape(1, D)), ln2b=f(ln2_b[0].reshape(1, D)), w_up=f(w_up[0]),
        cwf=f(conv_w[0].reshape(9, 44, 128).transpose(2, 1, 0).reshape(128, 44 * 9)),
        cbf=f(conv_b[0].reshape(44, 128).T), w_down=f(w_down[0]),
    )
    in_maps = []
    for core in range(NCORE):
        b, s = core // 4, core % 4
        xh = np.zeros((16, D), np.float32)
        if s > 0:
            xh[0:8] = x[b, s * SEG - 8:s * SEG]
        if s < 3:
            xh[8:16] = x[b, (s + 1) * SEG:(s + 1) * SEG + 8]
        cfm = np.concatenate([c[b].reshape(8, 128).T, c_ctx.reshape(8, 128).T], axis=1)
        m = dict(shared)
        m.update(xs=f(x[b, s * SEG:(s + 1) * SEG]), xh=xh, ctxs=f(ctx[b]), cfm=f(cfm), ct=_const_table(core))
        in_maps.append(m)
    res = run_bass_kernel_spmd(nc, in_maps, core_ids=list(range(NCORE)))
    out = np.zeros((2, 8192, D), np.float32)
    for core in range(NCORE):
        b, s = core // 4, core % 4
        out[b, s * SEG:(s + 1) * SEG] = np.asarray(res.results[core]["out"])
    return out
```

```python
import numpy as np
import concourse.bass as bass
import concourse.mybir as mybir
from concourse.bass_utils import run_bass_kernel_spmd

F32 = mybir.dt.float32
BF16 = mybir.dt.bfloat16
AF = mybir.ActivationFunctionType
ALU = mybir.AluOpType

STAGE = 3
NCORE = 8
import os
CUT = int(os.environ.get('KCUT', '99'))


class _Stop(Exception):
    pass

D = 1024
SEG = 2048
NCH = 16
EPS = 1e-6
ALPHA = 2.0 ** 0.25
NDS = 24
GROUPS = [[0, 1, 2, 3], [4, 5, 6, 7]]

_CT = {}
_off = 0
for _n, _w in [("IDENT", 128), ("D1", 128), ("D2", 128), ("EYE8", 128), ("IOTA1", 128), ("IOTA2", 128),
               ("ONES", 128), ("JREV", 1), ("JFWD", 1), ("CEXP", 17), ("NEXPF", 5), ("MASKF", 5),
               ("NEXPB", 5), ("MASKB", 5), ("PMASK", 2), ("PE", 1), ("PO", 1), ("EDGEL", 32), ("EDGER", 32), ("OH", 4), ("HM", 2)]:
    _CT[_n] = (_off, _w)
    _off += _w
NCT = _off


def _const_table(core):
    s = core % 4
    t = np.zeros((128, NCT), np.float32)

    def put(name, arr):
        o, w = _CT[name]
        t[:, o:o + w] = arr

    p = np.arange(128)[:, None].astype(np.float32)
    i = np.arange(128)[None, :].astype(np.float32)
    put("IDENT", np.eye(128, dtype=np.float32))
    put("D1", np.maximum(i - p, 0))
    put("D2", np.maximum(p - i, 0))
    put("EYE8", 0.125 * np.eye(128, dtype=np.float32))
    put("IOTA1", np.broadcast_to(i + 1, (128, 128)))
    put("IOTA2", np.broadcast_to(128 - i, (128, 128)))
    put("ONES", np.ones((128, 128), np.float32))
    put("JREV", 127 - p)
    put("JFWD", p)
    put("CEXP", np.broadcast_to(128.0 * np.arange(17)[None, :], (128, 17)))
    nf = np.zeros(5); mf = np.zeros(5); nb = np.zeros(5); mb = np.zeros(5)
    for r in range(4):
        if r < s:
            nf[r] = s - 1 - r; mf[r] = 1
        if r > s:
            nb[r] = r - s - 1; mb[r] = 1
    nf[4] = s; mf[4] = 1; nb[4] = 3 - s; mb[4] = 1
    put("NEXPF", np.broadcast_to(2048.0 * nf[None, :], (128, 5)))
    put("MASKF", np.broadcast_to(mf[None, :], (128, 5)))
    put("NEXPB", np.broadcast_to(2048.0 * nb[None, :], (128, 5)))
    put("MASKB", np.broadcast_to(mb[None, :], (128, 5)))
    put("PE", (p < 64).astype(np.float32))
    put("PO", (p >= 64).astype(np.float32))
    put("PMASK", np.broadcast_to(np.array([1.0 if s > 0 else 0.0, 1.0 if s < 3 else 0.0])[None, :], (128, 2)))
    L = 8192
    el = np.ones((4, 8)); er = np.ones((4, 8))
    for g, w in enumerate((2, 4, 8, 16)):
        for j in range(8):
            tpos = s * SEG + j
            cnt = min(tpos + w // 2, L) - max(tpos - w // 2, 0)
            el[g, j] = w / cnt
            tpos = s * SEG + SEG - 8 + j
            cnt = min(tpos + w // 2, L) - max(tpos - w // 2, 0)
            er[g, j] = w / cnt
    put("EDGEL", np.broadcast_to(el.reshape(1, 32), (128, 32)))
    put("EDGER", np.broadcast_to(er.reshape(1, 32), (128, 32)))
    oh = np.zeros((128, 4))
    if s > 0:
        oh[0:64, s - 1] = 1
    if s < 3:
        oh[64:128, s + 1] = 1
    put("OH", oh)
    put("HM", np.broadcast_to(np.array([1.0 if s > 0 else 0.0, 1.0 if s < 3 else 0.0])[None, :], (128, 2)))
    return t


class Rec:
    def __init__(self):
        self.ops = []

    def add(self, eng, fn, r=(), w=(), kind="c"):
        self.ops.append(dict(eng=eng, fn=fn, r=tuple(r), w=tuple(w), kind=kind))
        return len(self.ops) - 1

    def barrier(self, keep=()):
        self.ops.append(dict(eng="*", fn=None, r=(), w=(), kind="bar", keep=tuple(keep)))

    def emit(self, nc, block, sems, dsems, ccsem):
        ops = self.ops
        n = len(ops)
        deps = [set() for _ in range(n)]
        lastw = {}
        readers = {}
        last_eng = {}
        dma_since = []
        frontier = set()
        dma_slot_last = {}
        ndma = 0
        npool = 0
        slot_of = {}
        last_cc = []
        for i, o in enumerate(ops):
            if o["kind"] == "bar":
                keep = o.get("keep", ())
                frontier = set(last_eng.values())
                for q in dma_slot_last.values():
                    if keep and ops[q]["w"] and all(k in keep for k in ops[q]["w"]):
                        continue
                    frontier.add(q)
                if not keep:
                    frontier |= set(last_cc)
                kw_ = {k: lastw[k] for k in keep if k in lastw}
                kr_ = {k: readers[k] for k in keep if k in readers}
                lastw.clear(); readers.clear()
                lastw.update(kw_); readers.update(kr_)
                continue
            dp = deps[i]
            dp |= frontier
            for k in o["r"]:
                if k in lastw:
                    dp.update(lastw[k])
            for k in o["w"]:
                if k in lastw:
                    same_burst = (o["kind"] == "dma" and all(ops[q]["kind"] == "dma" for q in lastw[k])
                                  and not readers.get(k))
                    if not same_burst:
                        dp.update(lastw[k])
                for rr in readers.get(k, ()):
                    dp.add(rr)
            for k in o["w"]:
                if o["kind"] == "dma" and k in lastw and all(ops[q]["kind"] == "dma" for q in lastw[k]) and not readers.get(k):
                    lastw[k] = lastw[k] + [i]
                else:
                    lastw[k] = [i]
                readers[k] = []
            for k in o["r"]:
                lst = readers.setdefault(k, [])
                if o["kind"] == "c":
                    lst[:] = [q for q in lst if not (ops[q]["kind"] == "c" and ops[q]["eng"] == o["eng"])]
                lst.append(i)
            if o["kind"] == "dma":
                half = NDS // 2
                if o["eng"] == "pool":
                    sl = half + (npool % half)
                    npool += 1
                else:
                    sl = ndma % half
                    ndma += 1
                slot_of[i] = sl
                if sl in dma_slot_last:
                    dp.add(dma_slot_last[sl])
                dma_slot_last[sl] = i
            elif o["kind"] == "c":
                last_eng[o["eng"]] = i
            elif o["kind"] == "cc":
                last_cc = [i]
            dp.discard(i)
        hasdep = [False] * n
        for i in range(n):
            for d in deps[i]:
                hasdep[d] = True
        tok = [None] * n
        cnt = {e: 0 for e in sems}
        dcnt = [0] * NDS
        cccnt = 0
        pos_in_eng = [0] * n
        epos = {}
        for i, o in enumerate(ops):
            if o["kind"] == "bar":
                continue
            if o["kind"] == "dma":
                sl = slot_of[i]
                dcnt[sl] += 16
                tok[i] = (dsems[sl], dcnt[sl], 16)
            elif o["kind"] == "cc":
                cccnt += 1
                tok[i] = (ccsem, cccnt, 1)
            else:
                e = o["eng"]
                epos[e] = epos.get(e, 0) + 1
                pos_in_eng[i] = epos[e]
                if hasdep[i]:
                    cnt[e] += 1
                    tok[i] = (sems[e], cnt[e], 1)
        final_d = list(dcnt)

        def run_engine(ename, eh):
            waited = {}
            for i, o in enumerate(ops):
                if o["kind"] == "bar" or o["eng"] != ename:
                    continue
                for d in sorted(deps[i]):
                    od = ops[d]
                    t = tok[d]
                    if t is None:
                        continue
                    if od["kind"] == "c" and od["eng"] == ename:
                        if ename == "pe":
                            continue
                        if o["kind"] == "c" and pos_in_eng[i] - pos_in_eng[d] > 3:
                            continue
                    sem, val, _ = t
                    if waited.get(sem.num, 0) < val:
                        eh.wait_ge(sem, val)
                        waited[sem.num] = val
                ins = o["fn"](eh)
                t = tok[i]
                if t is not None:
                    ins.then_inc(t[0], t[2])
            if ename == "sp":
                for sl in range(NDS):
                    if final_d[sl] > 0:
                        eh.wait_ge(dsems[sl], final_d[sl])

        block.sync(lambda e: run_engine("sp", e))
        block.tensor(lambda e: run_engine("pe", e))
        block.scalar(lambda e: run_engine("act", e))
        block.vector(lambda e: run_engine("dve", e))
        block.gpsimd(lambda e: run_engine("pool", e))


def build():
    nc = bass.Bass("TRN2", target_bir_lowering=False)
    R = Rec()

    def din(name, shape, dt=F32):
        return nc.dram_tensor(name, list(shape), dt, kind="ExternalInput").ap()

    xs = din("xs", [SEG, D]); xh = din("xh", [16, D]); ctxs = din("ctxs", [256, D])
    cfm = din("cfm", [128, 16]); ct_d = din("ct", [128, NCT]); lgt = din("lgt", [1, 16])
    w_ada = din("w_ada", [D, 6 * D]); bada = din("bada", [128, 48])
    w_in = din("w_in", [D, 5632]); pool_w = din("pool_w", [4, 128, 128]); psc = din("psc", [128, 4])
    w_br = din("w_br", [D, D]); w_bp = din("w_bp", [512, D]); w_out = din("w_out", [D, D])
    ln1g = din("ln1g", [1, D]); ln1b = din("ln1b", [1, D]); ln2g = din("ln2g", [1, D]); ln2b = din("ln2b", [1, D])
    w_up = din("w_up", [D, 5632]); cwf = din("cwf", [128, 44 * 9]); cbf = din("cbf", [128, 44])
    w_down = din("w_down", [2816, D])
    out_d = nc.dram_tensor("out", [SEG, D], F32, kind="ExternalOutput").ap()
    x1d = nc.dram_tensor("x1d", [SEG, D], F32).ap()
    ccin = nc.dram_tensor("ccin", [128, 1024], F32)
    ccout = nc.dram_tensor("ccout", [4 * 128, 1024], F32)
    ffin = nc.dram_tensor("ffin", [128, 1024], F32)
    ffout = nc.dram_tensor("ffout", [4 * 128, 1024], F32)
    hxd = nc.dram_tensor("hxd", [128, 1024], F32)

    w_in_v = w_in.rearrange("(kc p) n -> p kc n", p=128)
    w_up_v = w_up.rearrange("(kc p) n -> p kc n", p=128)
    w_ada_v = w_ada.rearrange("(kc p) n -> p kc n", p=128)

    cur = [16640]

    def alloc(name, shape, dt):
        nb = int(np.prod(shape[1:])) * (4 if dt == F32 else 2)
        nb = (nb + 31) // 32 * 32
        assert cur[0] + nb <= 229120, (name, cur[0], nb)
        t = nc.alloc_sbuf_tensor_at(name, list(shape), dt, offset=cur[0])
        cur[0] += nb
        return t

    CT = alloc("CT", [128, NCT], F32)

    def ctc(name, a=0, b=None):
        o, w = _CT[name]
        return CT[:, o + a:o + (w if b is None else b)]

    IDB = alloc("IDB", [128, 128], BF16)
    LG = alloc("LG", [128, 16], F32)
    LGP = alloc("LGP", [128, 8], F32)
    SCR = alloc("SCR", [128, 6, 16], F32)
    MT = alloc("MT", [128, 8, 128], F32)
    QFM = alloc("QFM", [128, 4, 4, 128], F32)
    KF = alloc("KF", [128, 8], F32)
    KB = alloc("KB", [128, 8], F32)
    CDX = alloc("CDX", [128, 2, 4, 128], F32)
    CD = alloc("CD", [128, 8], F32)
    CDPOW = alloc("CDPOW", [128, 8, 17], F32)
    COEF = alloc("COEF", [128, 8, 5], F32)
    CFM = alloc("CFM", [128, 16], F32)
    SC = alloc("SC", [128, 16], BF16)
    BADA = alloc("BADA", [128, 48], F32)
    MODT = alloc("MODT", [128, 48, 2], F32)
    OPSC1 = alloc("OPSC1", [128, 8, 2], F32)
    OPSC2 = alloc("OPSC2", [128, 8], F32)
    PSC = alloc("PSC", [128, 4], F32)
    CW = alloc("CW", [128, 44, 9], F32)
    CB = alloc("CB", [128, 44], F32)
    SST = alloc("SST", [128, 2, 4, 128], F32)
    SRUN = alloc("SRUN", [128, 4, 128], F32)
    LNS = alloc("LNS", [128, 2, 16], F32)
    LNM = alloc("LNM", [128, 2, 4], F32)
    W_BASE = cur[0]
    W_SIZE = 34816
    cur[0] += W_SIZE
    BIG_BASE = cur[0]
    WB_BASE = 229120 - 45056

    def arena(base):
        cur[0] = base

    PS = nc.alloc_psum_tensor("PS", [128, 4096], F32)

    def bank(b, n=1):
        return PS[:, b * 512:(b + n) * 512]

    def bkeys(b, n=1):
        return ["ps%d" % k for k in range(b, b + n)]

    def A(eng, fn, r=(), w=()):
        return R.add(eng, fn, r, w)

    def DMA(eng, out, in_, r=(), w=()):
        return R.add(eng, lambda e, o=out, i=in_: e.dma_start(out=o, in_=i), r, w, kind="dma")

    def mm(out, lhsT, rhs, start, stop, r, w):
        return R.add("pe", lambda e, o=out, l=lhsT, rh=rhs, s=start, p=stop: e.matmul(o, l, rh, start=s, stop=p), r, w)

    def act(out, in_, func, r, w, bias=0.0, scale=1.0):
        return R.add("act", lambda e, o=out, i=in_, f=func, b=bias, s=scale: e.activation(out=o, in_=i, func=f, bias=b, scale=s), r, w)

    def ts(eng, out, in0, s1, s2, op0, op1, r, w):
        if s2 is None:
            return R.add(eng, lambda e, o=out, i=in0, a=s1, p0=op0: e.tensor_scalar(out=o, in0=i, scalar1=a, scalar2=None, op0=p0), r, w)
        return R.add(eng, lambda e, o=out, i=in0, a=s1, b=s2, p0=op0, p1=op1: e.tensor_scalar(out=o, in0=i, scalar1=a, scalar2=b, op0=p0, op1=p1), r, w)

    def tt(eng, out, in0, in1, op, r, w):
        return R.add(eng, lambda e, o=out, a=in0, b=in1, p=op: e.tensor_tensor(out=o, in0=a, in1=b, op=p), r, w)

    def stt(eng, out, in0, scalar, in1, op0, op1, r, w):
        return R.add(eng, lambda e, o=out, a=in0, s=scalar, b=in1, p0=op0, p1=op1: e.scalar_tensor_tensor(out=o, in0=a, scalar=s, in1=b, op0=p0, op1=p1), r, w)

    def cp(eng, out, in_, r, w):
        return R.add(eng, lambda e, o=out, i=in_: e.tensor_copy(out=o, in_=i), r, w)


    def cut(n):
        if CUT == n:
            raise _Stop()

    try:
        DMA("sp", CT[:, :], ct_d, w=["CT"])
        DMA("sp", LG[:, :], lgt.partition_broadcast(128).rearrange("p o n -> p (o n)"), w=["LG"])
        DMA("sp", CFM[:, :], cfm, w=["CFM"])
        DMA("sp", BADA[:, :], bada, w=["BADA"])
        DMA("sp", PSC[:, :], psc, w=["PSC"])
        DMA("sp", CW[:, :, :].rearrange("p a b -> p (a b)"), cwf, w=["CW"])
        DMA("sp", CB[:, :], cbf, w=["CB"])
        cp("dve", IDB[:, :], ctc("IDENT"), ["CT"], ["IDB"])
        T0 = SCR[:, 0, :]; T1 = SCR[:, 1, :]; T2 = SCR[:, 2, :]; T3 = SCR[:, 3, :]
        act(T0, LG[:, :], AF.Exp, ["LG"], ["T0"], scale=-1.0)
        ts("dve", T1, T0, 2.0, None, ALU.add, None, ["T0"], ["T1"])
        R.add("dve", lambda e: e.reciprocal(out=T1, in_=T1), ["T1"], ["T1"])
        tt("dve", T1, T0, T1, ALU.mult, ["T0", "T1"], ["T1"])
        tt("dve", T2, T1, T1, ALU.mult, ["T1"], ["T2"])
        ts("dve", T3, T2, 1.0 / 15, 1.0 / 13, ALU.mult, ALU.add, ["T2"], ["T3"])
        for cst in (1.0 / 11, 1.0 / 9, 1.0 / 7, 1.0 / 5, 1.0 / 3, 1.0):
            tt("dve", T3, T3, T2, ALU.mult, ["T3", "T2"], ["T3"])
            ts("dve", T3, T3, cst, None, ALU.add, None, ["T3"], ["T3"])
        tt("dve", T3, T3, T1, ALU.mult, ["T3", "T1"], ["T3"])
        ts("dve", LG[:, :], T3, -2.0, None, ALU.mult, None, ["T3"], ["LG"])
        if CUT == -3:
            DMA("sp", out_d[128:256, 0:16], LG[:, :], r=["LG"], w=["o"])
        cut(-3)
        LGv = LG[:, :].rearrange("p (d m r) -> p d m r", d=2, m=4, r=2)
        LGPv = LGP[:, :].rearrange("p (d m) -> p d m", d=2)
        cp("dve", LGPv[0:64], LGv[0:64, :, :, 0], ["LG"], ["LGP"])
        cp("dve", LGPv[64:128], LGv[64:128, :, :, 1], ["LG"], ["LGP"])
        arena(BIG_BASE)
        UT = alloc("UT", [128, 8, 2064], BF16)
        VR = alloc("VR", [128, 16, 1024], BF16)
        AFB = alloc("AFB", [128, 16, 4, 128], BF16)
        KVB = alloc("KVB", [128, 17, 4, 128], BF16)
        XB = [alloc("XB%d" % i, [128, 1024], F32) for i in range(2)]
        XN = [alloc("XN%d" % i, [128, 1024], BF16) for i in range(2)]
        KW = [alloc("KW%d" % i, [128, 2, 512], BF16) for i in range(2)]
        UCS = [alloc("UC%d" % i, [128, 8, 128], BF16) for i in range(2)]
        UH = alloc("UH", [128, 8, 16], BF16)
        VC = alloc("VC", [128, 1024], BF16)
        PACC = alloc("PACC", [128, 2, 4, 128], F32)
        SCTX = alloc("SCTX", [128, 2, 4, 128], F32)
        KVS = alloc("KVS", [128, 2, 4, 128], F32)
        PGB = [alloc("PGB%d" % i, [128, 2, 4, 128], F32) for i in range(2)]
        P1_END = cur[0]
        arena(W_BASE)
        WKV = alloc("WKV", [128, 8, 1536], BF16)
        WAD = [AFB[:, :, :, :].rearrange("p a b c -> p (a b c)").rearrange("p (k n) -> p k n", k=8),
               KVB[:, 0:16, :, :].rearrange("p a b c -> p (a b c)").rearrange("p (k n) -> p k n", k=8)]
        MSC = XB[0][:, 0:128]
        MSC2 = XB[0][:, 128:256]
        for h in range(8):
            ts("dve", MSC, ctc("D1"), LG[:, h:h + 1], None, ALU.mult, None, ["CT", "LG"], ["MSC", "XB0"])
            stt("dve", MSC2, ctc("D2"), LG[:, 8 + h:9 + h], MSC, ALU.mult, ALU.add, ["CT", "LG", "MSC"], ["MSC2", "XB0"])
            act(MSC2, MSC2, AF.Exp, ["MSC2"], ["MSC2", "XB0"])
            stt("dve", MT[:, h, :], MSC2, 0.125, ctc("EYE8"), ALU.mult, ALU.add, ["MSC2", "CT", "XB0"], ["MT"])
        for m in range(4):
            act(QFM[:, 0, m, :], ctc("IOTA1"), AF.Exp, ["CT", "LGP"], ["QF"], scale=LGP[:, m:m + 1])
            act(QFM[:, 2, m, :], ctc("IOTA2"), AF.Exp, ["CT", "LGP"], ["QF"], scale=LGP[:, 4 + m:5 + m])
        cp("dve", QFM[:, 1, :, :], QFM[:, 0, :, :], ["QF"], ["QF"])
        cp("dve", QFM[:, 3, :, :], QFM[:, 2, :, :], ["QF"], ["QF"])
        for q_ in range(4):
            lo = 64 if q_ % 2 == 0 else 0
            R.add("dve", lambda e, t=QFM[lo:lo + 64, q_, :, :]: e.memset(t, 0.0), ["QF"], ["QF"])
        act(KF[:, :], LG[:, 0:8], AF.Exp, ["LG", "CT"], ["KF"], scale=ctc("JREV"))
        act(KB[:, :], LG[:, 8:16], AF.Exp, ["LG", "CT"], ["KB"], scale=ctc("JFWD"))
        ts("dve", KF[:, :], KF[:, :], 0.125, None, ALU.mult, None, ["KF"], ["KF"])
        ts("dve", KB[:, :], KB[:, :], 0.125, None, ALU.mult, None, ["KB"], ["KB"])
        act(CD[:, :], LGP[:, :], AF.Exp, ["LGP"], ["CD"], scale=128.0)
        for j in range(8):
            act(CDPOW[:, j, :], ctc("CEXP"), AF.Exp, ["CT", "LGP"], ["CDPOW"], scale=LGP[:, j:j + 1])
            act(COEF[:, j, :], ctc("NEXPF" if j < 4 else "NEXPB"), AF.Exp, ["CT", "LGP"], ["COEF"], scale=LGP[:, j:j + 1])
            tt("dve", COEF[:, j, :], COEF[:, j, :], ctc("MASKF" if j < 4 else "MASKB"), ALU.mult, ["COEF", "CT"], ["COEF"])
            d_, m_ = j // 4, j % 4
            ts("dve", CDX[:, d_, m_, :], ctc("ONES"), CD[:, j:j + 1], None, ALU.mult, None, ["CT", "CD"], ["CDX"])
        if CUT == -2:
            DMA("sp", out_d[128:256, 0:16], LG[:, :], r=["LG"], w=["o"])
            DMA("sp", out_d[256:384, 0:1024], MT[:, :, :].rearrange("p a b -> p (a b)"), r=["MT"], w=["o"])
        cut(-2)
        act(SC[:, :], CFM[:, :], AF.Silu, ["CFM"], ["SC"])
        SCv = SC[:, :].rearrange("p (v k) -> p k v", v=2)
        def mod_dma(v):
            DMA("pool", WAD[v % 2], w_ada_v[:, :, v * 1024:(v + 1) * 1024], w=["WAD%d" % (v % 2)])

        def mod_mm(v):
            wb = WAD[v % 2]
            for j in range(8):
                col = (v * 8 + j) * 2
                for kc in range(8):
                    mm(PS[:, col:col + 2], wb[:, kc, j * 128:(j + 1) * 128], SCv[:, kc, :], kc == 0, kc == 7,
                       ["WAD%d" % (v % 2), "SC"], ["ps0"])

        mod_dma(0)
        mod_dma(1)
        mod_mm(0)
        mod_dma(2)
        mod_mm(1)
        mod_dma(3)
        tt("dve", MODT[:, 0:16, :], PS[:, 0:32].rearrange("p (a b) -> p a b", b=2),
           BADA[:, 0:16].unsqueeze(2).to_broadcast([128, 16, 2]), ALU.add, ["ps0", "BADA"], ["MODT"])
        ts("dve", OPSC1[:, :, :], MODT[:, 8:16, :], 1.0, None, ALU.add, None, ["MODT"], ["OPSC"])

        lnctr = [0]

        def ln_stats(xt, ntok, rk):
            b = lnctr[0] % 2
            lnctr[0] += 1
            st = LNS[0:ntok, b, :]
            mv = LNM[0:ntok, b, :]
            k = "LN%d" % b
            R.add("dve", lambda e, o=st[:, 0:6], i=xt[:, 0:512]: e.bn_stats(out=o, in_=i), rk, [k])
            R.add("dve", lambda e, o=st[:, 6:12], i=xt[:, 512:1024]: e.bn_stats(out=o, in_=i), rk, [k])
            R.add("dve", lambda e, o=mv[:, 0:2], i=st[:, 0:12]: e.bn_aggr(out=o, in_=i), [k], [k])
            act(mv[:, 2:3], mv[:, 1:2], AF.Sqrt, [k], [k], bias=EPS)
            R.add("dve", lambda e, o=mv[:, 2:3]: e.reciprocal(out=o, in_=o), [k], [k])
            return mv[:, 0:1], mv[:, 2:3], k

        def ln_to_T(xt, ntok, rk, XN, xnk, dst_fn, dstk, sh_fn, osc_fn, pb, part=0):
            if part in (0, 1):
                mean, rstd, k = ln_stats(xt, ntok, rk)
                ts("dve", XN[0:ntok, :], xt, mean, rstd, ALU.subtract, ALU.mult, rk + [k], [xnk])
            if part == 1:
                return
            pbf = bank(pb).bitcast(BF16)
            for kc in range(8):
                R.add("pe", lambda e, o=pbf[:, kc * 128:kc * 128 + ntok], i=XN[0:ntok, kc * 128:(kc + 1) * 128],
                      idn=IDB[0:ntok, 0:ntok]: e.transpose(o, i, idn), [xnk, "IDB"], ["ps%d" % pb])
            for kc in range(8):
                act(dst_fn(kc), pbf[:, kc * 128:kc * 128 + ntok], AF.Identity, ["ps%d" % pb, "MODT", "OPSC"], [dstk],
                    bias=sh_fn(kc), scale=osc_fn(kc))

        for kc in range(8):
            DMA("pool", WKV[:, kc, :], w_in_v[:, kc, 0:1536], w=["W"])
        R.add("dve", lambda e: e.memset(PACC[:, :, :, :], 0.0), [], ["PACCf", "PACCb"])
        R.add("dve", lambda e: e.memset(SCTX[:, :, :, :], 0.0), [], ["SCTXf", "SCTXb"])

        def kv_chunk(u_fn, uk, vdst, vk, accF, accFk, accB, accBk, cidx, bi, afdst=None, kvbdst=None):
            kw = KW[bi % 2]
            kwk = "KW%d" % (bi % 2)
            for kc in range(8):
                mm(bank(1), u_fn(kc), WKV[:, kc, 0:512], kc == 0, kc == 7, [uk, "W"], ["ps1"])
            for nb in range(2):
                for kc in range(8):
                    mm(bank(2 + nb), u_fn(kc), WKV[:, kc, 512 + nb * 512:1024 + nb * 512], kc == 0, kc == 7, [uk, "W"], ["ps%d" % (2 + nb)])
            k3 = bank(1).rearrange("p (h d) -> p h d", d=64)
            tt("dve", kw[:, 0, :].rearrange("p (h d) -> p h d", d=64), k3, KF[:, :].unsqueeze(2).to_broadcast([128, 8, 64]),
               ALU.mult, ["ps1", "KF"], [kwk])
            tt("dve", kw[:, 1, :].rearrange("p (h d) -> p h d", d=64), k3, KB[:, :].unsqueeze(2).to_broadcast([128, 8, 64]),
               ALU.mult, ["ps1", "KB"], [kwk])
            act(vdst, bank(2, 2), AF.Copy, ["ps2", "ps3"], [vk])
            for d_ in range(2):
                for m in range(4):
                    b0 = 4 + 2 * d_ + m // 2
                    mm(PS[:, b0 * 512 + (m % 2) * 256: b0 * 512 + (m % 2) * 256 + 256], kw[:, d_, m * 128:(m + 1) * 128],
                       vdst[:, m * 256:(m + 1) * 256], True, True, [kwk, vk], ["ps%d" % b0])
            for d_ in range(2):
                src = bank(4 + 2 * d_, 2).rearrange("p (m x) -> p m x", x=256)
                act(KVS[0:64, d_, :, :], src[0:64, :, 0:128], AF.Copy, ["ps%d" % (4 + 2 * d_), "ps%d" % (5 + 2 * d_)], ["KVS%d" % d_])
                act(KVS[64:128, d_, :, :], src[64:128, :, 128:256], AF.Copy, ["ps%d" % (4 + 2 * d_), "ps%d" % (5 + 2 * d_)], ["KVS%d" % d_])
            if afdst is not None:
                act(afdst, accF, AF.Copy, [accFk], ["AFB", "WAD0"])
            tt("dve", accF, accF, CDX[:, 0, :, :], ALU.mult, [accFk, "CDX"], [accFk])
            tt("dve", accF, accF, KVS[:, 0, :, :], ALU.add, [accFk, "KVS0"], [accFk])
            for m in range(4):
                stt("dve", accB[:, m, :], KVS[:, 1, m, :], CDPOW[:, 4 + m, cidx:cidx + 1], accB[:, m, :], ALU.mult, ALU.add,
                    ["KVS1", "CDPOW", accBk], [accBk])
            if kvbdst is not None:
                act(kvbdst, KVS[:, 1, :, :], AF.Copy, ["KVS1"], ["KVB", "WAD1"])

        for cc_ in range(2):
            xb = XB[cc_ % 2]
            DMA("sp", xb[:, :], ctxs[cc_ * 128:(cc_ + 1) * 128, :], w=["XB%d" % (cc_ % 2)])
            ln_to_T(xb[:, :], 128, ["XB%d" % (cc_ % 2)], XN[cc_ % 2], "XN%d" % (cc_ % 2),
                    lambda kc, cc_=cc_: UCS[cc_][:, kc, :], "UC%d" % cc_, lambda kc: MODT[:, kc, 1:2], lambda kc: OPSC1[:, kc, 1:2], 7)
        DMA("sp", XB[0][0:16, :], xh, w=["XB0"])
        ln_to_T(XB[0][0:16, :], 16, ["XB0"], XN[0], "XN0", lambda kc: UH[:, kc, 0:16], "UH",
                lambda kc: MODT[:, kc, 0:1], lambda kc: OPSC1[:, kc, 0:1], 7)
        ts("dve", UT[:, :, 0:8], UH[:, :, 0:8], ctc("PMASK", 0, 1), None, ALU.mult, None, ["UH", "CT"], ["UTh"])
        ts("dve", UT[:, :, 2056:2064], UH[:, :, 8:16], ctc("PMASK", 1, 2), None, ALU.mult, None, ["UH", "CT"], ["UTh"])
        def p1_a(c):
            xb = XB[c % 2]
            DMA("sp", xb[:, :], xs[c * 128:(c + 1) * 128, :], w=["XB%d" % (c % 2)])
            ln_to_T(xb[:, :], 128, ["XB%d" % (c % 2)], XN[c % 2], "XN%d" % (c % 2),
                    lambda kc, c=c: UT[:, kc, 8 + c * 128:8 + (c + 1) * 128], "UT%d" % c,
                    lambda kc: MODT[:, kc, 0:1], lambda kc: OPSC1[:, kc, 0:1], 7)

        def p1_b(c):
            kv_chunk(lambda kc, c=c: UT[:, kc, 8 + c * 128:8 + (c + 1) * 128], "UT%d" % c, VR[:, c, :], "VR%d" % c,
                     PACC[:, 0, :, :], "PACCf", PACC[:, 1, :, :], "PACCb", c, c,
                     afdst=AFB[:, c, :, :], kvbdst=KVB[:, c, :, :])

        for c in range(NCH):
            p1_a(c)
            if c in (1, 5, 9, 13):
                v_ = 2 + (c - 1) // 4
                mod_mm(v_)
                if v_ + 2 < 6:
                    mod_dma(v_ + 2)
        tt("dve", MODT[:, 16:48, :], PS[:, 32:96].rearrange("p (a b) -> p a b", b=2),
           BADA[:, 16:48].unsqueeze(2).to_broadcast([128, 32, 2]), ALU.add, ["ps0", "BADA"], ["MODT2"])
        ts("dve", OPSC2[:, :], MODT[:, 32:40, 0], 1.0, None, ALU.add, None, ["MODT2"], ["OPSC2"])
        for cc_ in range(2):
            kv_chunk(lambda kc, cc_=cc_: UCS[cc_][:, kc, :], "UC%d" % cc_, VC[:, :], "VC", SCTX[:, 0, :, :], "SCTXf",
                     SCTX[:, 1, :, :], "SCTXb", cc_, cc_)
        for c in range(NCH):
            p1_b(c)
        DMA("sp", ccin.ap(), PACC[:, :, :, :].rearrange("p a b c -> p (a b c)"), r=["PACCf", "PACCb"], w=["ccin"])
        R.add("pool", lambda e: e.collective_compute("AllGather", ALU.bypass, replica_groups=GROUPS,
                                                     ins=[ccin.ap().opt()], outs=[ccout.ap().opt()]),
              ["ccin"], ["ccout"], kind="cc")
        arena(W_BASE)
        WQ = alloc("WQ", [128, 8, 512], BF16)
        WK = alloc("WK", [128, 8, 512], BF16)
        WG = alloc("WG", [128, 8, 1024], BF16)
        for kc in range(8):
            DMA("pool", WK[:, kc, :], w_in_v[:, kc, 0:512], r=[], w=["W"])
            DMA("pool", WQ[:, kc, :], w_in_v[:, kc, 1536:2048], w=["W"])
            DMA("pool", WG[:, kc, :], w_in_v[:, kc, 2048:3072], w=["W"])
        for d_ in range(2):
            for m in range(4):
                ts("dve", SST[:, d_, m, :], SCTX[:, d_, m, :], COEF[:, d_ * 4 + m, 4:5], None, ALU.mult, None,
                   ["SCTXf", "SCTXb", "COEF"], ["SST"])
        for r_ in range(4):
            pg = PGB[r_ % 2]
            DMA("sp", pg[:, :, :, :].rearrange("p a b c -> p (a b c)"), ccout.ap()[r_ * 128:(r_ + 1) * 128, :],
                r=["ccout"], w=["PGB%d" % (r_ % 2)])
            for d_ in range(2):
                for m in range(4):
                    stt("dve", SST[:, d_, m, :], pg[:, d_, m, :], COEF[:, d_ * 4 + m, r_:r_ + 1], SST[:, d_, m, :], ALU.mult, ALU.add,
                        ["PGB%d" % (r_ % 2), "COEF", "SST"], ["SST"])
        cp("dve", SRUN[:, :, :], SST[:, 1, :, :], ["SST"], ["SRUN"])
        act(KVB[:, 16, :, :], SRUN[:, :, :], AF.Copy, ["SRUN"], ["SB16"])
        R.barrier()
        if CUT == 1:
            DMA("sp", out_d[0:128, 0:1024], SST[:, :, :, :].rearrange("p a b c -> p (a b c)"), r=["SST"], w=["o"])
            DMA("sp", out_d[128:256, 0:1024], SCTX[:, :, :, :].rearrange("p a b c -> p (a b c)"), r=["SCTXf", "SCTXb"], w=["o"])
            DMA("sp", out_d[256:384, 0:1024], PACC[:, :, :, :].rearrange("p a b c -> p (a b c)"), r=["PACCf", "PACCb"], w=["o"])
        cut(1)

        arena(BIG_BASE + 8 * 2064 * 2 + 16 * 1024 * 2 + 16 * 512 * 2 + 17 * 512 * 2)
        QK = [alloc("QK%d" % i, [128, 7, 512], BF16) for i in range(2)]
        STt = [alloc("ST%d" % i, [128, 8, 128], BF16) for i in range(2)]
        SG = [alloc("SG%d" % i, [128, 1024], F32) for i in range(2)]
        YN = alloc("YN", [128, 1024], F32)
        RGT = [alloc("RGT%d" % i, [128, 1024], BF16) for i in range(2)]
        YST = alloc("YST", [128, 8, 6], F32)
        YMV = alloc("YMV", [128, 8, 4], F32)
        def p2a_a(c):
            for m in range(4):
                stt("dve", AFB[:, c, m, :], SST[:, 0, m, :], CDPOW[:, m, c:c + 1], AFB[:, c, m, :],
                    ALU.mult, ALU.add, ["SST", "CDPOW", "AFB"], ["SF%d" % c])
            if c < NCH - 1:
                tt("dve", SRUN[:, :, :], SRUN[:, :, :], CDX[:, 1, :, :], ALU.mult, ["SRUN", "CDX"], ["SRUN"])
                tt("dve", SRUN[:, :, :], SRUN[:, :, :], KVB[:, c + 1, :, :], ALU.add, ["SRUN", "KVB", "SB%d" % (c + 1)], ["SRUN"])
                act(KVB[:, c + 1, :, :], SRUN[:, :, :], AF.Copy, ["SRUN"], ["SB%d" % (c + 1), "KVB"])
            qk = QK[c % 2]; qkk = "QK%d" % (c % 2)
            st = STt[c % 2]; stk = "ST%d" % (c % 2)
            ucols = slice(8 + c * 128, 8 + (c + 1) * 128)
            sg = SG[c % 2]; sgk_ = "SG%d" % (c % 2)
            for m in range(4):
                for kc in range(8):
                    mm(PS[:, m * 128:(m + 1) * 128], WQ[:, kc, m * 128:(m + 1) * 128], UT[:, kc, ucols], kc == 0, kc == 7,
                       ["W", "UT%d" % c], ["ps0"])
            for m in range(4):
                for kc in range(8):
                    mm(PS[:, 512 + m * 128:512 + (m + 1) * 128], WK[:, kc, m * 128:(m + 1) * 128], UT[:, kc, ucols], kc == 0, kc == 7,
                       ["W", "UT%d" % c], ["ps1"])
            for nb in range(2):
                for kc in range(8):
                    mm(bank(2 + nb), UT[:, kc, ucols], WG[:, kc, nb * 512:(nb + 1) * 512], kc == 0, kc == 7,
                       ["W", "UT%d" % c], ["ps%d" % (2 + nb)])
            for q_ in range(4):
                tt("dve", qk[:, q_, :], bank(0), QFM[:, q_, :, :].rearrange("p a b -> p (a b)"), ALU.mult, ["ps0", "QF"], [qkk])
            act(qk[:, 4, :], bank(0), AF.Copy, ["ps0"], [qkk])
            act(qk[:, 5, :], bank(1), AF.Copy, ["ps1", "CT"], [qkk], scale=ctc("PE"))
            act(qk[:, 6, :], bank(1), AF.Copy, ["ps1", "CT"], [qkk], scale=ctc("PO"))
            act(sg[:, :], bank(2, 2), AF.Silu, ["ps2", "ps3"], [sgk_])

        def p2a_b(c):
            qk = QK[c % 2]; qkk = "QK%d" % (c % 2)
            st = STt[c % 2]; stk = "ST%d" % (c % 2)
            ucols = slice(8 + c * 128, 8 + (c + 1) * 128)
            sg = SG[c % 2]; sgk_ = "SG%d" % (c % 2)
            for h in range(8):
                m, par = h // 2, h % 2
                pr = slice(par * 64, par * 64 + 64)
                b0 = 4 + h // 4
                mm(PS[:, b0 * 512 + (h % 4) * 128: b0 * 512 + (h % 4 + 1) * 128], qk[:, 5 + par, m * 128:(m + 1) * 128],
                   qk[:, 4, m * 128:(m + 1) * 128], True, True, [qkk], ["ps%d" % b0])
            for hb in range(2):
                tt("dve", st[:, hb * 4:(hb + 1) * 4, :], bank(4 + hb).rearrange("p (a b) -> p a b", b=128), MT[:, hb * 4:(hb + 1) * 4, :],
                   ALU.mult, ["ps%d" % (4 + hb), "MT"], [stk])
            for h in range(8):
                m, par = h // 2, h % 2
                pr = slice(par * 64, par * 64 + 64)
                b0 = 6 + h // 4
                o = PS[:, b0 * 512 + (h % 4) * 128: b0 * 512 + (h % 4 + 1) * 128]
                mm(o, st[:, h, :], VR[:, c, h * 128:(h + 1) * 128], True, False, [stk, "VR%d" % c], ["ps%d" % b0])
                mm(o, qk[:, 0 + par, m * 128:(m + 1) * 128], AFB[:, c, m, :], False, False, [qkk, "SF%d" % c], ["ps%d" % b0])
                mm(o, qk[:, 2 + par, m * 128:(m + 1) * 128], KVB[:, c + 1, m, :], False, True, [qkk, "SB%d" % (c + 1)], ["ps%d" % b0])
            for h in range(8):
                b0 = 6 + h // 4
                yh = PS[:, b0 * 512 + (h % 4) * 128: b0 * 512 + (h % 4 + 1) * 128]
                R.add("dve", lambda e, o=YST[:, h, :], i=yh: e.bn_stats(out=o, in_=i), ["ps%d" % b0], ["YST"])
            for h in range(8):
                R.add("dve", lambda e, o=YMV[:, h, 0:2], i=YST[:, h, :]: e.bn_aggr(out=o, in_=i), ["YST"], ["YMV"])
            act(YMV[:, :, 2], YMV[:, :, 1], AF.Sqrt, ["YMV"], ["YMV"], bias=EPS)
            R.add("dve", lambda e: e.reciprocal(out=YMV[:, :, 2], in_=YMV[:, :, 2]), ["YMV"], ["YMV"])
            stt("dve", YMV[:, :, 3], YMV[:, :, 0], -1.0, YMV[:, :, 2], ALU.mult, ALU.mult, ["YMV"], ["YMV"])
            for h in range(8):
                b0 = 6 + h // 4
                yh = PS[:, b0 * 512 + (h % 4) * 128: b0 * 512 + (h % 4 + 1) * 128]
                act(YN[:, h * 128:(h + 1) * 128], yh, AF.Identity, ["ps%d" % b0, "YMV"], ["YN"], bias=YMV[:, h, 3:4], scale=YMV[:, h, 2:3])
            rgt = RGT[c % 2]; rgk = "RGT%d" % (c % 2)
            tt("dve", rgt[:, :], YN[:, :], sg[:, :], ALU.mult, ["YN", sgk_], [rgk])

        def p2a_b2(c):
            rgt = RGT[c % 2]; rgk = "RGT%d" % (c % 2)
            pbf = bank(4).bitcast(BF16)
            for kc in range(8):
                R.add("pe", lambda e, o=pbf[:, kc * 128:(kc + 1) * 128], i=rgt[:, kc * 128:(kc + 1) * 128]: e.transpose(o, i, IDB[:, :]),
                      [rgk, "IDB"], ["ps4"])
            act(VR[:, c, :], pbf, AF.Copy, ["ps4"], ["VR%d" % c])

        w_br_v = w_br.rearrange("(kc p) n -> p kc n", p=128)

        def prefetch_2bi():
            arena(W_BASE)
            wbr = alloc("WBR", [128, 8, 1024], BF16)
            wga = alloc("WGA", [128, 8, 1024], BF16)
            for kc in range(8):
                DMA("pool", wbr[:, kc, :], w_br_v[:, kc, :], w=["W"])
                DMA("pool", wga[:, kc, :], w_in_v[:, kc, 3584:4608], w=["W"])
            return wbr, wga

        order2a = list(range(NCH - 1, -1, -1))
        p2a_a(order2a[0])
        for i_, c in enumerate(order2a):
            if i_ + 1 < NCH:
                p2a_a(order2a[i_ + 1])
            if i_ == NCH - 2:
                WBR, WGA = prefetch_2bi()
            p2a_b(c)
            if i_ >= 1:
                p2a_b2(order2a[i_ - 1])
        p2a_b2(order2a[-1])
        R.barrier()
        if CUT == 2:
            for c_ in range(16):
                DMA("sp", out_d[c_ * 128:(c_ + 1) * 128, :], VR[:, c_, :].bitcast(F32) if False else XB[0][:, :], r=["o"], w=["o"]) if False else None
        cut(2)

        arena(BIG_BASE + 8 * 2064 * 2 + 16 * 1024 * 2)
        MG = alloc("MG", [128, 8, 2048], BF16)
        SGA = [alloc("SGA%d" % i, [128, 512], BF16) for i in range(2)]
        P2B_END = cur[0]
        assert P2B_END <= WB_BASE, P2B_END
        assert cur[0] <= WB_BASE, cur[0]
        arena(WB_BASE)
        WP = alloc("WP", [128, 8, 512], BF16)
        WGB = alloc("WGB", [128, 8, 1024], BF16)
        WBP = alloc("WBP", [128, 4, 1024], BF16)
        PLW = alloc("PLW", [128, 4, 128], BF16)
        w_bp_v = w_bp.rearrange("(kc p) n -> p kc n", p=128)
        for kc in range(8):
            DMA("pool", WP[:, kc, :], w_in_v[:, kc, 3072:3584], w=["WB"])
            DMA("pool", WGB[:, kc, :], w_in_v[:, kc, 4608:5632], w=["WB"])
        for kc in range(4):
            DMA("pool", WBP[:, kc, :], w_bp_v[:, kc, :], w=["WB"])
            DMA("pool", PLW[:, kc, :], pool_w[kc], w=["WB"])
        it = 0
        for t4 in range(4):
            for dc in range(8):
                bx, by = 2 * (it % 4), 2 * (it % 4) + 1
                sga = SGA[it % 2]; sgk = "SGA%d" % (it % 2)
                for kc in range(8):
                    mm(bank(bx), WBR[:, kc, dc * 128:(dc + 1) * 128], VR[:, 4 * t4:4 * t4 + 4, kc * 128:(kc + 1) * 128],
                       kc == 0, kc == 7, ["W"] + ["VR%d" % (4 * t4 + q) for q in range(4)], ["ps%d" % bx])
                for kc in range(8):
                    mm(bank(by), WGA[:, kc, dc * 128:(dc + 1) * 128], UT[:, kc, 8 + t4 * 512:8 + (t4 + 1) * 512],
                       kc == 0, kc == 7, ["W", "UTall"], ["ps%d" % by])
                act(sga[:, :], bank(by), AF.Sigmoid, ["ps%d" % by], [sgk])
                tt("dve", MG[:, dc, t4 * 512:(t4 + 1) * 512], bank(bx), sga[:, :], ALU.mult, ["ps%d" % bx, sgk], ["MG"])
                it += 1
        R.barrier()
        cut(3)

        arena(W_BASE)
        WO = alloc("WO", [128, 8, 1024], BF16)
        TMPM = [alloc("TMPM%d" % i, [128, 512], F32) for i in range(1)]
        w_out_v = w_out.rearrange("(kc p) n -> p kc n", p=128)
        for kc in range(8):
            DMA("pool", WO[:, kc, :], w_out_v[:, kc, :], w=["W"])
        arena(BIG_BASE + 8 * 2064 * 2)
        PT = alloc("PT", [128, 4, 528], F32)
        TA = alloc("TA", [128, 4, 528], F32)
        TB = alloc("TB", [128, 3, 528], F32)
        DT = alloc("DT", [128, 4, 512], BF16)
        PM = alloc("PM", [128, 4, 512], BF16)
        assert cur[0] <= BIG_BASE + 8 * 2064 * 2 + 16 * 1024 * 2, cur[0]
        def p2b_P(t4):
            for g in range(4):
                for kc in range(8):
                    mm(bank(g), WP[:, kc, g * 128:(g + 1) * 128], UT[:, kc, t4 * 512:t4 * 512 + 512], kc == 0, kc == 7,
                       ["WB", "UTall"], ["ps%d" % g])
            for g in range(4):
                for kc in range(8):
                    mm(PS[:, 2048 + g * 16:2048 + (g + 1) * 16], WP[:, kc, g * 128:(g + 1) * 128],
                       UT[:, kc, t4 * 512 + 512:t4 * 512 + 528], kc == 0, kc == 7, ["WB", "UTall"], ["ps4"])
            act(PT[:, :, 0:512], PS[:, 0:2048].rearrange("p (g x) -> p g x", x=512), AF.Copy, bkeys(0, 4), ["PT"])
            act(PT[:, :, 512:528], PS[:, 2048:2112].rearrange("p (g x) -> p g x", x=16), AF.Copy, ["ps4"], ["PT"])
            tt("dve", TA[:, 0:4, 1:528], PT[:, 0:4, 0:527], PT[:, 0:4, 1:528], ALU.add, ["PT"], ["TA"])
            tt("dve", TB[:, 0:3, 2:527], TA[:, 1:4, 1:526], TA[:, 1:4, 3:528], ALU.add, ["TA"], ["TB"])
            tt("dve", TA[:, 2:4, 4:525], TB[:, 1:3, 2:523], TB[:, 1:3, 6:527], ALU.add, ["TB", "TA"], ["TA2"])
            tt("dve", TB[:, 2:3, 8:520], TA[:, 3:4, 4:516], TA[:, 3:4, 12:524], ALU.add, ["TA2", "TB"], ["TB2"])
            srcs = [TA[:, 0, 8:520], TB[:, 0, 8:520], TA[:, 2, 8:520], TB[:, 2, 8:520]]
            if t4 == 0 or t4 == 3:
                o_, w_ = _CT["EDGEL" if t4 == 0 else "EDGER"]
                for g in range(4):
                    sl = srcs[g][:, 0:8] if t4 == 0 else srcs[g][:, 504:512]
                    tt("dve", sl, sl, CT[:, o_ + g * 8:o_ + g * 8 + 8], ALU.mult, ["TA", "TB", "TA2", "TB2", "CT"], ["TA", "TB", "TA2", "TB2"])
            for g, wdw in enumerate((2, 4, 8, 16)):
                stt("dve", DT[:, g, :], srcs[g], 1.0 / wdw, PT[:, g, 8:520], ALU.mult, ALU.subtract,
                    ["TA", "TB", "TA2", "TB2", "PT"], ["DT"])

        def p2b_pre(t4):
            for g in range(4):
                mm(bank(5 + (g % 2)), PLW[:, g, :], DT[:, g, :], True, True, ["WB", "DT"], ["ps%d" % (5 + g % 2)])
                act(PM[:, g, :], bank(5 + (g % 2)), AF.Identity, ["ps%d" % (5 + g % 2), "PSC"], ["PM"], scale=PSC[:, g:g + 1])

        def p2b_dc(t4):
            for dc in range(8):
                bx, by = (0, 1) if dc % 2 == 0 else (2, 3)
                sga = SGA[dc % 2]; sgk = "SGA%d" % (dc % 2)
                tm = TMPM[0]; tmk = "TMPM0"
                for g in range(4):
                    mm(bank(bx), WBP[:, g, dc * 128:(dc + 1) * 128], PM[:, g, :], g == 0, g == 3, ["WB", "PM"], ["ps%d" % bx])
                for kc in range(8):
                    mm(bank(by), WGB[:, kc, dc * 128:(dc + 1) * 128], UT[:, kc, 8 + t4 * 512:8 + (t4 + 1) * 512],
                       kc == 0, kc == 7, ["WB", "UTall"], ["ps%d" % by])
                act(sga[:, :], bank(by), AF.Sigmoid, ["ps%d" % by], [sgk])
                tt("dve", tm[:, :], bank(bx), sga[:, :], ALU.mult, ["ps%d" % bx, sgk], [tmk])
                mgs = MG[:, dc, t4 * 512:(t4 + 1) * 512]
                tt("dve", mgs, mgs, tm[:, :], ALU.add, ["MG", tmk], ["MG"])

        p2b_P(0)
        for t4 in range(4):
            p2b_pre(t4)
            if t4 + 1 < 4:
                p2b_P(t4 + 1)
            p2b_dc(t4)
        R.barrier()
        cut(4)

        def bcast_rows(BC, gcol0, lng, lnb):
            DMA("sp", BC[:, 1, :], lng.partition_broadcast(128).rearrange("p o n -> p (o n)"), w=["BC"])
            DMA("sp", BC[:, 2, :], lnb.partition_broadcast(128).rearrange("p o n -> p (o n)"), w=["BC"])
            for kc in range(8):
                ts("dve", BC[:, 0, kc * 128:(kc + 1) * 128], ctc("IDENT"), MODT[:, gcol0 + kc, 0:1], None, ALU.mult, None,
                   ["CT", "MODT"], ["BC"])
            for kc in range(8):
                mm(PS[:, (6 + kc // 4) * 512 + (kc % 4) * 128:(6 + kc // 4) * 512 + (kc % 4 + 1) * 128],
                   ctc("ONES"), BC[:, 0, kc * 128:(kc + 1) * 128], True, True, ["CT", "BC"], ["ps%d" % (6 + kc // 4)])
            act(BC[:, 0, :], bank(6, 2), AF.Copy, ["ps6", "ps7"], ["BC"])

        def epilogue(pb, xb, xbk, BC, Z, outb, outk, zk="Z", folded=True):
            if folded:
                stt("dve", Z[:, :], xb[:, :], ALPHA, bank(pb, 2), ALU.mult, ALU.add, [xbk, "ps%d" % pb, "ps%d" % (pb + 1)], [zk])
            else:
                tt("dve", Z[:, :], bank(pb, 2), BC[:, 0, :], ALU.mult, ["ps%d" % pb, "ps%d" % (pb + 1), "BC"], [zk])
                stt("dve", Z[:, :], xb[:, :], ALPHA, Z[:, :], ALU.mult, ALU.add, [xbk, zk], [zk])
            mean, rstd, k = ln_stats(Z[:, :], 128, [zk])
            ts("dve", Z[:, :], Z[:, :], mean, rstd, ALU.subtract, ALU.mult, [zk, k], [zk])
            tt("dve", Z[:, :], Z[:, :], BC[:, 1, :], ALU.mult, [zk, "BC"], [zk])
            tt("dve", outb[:, :], Z[:, :], BC[:, 2, :], ALU.add, [zk, "BC"], [outk])

        arena(BIG_BASE)
        BC = alloc("BC", [128, 3, 1024], F32)
        Z = [alloc("Z%d" % i, [128, 1024], F32) for i in range(2)]
        XC = [alloc("XC%d" % i, [128, 1024], F32) for i in range(4)]
        XO = [alloc("XO%d" % i, [128, 1024], F32) for i in range(2)]
        assert cur[0] <= WB_BASE, cur[0]
        arena(WB_BASE)
        WD = alloc("WD", [128, 22, 1024], BF16)
        w_down_v = w_down.rearrange("(kc p) n -> p kc n", p=128)
        bcast_rows(BC, 16, ln1g, ln1b)
        for kc in range(8):
            tt("dve", WO[:, kc, :], WO[:, kc, :], BC[:, 0, :], ALU.mult, ["W", "BC"], ["W"])
        x1_target = out_d if STAGE == 2 else x1d
        order = [0, NCH - 1] + list(range(1, NCH - 1))
        for i0 in range(3):
            c0 = order[i0]
            DMA("sp", XC[i0 % 4][:, :], xs[c0 * 128:(c0 + 1) * 128, :], w=["XC%d" % (i0 % 4)])
        for it_, c in enumerate(order):
            xc = XC[it_ % 4]; xck = "XC%d" % (it_ % 4)
            xo = XO[it_ % 2]; xok = "XO%d" % (it_ % 2)
            if it_ + 3 < NCH:
                cn = order[it_ + 3]
                DMA("sp", XC[(it_ + 3) % 4][:, :], xs[cn * 128:(cn + 1) * 128, :], w=["XC%d" % ((it_ + 3) % 4)])
            if STAGE >= 3 and it_ < 11:
                for kc in (2 * it_, 2 * it_ + 1):
                    DMA("pool", WD[:, kc, :], w_down_v[:, kc, :], w=["WD%d" % kc])
            pb = 2 * (it_ % 2)
            for nb in range(2):
                for kc in range(8):
                    mm(bank(pb + nb), MG[:, kc, c * 128:(c + 1) * 128], WO[:, kc, nb * 512:(nb + 1) * 512], kc == 0, kc == 7,
                       ["W", "MG"], ["ps%d" % (pb + nb)])
            epilogue(pb, xc, xck, BC, Z[it_ % 2], xo, xok, zk="Z%d" % (it_ % 2))
            DMA("sp", x1_target[c * 128:(c + 1) * 128, :], xo[:, :], r=[xok], w=["x1d%d" % c])
            if c == 0:
                DMA("sp", ffin.ap()[0:64, :], xo[0:64, :], r=[xok], w=["ffin"])
            if c == NCH - 1:
                DMA("sp", ffin.ap()[64:128, :], xo[64:128, :], r=[xok], w=["ffin"])
                if STAGE >= 3:
                    R.add("pool", lambda e: e.collective_compute("AllGather", ALU.bypass, replica_groups=GROUPS,
                                                                 ins=[ffin.ap().opt()], outs=[ffout.ap().opt()]),
                          ["ffin"], ["ffout"], kind="cc")
        R.barrier(keep=["ffout"] + ["WD%d" % kc for kc in range(22)])

        if STAGE >= 3:
            arena(W_BASE)
            U2 = alloc("U2", [128, 8, 1152], BF16)
            GT = alloc("GT", [128, 22, 1024], BF16)
            WUP = [alloc("WUP%d" % i, [128, 8, 256], BF16) for i in range(2)]
            HA = [alloc("HA%d" % i, [128, 3, 18, 64], BF16) for i in range(1)]
            HBP = alloc("HBP", [128, 18, 66], BF16)
            ACC = alloc("ACC", [128, 1024], F32)
            ACCB = ACC[:, 0:512].bitcast(BF16)
            GA = [alloc("GA%d" % i, [128, 1024], F32) for i in range(1)]
            DG = [alloc("DG%d" % i, [128, 1, 9, 128], BF16) for i in range(1)]
            BC2 = alloc("BC2", [128, 3, 1024], F32)
            Z2 = alloc("Z2", [128, 1024], F32)
            X2 = [alloc("X2%d" % i, [128, 1024], F32) for i in range(2)]
            XN2 = [alloc("XN2%d" % i, [128, 1024], BF16) for i in range(1)]
            XO2 = [alloc("XO2%d" % i, [128, 1024], F32) for i in range(2)]
            HX = alloc("HX", [128, 1024], F32)
            X2U = [alloc("X2U%d" % i, [128, 1024], F32) for i in range(1)]
            assert cur[0] <= WB_BASE, cur[0]
            bcast_rows(BC2, 40, ln2g, ln2b)
            for i in range(1):
                R.add("pool", lambda e, t=HA[i]: e.memset(t[:, :, :, :], 0.0), [], ["HA%d" % i])
                R.add("pool", lambda e: e.memset(HBP[:, :, :], 0.0), [], ["HBP"])
            ci = 0
            pi_glob = 0
            u2ctr = [0]

            def u2_buf(hf, j):
                if hf == 0:
                    bufs = [(X2U[0], "X2U0"), (X2[0], "X20"), (X2[1], "X21")]
                else:
                    bufs = [(X2U[0], "X2U0"), (HX, "HX")]
                return bufs[j % len(bufs)]

            def u2_load(hf, j):
                x2, x2k = u2_buf(hf, j)
                row0 = 16 * hf + 2 * j
                if row0 == 0:
                    cp("pool", x2[0:64, :], HX[0:64, :], ["HX"], [x2k])
                    DMA("sp", x2[64:128, :], x1d[0:64, :], r=["x1d0"], w=[x2k])
                elif row0 == 32:
                    DMA("sp", x2[0:64, :], x1d[31 * 64:32 * 64, :], r=["x1d15"], w=[x2k])
                    DMA("sp", x2[64:128, :], hxd.ap()[64:128, :], r=["hxd"], w=[x2k])
                else:
                    t0 = (row0 - 1) * 64
                    DMA("sp", x2[:, :], x1d[t0:t0 + 128, :], r=["x1d%d" % (t0 // 128), "x1d%d" % ((t0 + 127) // 128)], w=[x2k])

            def u2_chunk(hf, j, part=0, load=True):
                x2, x2k = u2_buf(hf, j)
                row0 = 16 * hf + 2 * j
                if load and part != 2:
                    u2_load(hf, j)
                par_ = j % 2
                xnb, xnk_ = (XN2[0], "XN20") if par_ == 0 else (ACCB, "ACC")
                ln_to_T(x2[:, :], 128, [x2k], xnb, xnk_,
                        lambda kc, j=j: U2[:, kc, j * 128:(j + 1) * 128], "U2",
                        lambda kc: MODT[:, 24 + kc, 0:1], lambda kc: OPSC2[:, kc:kc + 1], 6 + par_, part=part)
                if part == 1:
                    return
                if row0 == 0:
                    ts("dve", U2[:, :, 0:64], U2[:, :, 0:64], ctc("HM", 0, 1), None, ALU.mult, None, ["U2", "CT"], ["U2"])
                if row0 == 32:
                    ts("dve", U2[:, :, 1088:1152], U2[:, :, 1088:1152], ctc("HM", 1, 2), None, ALU.mult, None, ["U2", "CT"], ["U2"])

            for hf in range(2):
                if hf == 0:
                    for j in [1, 2, 3, 4, 5, 6, 7, 8]:
                        u2_chunk(0, j)
                    R.add("dve", lambda e: e.memset(HX[:, :], 0.0), [], ["HX"])
                    ghs = [(XO2[0], "XO20"), (XO2[1], "XO21"), (Z2, "Z"), (GA[0], "GA0")]
                    for r_ in range(4):
                        gh, ghk = ghs[r_]
                        DMA("sp", gh[0:64, :], ffout.ap()[r_ * 128 + 64:r_ * 128 + 128, :], r=["ffout"], w=[ghk])
                        DMA("sp", gh[64:128, :], ffout.ap()[r_ * 128:r_ * 128 + 64, :], r=["ffout"], w=[ghk])
                    for r_ in range(4):
                        gh, ghk = ghs[r_]
                        stt("dve", HX[:, :], gh[:, :], ctc("OH", r_, r_ + 1), HX[:, :], ALU.mult, ALU.add, [ghk, "CT", "HX"], ["HX"])
                    DMA("sp", hxd.ap()[64:128, :], HX[64:128, :], r=["HX"], w=["hxd"])
                    u2_chunk(0, 0)
                for pi in range(22):
                    wu = WUP[pi_glob % 2]; wuk = "WUP%d" % (pi_glob % 2)
                    dg = DG[0]; dgk = "DG0"
                    ha = HA[0]; hak = "HA0"
                    ga = GA[0]; gak = "GA0"
                    DMA("pool", wu[:, :, 0:128], w_up_v[:, :, pi * 128:(pi + 1) * 128], w=[wuk])
                    DMA("pool", wu[:, :, 128:256], w_up_v[:, :, 2816 + pi * 128:2816 + (pi + 1) * 128], w=[wuk])
                    for t in range(9):
                        act(dg[:, 0, t, :], IDB[:, :], AF.Copy, ["IDB", "CW"], [dgk], scale=CW[:, pi, t:t + 1])
                    for ab in range(2):
                        for nb, (n0, n1) in enumerate(((0, 384), (384, 768), (768, 1152))):
                            bb = 3 * ab + nb
                            for kc in range(8):
                                mm(PS[:, bb * 512:bb * 512 + (n1 - n0)], wu[:, kc, ab * 128:(ab + 1) * 128], U2[:, kc, n0:n1],
                                   kc == 0, kc == 7, [wuk, "U2"], ["ps%d" % bb])
                    pa = PS[:, 0:1536].rearrange("p (b x) -> p b x", x=512)[:, :, 0:384].rearrange("p b (r c) -> p b r c", c=64)
                    pb_ = PS[:, 1536:3072].rearrange("p (b x) -> p b x", x=512)[:, :, 0:384].rearrange("p b (r c) -> p b r c", c=64)

                    def v4(ap):
                        return ap.rearrange("p (b r) c -> p b r c", b=3)

                    act(v4(ha[:, 1, :, :]), pa, AF.Copy, bkeys(0, 3), [hak])
                    act(v4(ha[:, 0, :, 1:64]), pa[:, :, :, 0:63], AF.Copy, bkeys(0, 3), [hak])
                    act(v4(ha[:, 2, :, 0:63]), pa[:, :, :, 1:64], AF.Copy, bkeys(0, 3), [hak])
                    act(v4(HBP[:, :, 1:65]), pb_, AF.Copy, bkeys(3, 3), ["HBP"])
                    for blk in range(2):
                        bb = 6 + blk
                        for t in range(9):
                            dr, dc_ = t // 3 - 1, t % 3 - 1
                            mm(bank(bb), dg[:, 0, t, :], ha[:, dc_ + 1, 1 + 8 * blk + dr:9 + 8 * blk + dr, :],
                               t == 0, t == 8, [dgk, hak], ["ps%d" % bb])
                    chb = 22 + pi
                    accv = ACC[:, :].rearrange("p (r c) -> p r c", c=64)
                    for t in range(9):
                        dr, dc_ = t // 3 - 1, t % 3 - 1
                        win = HBP[:, 1 + dr:17 + dr, 1 + dc_:65 + dc_]
                        if t == 0:
                            act(accv, win, AF.Identity, ["HBP", "CW", "CB"], ["ACC"], bias=CB[:, chb:chb + 1], scale=CW[:, chb, 0:1])
                        else:
                            stt("dve", accv, win, CW[:, chb, t:t + 1], accv, ALU.mult, ALU.add, ["HBP", "CW", "ACC"], ["ACC"])
                    act(ga[:, :], bank(6, 2), AF.Gelu_apprx_tanh, ["ps6", "ps7"], [gak], bias=CB[:, pi:pi + 1])
                    tt("dve", GT[:, pi, :], ACC[:, :], ga[:, :], ALU.mult, ["ACC", gak], ["GT"])
                    pi_glob += 1
                if hf == 0:
                    u2_load(1, 0)
                DMA("sp", X2[ci % 2][:, :], x1d[hf * 8 * 128:(hf * 8 + 1) * 128, :], r=["x1d%d" % (hf * 8)], w=["X2%d" % (ci % 2)])
                for c in range(8):
                    gc = hf * 8 + c
                    xc = X2[ci % 2]; xck = "X2%d" % (ci % 2)
                    xo = XO2[ci % 2]; xok = "XO2%d" % (ci % 2)
                    ci += 1
                    if c + 1 < 8:
                        DMA("sp", X2[ci % 2][:, :], x1d[(gc + 1) * 128:(gc + 2) * 128, :], r=["x1d%d" % (gc + 1)], w=["X2%d" % (ci % 2)])
                    pb = 2 + 2 * (c % 2)
                    if hf == 0:
                        u2_load(1, c + 1)
                        u2_chunk(1, c, part=1, load=False)
                    for nb in range(2):
                        for kc in range(22):
                            mm(bank(pb + nb), GT[:, kc, c * 128:(c + 1) * 128], WD[:, kc, nb * 512:(nb + 1) * 512], kc == 0, kc == 21,
                               ["WD%d" % kc, "GT"], ["ps%d" % (pb + nb)])
                    if hf == 0:
                        u2_chunk(1, c, part=2)
                    epilogue(pb, xc, xck, BC2, Z2, xo, xok, folded=False)
                    DMA("sp", out_d[gc * 128:(gc + 1) * 128, :], xo[:, :], r=[xok], w=["out%d" % gc])
                if hf == 0:
                    u2_chunk(1, 8, load=False)

    except _Stop:
        pass

    if STAGE == 1:
        DMA("sp", out_d[0:128, 0:1024], SST[:, :, :, :].rearrange("p a b c -> p (a b c)"), r=["SST"], w=["o"])
        DMA("sp", out_d[128:256, 0:1024], SCTX[:, :, :, :].rearrange("p a b c -> p (a b c)"), r=["SCTXf", "SCTXb"], w=["o"])
        DMA("sp", out_d[256:384, 0:96], MODT[:, :, :].rearrange("p a b -> p (a b)"), r=["MODT"], w=["o"])
        DMA("sp", out_d[384:512, 0:16], LG[:, :], r=["LG"], w=["o"])

    with (nc.semaphore("s_pe") as s_pe, nc.semaphore("s_act") as s_act, nc.semaphore("s_dve") as s_dve,
          nc.semaphore("s_pool") as s_pool, nc.semaphore("s_sp") as s_sp, nc.semaphore("s_cc") as s_cc):
        import contextlib
        with contextlib.ExitStack() as es:
            dsems = [es.enter_context(nc.semaphore("s_d%d" % i)) for i in range(NDS)]
            with nc.Block() as block:
                R.emit(nc, block, dict(pe=s_pe, act=s_act, dve=s_dve, pool=s_pool, sp=s_sp), dsems, s_cc)
    return nc


_NC_CACHE = {}


def kernel(x, c, ctx, c_ctx, w_ada, b_ada, w_in, ret_decay_logit, pool_w, pool_scale,
           w_branch_ret, w_branch_pool, w_out, ln1_g, ln1_b, w_up, conv_w, conv_b, w_down, ln2_g, ln2_b):
    f = lambda a: np.ascontiguousarray(np.asarray(a, dtype=np.float32))
    x = f(x); ctx = f(ctx); c = f(c); c_ctx = f(c_ctx)
    if "nc" not in _NC_CACHE:
        _NC_CACHE["nc"] = build()
    nc = _NC_CACHE["nc"]
    shared = dict(
        w_ada=f(w_ada[0]), bada=f(b_ada[0].reshape(48, 128).T), w_in=f(w_in[0]), lgt=f(ret_decay_logit[0].reshape(1, 16)),
        pool_w=f(pool_w[0]), psc=f(pool_scale[0].reshape(4, 128).T), w_br=f(w_branch_ret[0]), w_bp=f(w_branch_pool[0]),
        w_out=f(w_out[0]), ln1g=f(ln1_g[0].reshape(1, D)), ln1b=f(ln1_b[0].reshape(1, D)),
        ln2g=f(ln2_g[0].reshape(1, D)), ln2b=f(ln2_b[0].reshape(1, D)), w_up=f(w_up[0]),
        cwf=f(conv_w[0].reshape(9, 44, 128).transpose(2, 1, 0).reshape(128, 44 * 9)),
        cbf=f(conv_b[0].reshape(44, 128).T), w_down=f(w_down[0]),
    )
    in_maps = []
    for core in range(NCORE):
        b, s = core // 4, core % 4
        xh = np.zeros((16, D), np.float32)
        if s > 0:
            xh[0:8] = x[b, s * SEG - 8:s * SEG]
        if s < 3:
            xh[8:16] = x[b, (s + 1) * SEG:(s + 1) * SEG + 8]
        cfm = np.concatenate([c[b].reshape(8, 128).T, c_ctx.reshape(8, 128).T], axis=1)
        m = dict(shared)
        m.update(xs=f(x[b, s * SEG:(s + 1) * SEG]), xh=xh, ctxs=f(ctx[b]), cfm=f(cfm), ct=_const_table(core))
        in_maps.append(m)
    res = run_bass_kernel_spmd(nc, in_maps, core_ids=list(range(NCORE)))
    out = np.zeros((2, 8192, D), np.float32)
    for core in range(NCORE):
        b, s = core // 4, core % 4
        out[b, s * SEG:(s + 1) * SEG] = np.asarray(res.results[core]["out"])
    return out
```

```python
import numpy as np
import concourse.bass as bass
import concourse.mybir as mybir
from concourse.bass_utils import run_bass_kernel_spmd

F32 = mybir.dt.float32
BF16 = mybir.dt.bfloat16
AF = mybir.ActivationFunctionType
ALU = mybir.AluOpType

STAGE = 3
NCORE = 8
import os
CUT = int(os.environ.get('KCUT', '99'))


class _Stop(Exception):
    pass

D = 1024
SEG = 2048
NCH = 16
EPS = 1e-6
ALPHA = 2.0 ** 0.25
NDS = 24
GROUPS = [[0, 1, 2, 3], [4, 5, 6, 7]]

_CT = {}
_off = 0
for _n, _w in [("IDENT", 128), ("D1", 128), ("D2", 128), ("EYE8", 128), ("IOTA1", 128), ("IOTA2", 128),
               ("ONES", 128), ("JREV", 1), ("JFWD", 1), ("CEXP", 17), ("NEXPF", 5), ("MASKF", 5),
               ("NEXPB", 5), ("MASKB", 5), ("PMASK", 2), ("PE", 1), ("PO", 1), ("EDGEL", 32), ("EDGER", 32), ("OH", 4), ("HM", 2)]:
    _CT[_n] = (_off, _w)
    _off += _w
NCT = _off


def _const_table(core):
    s = core % 4
    t = np.zeros((128, NCT), np.float32)

    def put(name, arr):
        o, w = _CT[name]
        t[:, o:o + w] = arr

    p = np.arange(128)[:, None].astype(np.float32)
    i = np.arange(128)[None, :].astype(np.float32)
    put("IDENT", np.eye(128, dtype=np.float32))
    put("D1", np.maximum(i - p, 0))
    put("D2", np.maximum(p - i, 0))
    put("EYE8", 0.125 * np.eye(128, dtype=np.float32))
    put("IOTA1", np.broadcast_to(i + 1, (128, 128)))
    put("IOTA2", np.broadcast_to(128 - i, (128, 128)))
    put("ONES", np.ones((128, 128), np.float32))
    put("JREV", 127 - p)
    put("JFWD", p)
    put("CEXP", np.broadcast_to(128.0 * np.arange(17)[None, :], (128, 17)))
    nf = np.zeros(5); mf = np.zeros(5); nb = np.zeros(5); mb = np.zeros(5)
    for r in range(4):
        if r < s:
            nf[r] = s - 1 - r; mf[r] = 1
        if r > s:
            nb[r] = r - s - 1; mb[r] = 1
    nf[4] = s; mf[4] = 1; nb[4] = 3 - s; mb[4] = 1
    put("NEXPF", np.broadcast_to(2048.0 * nf[None, :], (128, 5)))
    put("MASKF", np.broadcast_to(mf[None, :], (128, 5)))
    put("NEXPB", np.broadcast_to(2048.0 * nb[None, :], (128, 5)))
    put("MASKB", np.broadcast_to(mb[None, :], (128, 5)))
    put("PE", (p < 64).astype(np.float32))
    put("PO", (p >= 64).astype(np.float32))
    put("PMASK", np.broadcast_to(np.array([1.0 if s > 0 else 0.0, 1.0 if s < 3 else 0.0])[None, :], (128, 2)))
    L = 8192
    el = np.ones((4, 8)); er = np.ones((4, 8))
    for g, w in enumerate((2, 4, 8, 16)):
        for j in range(8):
            tpos = s * SEG + j
            cnt = min(tpos + w // 2, L) - max(tpos - w // 2, 0)
            el[g, j] = w / cnt
            tpos = s * SEG + SEG - 8 + j
            cnt = min(tpos + w // 2, L) - max(tpos - w // 2, 0)
            er[g, j] = w / cnt
    put("EDGEL", np.broadcast_to(el.reshape(1, 32), (128, 32)))
    put("EDGER", np.broadcast_to(er.reshape(1, 32), (128, 32)))
    oh = np.zeros((128, 4))
    if s > 0:
        oh[0:64, s - 1] = 1
    if s < 3:
        oh[64:128, s + 1] = 1
    put("OH", oh)
    put("HM", np.broadcast_to(np.array([1.0 if s > 0 else 0.0, 1.0 if s < 3 else 0.0])[None, :], (128, 2)))
    return t


class Rec:
    def __init__(self):
        self.ops = []

    def add(self, eng, fn, r=(), w=(), kind="c"):
        self.ops.append(dict(eng=eng, fn=fn, r=tuple(r), w=tuple(w), kind=kind))
        return len(self.ops) - 1

    def barrier(self, keep=()):
        self.ops.append(dict(eng="*", fn=None, r=(), w=(), kind="bar", keep=tuple(keep)))

    def emit(self, nc, block, sems, dsems, ccsem):
        ops = self.ops
        n = len(ops)
        deps = [set() for _ in range(n)]
        lastw = {}
        readers = {}
        last_eng = {}
        dma_since = []
        frontier = set()
        dma_slot_last = {}
        ndma = 0
        npool = 0
        slot_of = {}
        last_cc = []
        for i, o in enumerate(ops):
            if o["kind"] == "bar":
                keep = o.get("keep", ())
                frontier = set(last_eng.values())
                for q in dma_slot_last.values():
                    if keep and ops[q]["w"] and all(k in keep for k in ops[q]["w"]):
                        continue
                    frontier.add(q)
                if not keep:
                    frontier |= set(last_cc)
                kw_ = {k: lastw[k] for k in keep if k in lastw}
                kr_ = {k: readers[k] for k in keep if k in readers}
                lastw.clear(); readers.clear()
                lastw.update(kw_); readers.update(kr_)
                continue
            dp = deps[i]
            dp |= frontier
            for k in o["r"]:
                if k in lastw:
                    dp.update(lastw[k])
            for k in o["w"]:
                if k in lastw:
                    same_burst = (o["kind"] == "dma" and all(ops[q]["kind"] == "dma" for q in lastw[k])
                                  and not readers.get(k))
                    if not same_burst:
                        dp.update(lastw[k])
                for rr in readers.get(k, ()):
                    dp.add(rr)
            for k in o["w"]:
                if o["kind"] == "dma" and k in lastw and all(ops[q]["kind"] == "dma" for q in lastw[k]) and not readers.get(k):
                    lastw[k] = lastw[k] + [i]
                else:
                    lastw[k] = [i]
                readers[k] = []
            for k in o["r"]:
                lst = readers.setdefault(k, [])
                if o["kind"] == "c":
                    lst[:] = [q for q in lst if not (ops[q]["kind"] == "c" and ops[q]["eng"] == o["eng"])]
                lst.append(i)
            if o["kind"] == "dma":
                half = NDS // 2
                if o["eng"] == "pool":
                    sl = half + (npool % half)
                    npool += 1
                else:
                    sl = ndma % half
                    ndma += 1
                slot_of[i] = sl
                if sl in dma_slot_last:
                    dp.add(dma_slot_last[sl])
                dma_slot_last[sl] = i
            elif o["kind"] == "c":
                last_eng[o["eng"]] = i
            elif o["kind"] == "cc":
                last_cc = [i]
            dp.discard(i)
        hasdep = [False] * n
        for i in range(n):
            for d in deps[i]:
                hasdep[d] = True
        tok = [None] * n
        cnt = {e: 0 for e in sems}
        dcnt = [0] * NDS
        cccnt = 0
        pos_in_eng = [0] * n
        epos = {}
        for i, o in enumerate(ops):
            if o["kind"] == "bar":
                continue
            if o["kind"] == "dma":
                sl = slot_of[i]
                dcnt[sl] += 16
                tok[i] = (dsems[sl], dcnt[sl], 16)
            elif o["kind"] == "cc":
                cccnt += 1
                tok[i] = (ccsem, cccnt, 1)
            else:
                e = o["eng"]
                epos[e] = epos.get(e, 0) + 1
                pos_in_eng[i] = epos[e]
                if hasdep[i]:
                    cnt[e] += 1
                    tok[i] = (sems[e], cnt[e], 1)
        final_d = list(dcnt)

        def run_engine(ename, eh):
            waited = {}
            for i, o in enumerate(ops):
                if o["kind"] == "bar" or o["eng"] != ename:
                    continue
                for d in sorted(deps[i]):
                    od = ops[d]
                    t = tok[d]
                    if t is None:
                        continue
                    if od["kind"] == "c" and od["eng"] == ename:
                        if ename == "pe":
                            continue
                        if o["kind"] == "c" and pos_in_eng[i] - pos_in_eng[d] > 3:
                            continue
                    sem, val, _ = t
                    if waited.get(sem.num, 0) < val:
                        eh.wait_ge(sem, val)
                        waited[sem.num] = val
                ins = o["fn"](eh)
                t = tok[i]
                if t is not None:
                    ins.then_inc(t[0], t[2])
            if ename == "sp":
                for sl in range(NDS):
                    if final_d[sl] > 0:
                        eh.wait_ge(dsems[sl], final_d[sl])

        block.sync(lambda e: run_engine("sp", e))
        block.tensor(lambda e: run_engine("pe", e))
        block.scalar(lambda e: run_engine("act", e))
        block.vector(lambda e: run_engine("dve", e))
        block.gpsimd(lambda e: run_engine("pool", e))


def build():
    nc = bass.Bass("TRN2", target_bir_lowering=False)
    R = Rec()

    def din(name, shape, dt=F32):
        return nc.dram_tensor(name, list(shape), dt, kind="ExternalInput").ap()

    xs = din("xs", [SEG, D]); xh = din("xh", [16, D]); ctxs = din("ctxs", [256, D])
    cfm = din("cfm", [128, 16]); ct_d = din("ct", [128, NCT]); lgt = din("lgt", [1, 16])
    w_ada = din("w_ada", [D, 6 * D]); bada = din("bada", [128, 48])
    w_in = din("w_in", [D, 5632]); pool_w = din("pool_w", [4, 128, 128]); psc = din("psc", [128, 4])
    w_br = din("w_br", [D, D]); w_bp = din("w_bp", [512, D]); w_out = din("w_out", [D, D])
    ln1g = din("ln1g", [1, D]); ln1b = din("ln1b", [1, D]); ln2g = din("ln2g", [1, D]); ln2b = din("ln2b", [1, D])
    w_up = din("w_up", [D, 5632]); cwf = din("cwf", [128, 44 * 9]); cbf = din("cbf", [128, 44])
    w_down = din("w_down", [2816, D])
    out_d = nc.dram_tensor("out", [SEG, D], F32, kind="ExternalOutput").ap()
    x1d = nc.dram_tensor("x1d", [SEG, D], F32).ap()
    ccin = nc.dram_tensor("ccin", [128, 1024], F32)
    ccout = nc.dram_tensor("ccout", [4 * 128, 1024], F32)
    ffin = nc.dram_tensor("ffin", [128, 1024], F32)
    ffout = nc.dram_tensor("ffout", [4 * 128, 1024], F32)
    hxd = nc.dram_tensor("hxd", [128, 1024], F32)

    w_in_v = w_in.rearrange("(kc p) n -> p kc n", p=128)
    w_up_v = w_up.rearrange("(kc p) n -> p kc n", p=128)
    w_ada_v = w_ada.rearrange("(kc p) n -> p kc n", p=128)

    cur = [16640]

    def alloc(name, shape, dt):
        nb = int(np.prod(shape[1:])) * (4 if dt == F32 else 2)
        nb = (nb + 31) // 32 * 32
        assert cur[0] + nb <= 229120, (name, cur[0], nb)
        t = nc.alloc_sbuf_tensor_at(name, list(shape), dt, offset=cur[0])
        cur[0] += nb
        return t

    CT = alloc("CT", [128, NCT], F32)

    def ctc(name, a=0, b=None):
        o, w = _CT[name]
        return CT[:, o + a:o + (w if b is None else b)]

    IDB = alloc("IDB", [128, 128], BF16)
    LG = alloc("LG", [128, 16], F32)
    LGP = alloc("LGP", [128, 8], F32)
    SCR = alloc("SCR", [128, 6, 16], F32)
    MT = alloc("MT", [128, 8, 128], F32)
    QFM = alloc("QFM", [128, 4, 4, 128], F32)
    KF = alloc("KF", [128, 8], F32)
    KB = alloc("KB", [128, 8], F32)
    CDX = alloc("CDX", [128, 2, 4, 128], F32)
    CD = alloc("CD", [128, 8], F32)
    CDPOW = alloc("CDPOW", [128, 8, 17], F32)
    COEF = alloc("COEF", [128, 8, 5], F32)
    CFM = alloc("CFM", [128, 16], F32)
    SC = alloc("SC", [128, 16], BF16)
    BADA = alloc("BADA", [128, 48], F32)
    MODT = alloc("MODT", [128, 48, 2], F32)
    OPSC1 = alloc("OPSC1", [128, 8, 2], F32)
    OPSC2 = alloc("OPSC2", [128, 8], F32)
    PSC = alloc("PSC", [128, 4], F32)
    CW = alloc("CW", [128, 44, 9], F32)
    CB = alloc("CB", [128, 44], F32)
    SST = alloc("SST", [128, 2, 4, 128], F32)
    SRUN = alloc("SRUN", [128, 4, 128], F32)
    LNS = alloc("LNS", [128, 2, 16], F32)
    LNM = alloc("LNM", [128, 2, 4], F32)
    W_BASE = cur[0]
    W_SIZE = 34816
    cur[0] += W_SIZE
    BIG_BASE = cur[0]
    WB_BASE = 229120 - 45056

    def arena(base):
        cur[0] = base

    PS = nc.alloc_psum_tensor("PS", [128, 4096], F32)

    def bank(b, n=1):
        return PS[:, b * 512:(b + n) * 512]

    def bkeys(b, n=1):
        return ["ps%d" % k for k in range(b, b + n)]

    def A(eng, fn, r=(), w=()):
        return R.add(eng, fn, r, w)

    def DMA(eng, out, in_, r=(), w=()):
        return R.add(eng, lambda e, o=out, i=in_: e.dma_start(out=o, in_=i), r, w, kind="dma")

    def mm(out, lhsT, rhs, start, stop, r, w):
        return R.add("pe", lambda e, o=out, l=lhsT, rh=rhs, s=start, p=stop: e.matmul(o, l, rh, start=s, stop=p), r, w)

    def act(out, in_, func, r, w, bias=0.0, scale=1.0):
        return R.add("act", lambda e, o=out, i=in_, f=func, b=bias, s=scale: e.activation(out=o, in_=i, func=f, bias=b, scale=s), r, w)

    def ts(eng, out, in0, s1, s2, op0, op1, r, w):
        if s2 is None:
            return R.add(eng, lambda e, o=out, i=in0, a=s1, p0=op0: e.tensor_scalar(out=o, in0=i, scalar1=a, scalar2=None, op0=p0), r, w)
        return R.add(eng, lambda e, o=out, i=in0, a=s1, b=s2, p0=op0, p1=op1: e.tensor_scalar(out=o, in0=i, scalar1=a, scalar2=b, op0=p0, op1=p1), r, w)

    def tt(eng, out, in0, in1, op, r, w):
        return R.add(eng, lambda e, o=out, a=in0, b=in1, p=op: e.tensor_tensor(out=o, in0=a, in1=b, op=p), r, w)

    def stt(eng, out, in0, scalar, in1, op0, op1, r, w):
        return R.add(eng, lambda e, o=out, a=in0, s=scalar, b=in1, p0=op0, p1=op1: e.scalar_tensor_tensor(out=o, in0=a, scalar=s, in1=b, op0=p0, op1=p1), r, w)

    def cp(eng, out, in_, r, w):
        return R.add(eng, lambda e, o=out, i=in_: e.tensor_copy(out=o, in_=i), r, w)


    def cut(n):
        if CUT == n:
            raise _Stop()

    try:
        DMA("sp", CT[:, :], ct_d, w=["CT"])
        DMA("sp", LG[:, :], lgt.partition_broadcast(128).rearrange("p o n -> p (o n)"), w=["LG"])
        DMA("sp", CFM[:, :], cfm, w=["CFM"])
        DMA("sp", BADA[:, :], bada, w=["BADA"])
        DMA("sp", PSC[:, :], psc, w=["PSC"])
        DMA("sp", CW[:, :, :].rearrange("p a b -> p (a b)"), cwf, w=["CW"])
        DMA("sp", CB[:, :], cbf, w=["CB"])
        cp("dve", IDB[:, :], ctc("IDENT"), ["CT"], ["IDB"])
        T0 = SCR[:, 0, :]; T1 = SCR[:, 1, :]; T2 = SCR[:, 2, :]; T3 = SCR[:, 3, :]
        act(T0, LG[:, :], AF.Exp, ["LG"], ["T0"], scale=-1.0)
        ts("dve", T1, T0, 2.0, None, ALU.add, None, ["T0"], ["T1"])
        R.add("dve", lambda e: e.reciprocal(out=T1, in_=T1), ["T1"], ["T1"])
        tt("dve", T1, T0, T1, ALU.mult, ["T0", "T1"], ["T1"])
        tt("dve", T2, T1, T1, ALU.mult, ["T1"], ["T2"])
        ts("dve", T3, T2, 1.0 / 15, 1.0 / 13, ALU.mult, ALU.add, ["T2"], ["T3"])
        for cst in (1.0 / 11, 1.0 / 9, 1.0 / 7, 1.0 / 5, 1.0 / 3, 1.0):
            tt("dve", T3, T3, T2, ALU.mult, ["T3", "T2"], ["T3"])
            ts("dve", T3, T3, cst, None, ALU.add, None, ["T3"], ["T3"])
        tt("dve", T3, T3, T1, ALU.mult, ["T3", "T1"], ["T3"])
        ts("dve", LG[:, :], T3, -2.0, None, ALU.mult, None, ["T3"], ["LG"])
        if CUT == -3:
            DMA("sp", out_d[128:256, 0:16], LG[:, :], r=["LG"], w=["o"])
        cut(-3)
        LGv = LG[:, :].rearrange("p (d m r) -> p d m r", d=2, m=4, r=2)
        LGPv = LGP[:, :].rearrange("p (d m) -> p d m", d=2)
        cp("dve", LGPv[0:64], LGv[0:64, :, :, 0], ["LG"], ["LGP"])
        cp("dve", LGPv[64:128], LGv[64:128, :, :, 1], ["LG"], ["LGP"])
        arena(BIG_BASE)
        UT = alloc("UT", [128, 8, 2064], BF16)
        VR = alloc("VR", [128, 16, 1024], BF16)
        AFB = alloc("AFB", [128, 16, 4, 128], BF16)
        KVB = alloc("KVB", [128, 17, 4, 128], BF16)
        XB = [alloc("XB%d" % i, [128, 1024], F32) for i in range(2)]
        XN = [alloc("XN%d" % i, [128, 1024], BF16) for i in range(2)]
        KW = [alloc("KW%d" % i, [128, 2, 512], BF16) for i in range(2)]
        UCS = [alloc("UC%d" % i, [128, 8, 128], BF16) for i in range(2)]
        UH = alloc("UH", [128, 8, 16], BF16)
        VC = alloc("VC", [128, 1024], BF16)
        PACC = alloc("PACC", [128, 2, 4, 128], F32)
        SCTX = alloc("SCTX", [128, 2, 4, 128], F32)
        KVS = alloc("KVS", [128, 2, 4, 128], F32)
        PGB = [alloc("PGB%d" % i, [128, 2, 4, 128], F32) for i in range(2)]
        P1_END = cur[0]
        arena(W_BASE)
        WKV = alloc("WKV", [128, 8, 1536], BF16)
        WAD = [AFB[:, :, :, :].rearrange("p a b c -> p (a b c)").rearrange("p (k n) -> p k n", k=8),
               KVB[:, 0:16, :, :].rearrange("p a b c -> p (a b c)").rearrange("p (k n) -> p k n", k=8)]
        MSC = XB[0][:, 0:128]
        MSC2 = XB[0][:, 128:256]
        for h in range(8):
            ts("dve", MSC, ctc("D1"), LG[:, h:h + 1], None, ALU.mult, None, ["CT", "LG"], ["MSC", "XB0"])
            stt("dve", MSC2, ctc("D2"), LG[:, 8 + h:9 + h], MSC, ALU.mult, ALU.add, ["CT", "LG", "MSC"], ["MSC2", "XB0"])
            act(MSC2, MSC2, AF.Exp, ["MSC2"], ["MSC2", "XB0"])
            stt("dve", MT[:, h, :], MSC2, 0.125, ctc("EYE8"), ALU.mult, ALU.add, ["MSC2", "CT", "XB0"], ["MT"])
        for m in range(4):
            act(QFM[:, 0, m, :], ctc("IOTA1"), AF.Exp, ["CT", "LGP"], ["QF"], scale=LGP[:, m:m + 1])
            act(QFM[:, 2, m, :], ctc("IOTA2"), AF.Exp, ["CT", "LGP"], ["QF"], scale=LGP[:, 4 + m:5 + m])
        cp("dve", QFM[:, 1, :, :], QFM[:, 0, :, :], ["QF"], ["QF"])
        cp("dve", QFM[:, 3, :, :], QFM[:, 2, :, :], ["QF"], ["QF"])
        for q_ in range(4):
            lo = 64 if q_ % 2 == 0 else 0
            R.add("dve", lambda e, t=QFM[lo:lo + 64, q_, :, :]: e.memset(t, 0.0), ["QF"], ["QF"])
        act(KF[:, :], LG[:, 0:8], AF.Exp, ["LG", "CT"], ["KF"], scale=ctc("JREV"))
        act(KB[:, :], LG[:, 8:16], AF.Exp, ["LG", "CT"], ["KB"], scale=ctc("JFWD"))
        ts("dve", KF[:, :], KF[:, :], 0.125, None, ALU.mult, None, ["KF"], ["KF"])
        ts("dve", KB[:, :], KB[:, :], 0.125, None, ALU.mult, None, ["KB"], ["KB"])
        act(CD[:, :], LGP[:, :], AF.Exp, ["LGP"], ["CD"], scale=128.0)
        for j in range(8):
            act(CDPOW[:, j, :], ctc("CEXP"), AF.Exp, ["CT", "LGP"], ["CDPOW"], scale=LGP[:, j:j + 1])
            act(COEF[:, j, :], ctc("NEXPF" if j < 4 else "NEXPB"), AF.Exp, ["CT", "LGP"], ["COEF"], scale=LGP[:, j:j + 1])
            tt("dve", COEF[:, j, :], COEF[:, j, :], ctc("MASKF" if j < 4 else "MASKB"), ALU.mult, ["COEF", "CT"], ["COEF"])
            d_, m_ = j // 4, j % 4
            ts("dve", CDX[:, d_, m_, :], ctc("ONES"), CD[:, j:j + 1], None, ALU.mult, None, ["CT", "CD"], ["CDX"])
        if CUT == -2:
            DMA("sp", out_d[128:256, 0:16], LG[:, :], r=["LG"], w=["o"])
            DMA("sp", out_d[256:384, 0:1024], MT[:, :, :].rearrange("p a b -> p (a b)"), r=["MT"], w=["o"])
        cut(-2)
        act(SC[:, :], CFM[:, :], AF.Silu, ["CFM"], ["SC"])
        SCv = SC[:, :].rearrange("p (v k) -> p k v", v=2)
        def mod_dma(v):
            DMA("pool", WAD[v % 2], w_ada_v[:, :, v * 1024:(v + 1) * 1024], w=["WAD%d" % (v % 2)])

        def mod_mm(v):
            wb = WAD[v % 2]
            for j in range(8):
                col = (v * 8 + j) * 2
                for kc in range(8):
                    mm(PS[:, col:col + 2], wb[:, kc, j * 128:(j + 1) * 128], SCv[:, kc, :], kc == 0, kc == 7,
                       ["WAD%d" % (v % 2), "SC"], ["ps0"])

        mod_dma(0)
        mod_dma(1)
        mod_mm(0)
        mod_dma(2)
        mod_mm(1)
        mod_dma(3)
        tt("dve", MODT[:, 0:16, :], PS[:, 0:32].rearrange("p (a b) -> p a b", b=2),
           BADA[:, 0:16].unsqueeze(2).to_broadcast([128, 16, 2]), ALU.add, ["ps0", "BADA"], ["MODT"])
        ts("dve", OPSC1[:, :, :], MODT[:, 8:16, :], 1.0, None, ALU.add, None, ["MODT"], ["OPSC"])

        lnctr = [0]

        def ln_stats(xt, ntok, rk):
            b = lnctr[0] % 2
            lnctr[0] += 1
            st = LNS[0:ntok, b, :]
            mv = LNM[0:ntok, b, :]
            k = "LN%d" % b
            R.add("dve", lambda e, o=st[:, 0:6], i=xt[:, 0:512]: e.bn_stats(out=o, in_=i), rk, [k])
            R.add("dve", lambda e, o=st[:, 6:12], i=xt[:, 512:1024]: e.bn_stats(out=o, in_=i), rk, [k])
            R.add("dve", lambda e, o=mv[:, 0:2], i=st[:, 0:12]: e.bn_aggr(out=o, in_=i), [k], [k])
            act(mv[:, 2:3], mv[:, 1:2], AF.Sqrt, [k], [k], bias=EPS)
            R.add("dve", lambda e, o=mv[:, 2:3]: e.reciprocal(out=o, in_=o), [k], [k])
            return mv[:, 0:1], mv[:, 2:3], k

        def ln_to_T(xt, ntok, rk, XN, xnk, dst_fn, dstk, sh_fn, osc_fn, pb, part=0):
            if part in (0, 1):
                mean, rstd, k = ln_stats(xt, ntok, rk)
                ts("dve", XN[0:ntok, :], xt, mean, rstd, ALU.subtract, ALU.mult, rk + [k], [xnk])
            if part == 1:
                return
            pbf = bank(pb).bitcast(BF16)
            for kc in range(8):
                R.add("pe", lambda e, o=pbf[:, kc * 128:kc * 128 + ntok], i=XN[0:ntok, kc * 128:(kc + 1) * 128],
                      idn=IDB[0:ntok, 0:ntok]: e.transpose(o, i, idn), [xnk, "IDB"], ["ps%d" % pb])
            for kc in range(8):
                act(dst_fn(kc), pbf[:, kc * 128:kc * 128 + ntok], AF.Identity, ["ps%d" % pb, "MODT", "OPSC"], [dstk],
                    bias=sh_fn(kc), scale=osc_fn(kc))

        for kc in range(8):
            DMA("pool", WKV[:, kc, :], w_in_v[:, kc, 0:1536], w=["W"])
        R.add("dve", lambda e: e.memset(PACC[:, :, :, :], 0.0), [], ["PACCf", "PACCb"])
        R.add("dve", lambda e: e.memset(SCTX[:, :, :, :], 0.0), [], ["SCTXf", "SCTXb"])

        def kv_chunk(u_fn, uk, vdst, vk, accF, accFk, accB, accBk, cidx, bi, afdst=None, kvbdst=None):
            kw = KW[bi % 2]
            kwk = "KW%d" % (bi % 2)
            for kc in range(8):
                mm(bank(1), u_fn(kc), WKV[:, kc, 0:512], kc == 0, kc == 7, [uk, "W"], ["ps1"])
            for nb in range(2):
                for kc in range(8):
                    mm(bank(2 + nb), u_fn(kc), WKV[:, kc, 512 + nb * 512:1024 + nb * 512], kc == 0, kc == 7, [uk, "W"], ["ps%d" % (2 + nb)])
            k3 = bank(1).rearrange("p (h d) -> p h d", d=64)
            tt("dve", kw[:, 0, :].rearrange("p (h d) -> p h d", d=64), k3, KF[:, :].unsqueeze(2).to_broadcast([128, 8, 64]),
               ALU.mult, ["ps1", "KF"], [kwk])
            tt("dve", kw[:, 1, :].rearrange("p (h d) -> p h d", d=64), k3, KB[:, :].unsqueeze(2).to_broadcast([128, 8, 64]),
               ALU.mult, ["ps1", "KB"], [kwk])
            act(vdst, bank(2, 2), AF.Copy, ["ps2", "ps3"], [vk])
            for d_ in range(2):
                for m in range(4):
                    b0 = 4 + 2 * d_ + m // 2
                    mm(PS[:, b0 * 512 + (m % 2) * 256: b0 * 512 + (m % 2) * 256 + 256], kw[:, d_, m * 128:(m + 1) * 128],
                       vdst[:, m * 256:(m + 1) * 256], True, True, [kwk, vk], ["ps%d" % b0])
            for d_ in range(2):
                src = bank(4 + 2 * d_, 2).rearrange("p (m x) -> p m x", x=256)
                act(KVS[0:64, d_, :, :], src[0:64, :, 0:128], AF.Copy, ["ps%d" % (4 + 2 * d_), "ps%d" % (5 + 2 * d_)], ["KVS%d" % d_])
                act(KVS[64:128, d_, :, :], src[64:128, :, 128:256], AF.Copy, ["ps%d" % (4 + 2 * d_), "ps%d" % (5 + 2 * d_)], ["KVS%d" % d_])
            if afdst is not None:
                act(afdst, accF, AF.Copy, [accFk], ["AFB", "WAD0"])
            tt("dve", accF, accF, CDX[:, 0, :, :], ALU.mult, [accFk, "CDX"], [accFk])
            tt("dve", accF, accF, KVS[:, 0, :, :], ALU.add, [accFk, "KVS0"], [accFk])
            for m in range(4):
                stt("dve", accB[:, m, :], KVS[:, 1, m, :], CDPOW[:, 4 + m, cidx:cidx + 1], accB[:, m, :], ALU.mult, ALU.add,
                    ["KVS1", "CDPOW", accBk], [accBk])
            if kvbdst is not None:
                act(kvbdst, KVS[:, 1, :, :], AF.Copy, ["KVS1"], ["KVB", "WAD1"])

        for cc_ in range(2):
            xb = XB[cc_ % 2]
            DMA("sp", xb[:, :], ctxs[cc_ * 128:(cc_ + 1) * 128, :], w=["XB%d" % (cc_ % 2)])
            ln_to_T(xb[:, :], 128, ["XB%d" % (cc_ % 2)], XN[cc_ % 2], "XN%d" % (cc_ % 2),
                    lambda kc, cc_=cc_: UCS[cc_][:, kc, :], "UC%d" % cc_, lambda kc: MODT[:, kc, 1:2], lambda kc: OPSC1[:, kc, 1:2], 7)
        DMA("sp", XB[0][0:16, :], xh, w=["XB0"])
        ln_to_T(XB[0][0:16, :], 16, ["XB0"], XN[0], "XN0", lambda kc: UH[:, kc, 0:16], "UH",
                lambda kc: MODT[:, kc, 0:1], lambda kc: OPSC1[:, kc, 0:1], 7)
        ts("dve", UT[:, :, 0:8], UH[:, :, 0:8], ctc("PMASK", 0, 1), None, ALU.mult, None, ["UH", "CT"], ["UTh"])
        ts("dve", UT[:, :, 2056:2064], UH[:, :, 8:16], ctc("PMASK", 1, 2), None, ALU.mult, None, ["UH", "CT"], ["UTh"])
        def p1_a(c):
            xb = XB[c % 2]
            DMA("sp", xb[:, :], xs[c * 128:(c + 1) * 128, :], w=["XB%d" % (c % 2)])
            ln_to_T(xb[:, :], 128, ["XB%d" % (c % 2)], XN[c % 2], "XN%d" % (c % 2),
                    lambda kc, c=c: UT[:, kc, 8 + c * 128:8 + (c + 1) * 128], "UT%d" % c,
                    lambda kc: MODT[:, kc, 0:1], lambda kc: OPSC1[:, kc, 0:1], 7)

        def p1_b(c):
            kv_chunk(lambda kc, c=c: UT[:, kc, 8 + c * 128:8 + (c + 1) * 128], "UT%d" % c, VR[:, c, :], "VR%d" % c,
                     PACC[:, 0, :, :], "PACCf", PACC[:, 1, :, :], "PACCb", c, c,
                     afdst=AFB[:, c, :, :], kvbdst=KVB[:, c, :, :])

        for c in range(NCH):
            p1_a(c)
            if c in (1, 5, 9, 13):
                v_ = 2 + (c - 1) // 4
                mod_mm(v_)
                if v_ + 2 < 6:
                    mod_dma(v_ + 2)
        tt("dve", MODT[:, 16:48, :], PS[:, 32:96].rearrange("p (a b) -> p a b", b=2),
           BADA[:, 16:48].unsqueeze(2).to_broadcast([128, 32, 2]), ALU.add, ["ps0", "BADA"], ["MODT2"])
        ts("dve", OPSC2[:, :], MODT[:, 32:40, 0], 1.0, None, ALU.add, None, ["MODT2"], ["OPSC2"])
        for cc_ in range(2):
            kv_chunk(lambda kc, cc_=cc_: UCS[cc_][:, kc, :], "UC%d" % cc_, VC[:, :], "VC", SCTX[:, 0, :, :], "SCTXf",
                     SCTX[:, 1, :, :], "SCTXb", cc_, cc_)
        for c in range(NCH):
            p1_b(c)
        DMA("sp", ccin.ap(), PACC[:, :, :, :].rearrange("p a b c -> p (a b c)"), r=["PACCf", "PACCb"], w=["ccin"])
        R.add("pool", lambda e: e.collective_compute("AllGather", ALU.bypass, replica_groups=GROUPS,
                                                     ins=[ccin.ap().opt()], outs=[ccout.ap().opt()]),
              ["ccin"], ["ccout"], kind="cc")
        arena(W_BASE)
        WQ = alloc("WQ", [128, 8, 512], BF16)
        WK = alloc("WK", [128, 8, 512], BF16)
        WG = alloc("WG", [128, 8, 1024], BF16)
        for kc in range(8):
            DMA("pool", WK[:, kc, :], w_in_v[:, kc, 0:512], r=[], w=["W"])
            DMA("pool", WQ[:, kc, :], w_in_v[:, kc, 1536:2048], w=["W"])
            DMA("pool", WG[:, kc, :], w_in_v[:, kc, 2048:3072], w=["W"])
        for d_ in range(2):
            for m in range(4):
                ts("dve", SST[:, d_, m, :], SCTX[:, d_, m, :], COEF[:, d_ * 4 + m, 4:5], None, ALU.mult, None,
                   ["SCTXf", "SCTXb", "COEF"], ["SST"])
        for r_ in range(4):
            pg = PGB[r_ % 2]
            DMA("sp", pg[:, :, :, :].rearrange("p a b c -> p (a b c)"), ccout.ap()[r_ * 128:(r_ + 1) * 128, :],
                r=["ccout"], w=["PGB%d" % (r_ % 2)])
            for d_ in range(2):
                for m in range(4):
                    stt("dve", SST[:, d_, m, :], pg[:, d_, m, :], COEF[:, d_ * 4 + m, r_:r_ + 1], SST[:, d_, m, :], ALU.mult, ALU.add,
                        ["PGB%d" % (r_ % 2), "COEF", "SST"], ["SST"])
        cp("dve", SRUN[:, :, :], SST[:, 1, :, :], ["SST"], ["SRUN"])
        act(KVB[:, 16, :, :], SRUN[:, :, :], AF.Copy, ["SRUN"], ["SB16"])
        R.barrier()
        if CUT == 1:
            DMA("sp", out_d[0:128, 0:1024], SST[:, :, :, :].rearrange("p a b c -> p (a b c)"), r=["SST"], w=["o"])
            DMA("sp", out_d[128:256, 0:1024], SCTX[:, :, :, :].rearrange("p a b c -> p (a b c)"), r=["SCTXf", "SCTXb"], w=["o"])
            DMA("sp", out_d[256:384, 0:1024], PACC[:, :, :, :].rearrange("p a b c -> p (a b c)"), r=["PACCf", "PACCb"], w=["o"])
        cut(1)

        arena(BIG_BASE + 8 * 2064 * 2 + 16 * 1024 * 2 + 16 * 512 * 2 + 17 * 512 * 2)
        QK = [alloc("QK%d" % i, [128, 7, 512], BF16) for i in range(2)]
        STt = [alloc("ST%d" % i, [128, 8, 128], BF16) for i in range(2)]
        SG = [alloc("SG%d" % i, [128, 1024], BF16) for i in range(3)]
        YS = [alloc("YS%d" % i, [128, 1024], F32) for i in range(2)]
        YN = alloc("YN", [128, 1024], F32)
        RGT = [alloc("RGT%d" % i, [128, 1024], BF16) for i in range(2)]
        YSTd = [alloc("YST%d" % i, [128, 8, 6], F32) for i in range(2)]
        YMVd = [alloc("YMV%d" % i, [128, 8, 4], F32) for i in range(2)]
        def p2a_a(c):
            for m in range(4):
                stt("dve", AFB[:, c, m, :], SST[:, 0, m, :], CDPOW[:, m, c:c + 1], AFB[:, c, m, :],
                    ALU.mult, ALU.add, ["SST", "CDPOW", "AFB"], ["SF%d" % c])
            if c < NCH - 1:
                tt("dve", SRUN[:, :, :], SRUN[:, :, :], CDX[:, 1, :, :], ALU.mult, ["SRUN", "CDX"], ["SRUN"])
                tt("dve", SRUN[:, :, :], SRUN[:, :, :], KVB[:, c + 1, :, :], ALU.add, ["SRUN", "KVB", "SB%d" % (c + 1)], ["SRUN"])
                act(KVB[:, c + 1, :, :], SRUN[:, :, :], AF.Copy, ["SRUN"], ["SB%d" % (c + 1), "KVB"])
            qk = QK[c % 2]; qkk = "QK%d" % (c % 2)
            st = STt[c % 2]; stk = "ST%d" % (c % 2)
            ucols = slice(8 + c * 128, 8 + (c + 1) * 128)
            sg = SG[c % 3]; sgk_ = "SG%d" % (c % 3)
            for m in range(4):
                for kc in range(8):
                    mm(PS[:, m * 128:(m + 1) * 128], WQ[:, kc, m * 128:(m + 1) * 128], UT[:, kc, ucols], kc == 0, kc == 7,
                       ["W", "UT%d" % c], ["ps0"])
            for m in range(4):
                for kc in range(8):
                    mm(PS[:, 512 + m * 128:512 + (m + 1) * 128], WK[:, kc, m * 128:(m + 1) * 128], UT[:, kc, ucols], kc == 0, kc == 7,
                       ["W", "UT%d" % c], ["ps1"])
            for nb in range(2):
                for kc in range(8):
                    mm(bank(2 + nb), UT[:, kc, ucols], WG[:, kc, nb * 512:(nb + 1) * 512], kc == 0, kc == 7,
                       ["W", "UT%d" % c], ["ps%d" % (2 + nb)])
            for q_ in range(4):
                tt("dve", qk[:, q_, :], bank(0), QFM[:, q_, :, :].rearrange("p a b -> p (a b)"), ALU.mult, ["ps0", "QF"], [qkk])
            act(qk[:, 4, :], bank(0), AF.Copy, ["ps0"], [qkk])
            act(qk[:, 5, :], bank(1), AF.Copy, ["ps1", "CT"], [qkk], scale=ctc("PE"))
            act(qk[:, 6, :], bank(1), AF.Copy, ["ps1", "CT"], [qkk], scale=ctc("PO"))
            act(sg[:, :], bank(2, 2), AF.Silu, ["ps2", "ps3"], [sgk_])

        def p2a_b(c):
            qk = QK[c % 2]; qkk = "QK%d" % (c % 2)
            st = STt[c % 2]; stk = "ST%d" % (c % 2)
            ucols = slice(8 + c * 128, 8 + (c + 1) * 128)
            sg = SG[c % 3]; sgk_ = "SG%d" % (c % 3)
            for h in range(8):
                m, par = h // 2, h % 2
                pr = slice(par * 64, par * 64 + 64)
                b0 = 4 + h // 4
                mm(PS[:, b0 * 512 + (h % 4) * 128: b0 * 512 + (h % 4 + 1) * 128], qk[:, 5 + par, m * 128:(m + 1) * 128],
                   qk[:, 4, m * 128:(m + 1) * 128], True, True, [qkk], ["ps%d" % b0])
            for hb in range(2):
                tt("dve", st[:, hb * 4:(hb + 1) * 4, :], bank(4 + hb).rearrange("p (a b) -> p a b", b=128), MT[:, hb * 4:(hb + 1) * 4, :],
                   ALU.mult, ["ps%d" % (4 + hb), "MT"], [stk])
            for h in range(8):
                m, par = h // 2, h % 2
                pr = slice(par * 64, par * 64 + 64)
                b0 = 6 + h // 4
                o = PS[:, b0 * 512 + (h % 4) * 128: b0 * 512 + (h % 4 + 1) * 128]
                mm(o, st[:, h, :], VR[:, c, h * 128:(h + 1) * 128], True, False, [stk, "VR%d" % c], ["ps%d" % b0])
                mm(o, qk[:, 0 + par, m * 128:(m + 1) * 128], AFB[:, c, m, :], False, False, [qkk, "SF%d" % c], ["ps%d" % b0])
                mm(o, qk[:, 2 + par, m * 128:(m + 1) * 128], KVB[:, c + 1, m, :], False, True, [qkk, "SB%d" % (c + 1)], ["ps%d" % b0])
            ys = YS[c % 2]; ysk = "YS%d" % (c % 2)
            yst = YSTd[c % 2]; ystk = "YST%d" % (c % 2)
            ymv = YMVd[c % 2]; ymvk = "YMV%d" % (c % 2)
            act(ys[:, :], bank(6, 2), AF.Copy, ["ps6", "ps7"], [ysk])
            for h in range(8):
                R.add("dve", lambda e, o=yst[:, h, :], i=ys[:, h * 128:(h + 1) * 128]: e.bn_stats(out=o, in_=i), [ysk], [ystk])
            for h in range(8):
                R.add("dve", lambda e, o=ymv[:, h, 0:2], i=yst[:, h, :]: e.bn_aggr(out=o, in_=i), [ystk], [ymvk])

        def p2a_n1(c):
            ys = YS[c % 2]; ysk = "YS%d" % (c % 2)
            ymv = YMVd[c % 2]; ymvk = "YMV%d" % (c % 2)
            act(ymv[:, :, 2], ymv[:, :, 1], AF.Sqrt, [ymvk], [ymvk], bias=EPS)
            R.add("dve", lambda e, t=ymv[:, :, 2]: e.reciprocal(out=t, in_=t), [ymvk], [ymvk])
            stt("dve", ymv[:, :, 3], ymv[:, :, 0], -1.0, ymv[:, :, 2], ALU.mult, ALU.mult, [ymvk], [ymvk])
            for h in range(8):
                act(YN[:, h * 128:(h + 1) * 128], ys[:, h * 128:(h + 1) * 128], AF.Identity, [ysk, ymvk], ["YN"],
                    bias=ymv[:, h, 3:4], scale=ymv[:, h, 2:3])

        def p2a_n2(c):
            sg = SG[c % 3]; sgk_ = "SG%d" % (c % 3)
            rgt = RGT[c % 2]; rgk = "RGT%d" % (c % 2)
            tt("dve", rgt[:, :], YN[:, :], sg[:, :], ALU.mult, ["YN", sgk_], [rgk])

        def p2a_b2(c):
            rgt = RGT[c % 2]; rgk = "RGT%d" % (c % 2)
            pbf = bank(4).bitcast(BF16)
            for kc in range(8):
                R.add("pe", lambda e, o=pbf[:, kc * 128:(kc + 1) * 128], i=rgt[:, kc * 128:(kc + 1) * 128]: e.transpose(o, i, IDB[:, :]),
                      [rgk, "IDB"], ["ps4"])
            act(VR[:, c, :], pbf, AF.Copy, ["ps4"], ["VR%d" % c])

        w_br_v = w_br.rearrange("(kc p) n -> p kc n", p=128)

        def prefetch_2bi():
            arena(W_BASE)
            wbr = alloc("WBR", [128, 8, 1024], BF16)
            wga = alloc("WGA", [128, 8, 1024], BF16)
            for kc in range(8):
                DMA("pool", wbr[:, kc, :], w_br_v[:, kc, :], w=["W"])
                DMA("pool", wga[:, kc, :], w_in_v[:, kc, 3584:4608], w=["W"])
            return wbr, wga

        order2a = list(range(NCH - 1, -1, -1))
        p2a_a(order2a[0])
        for i_, c in enumerate(order2a):
            if i_ >= 1:
                p2a_n1(order2a[i_ - 1])
            if i_ + 1 < NCH:
                p2a_a(order2a[i_ + 1])
            if i_ == NCH - 2:
                WBR, WGA = prefetch_2bi()
            p2a_b(c)
            if i_ >= 1:
                p2a_n2(order2a[i_ - 1])
            if i_ >= 2:
                p2a_b2(order2a[i_ - 2])
        p2a_n1(order2a[-1])
        p2a_n2(order2a[-1])
        p2a_b2(order2a[-2])
        p2a_b2(order2a[-1])
        R.barrier()
        if CUT == 2:
            for c_ in range(16):
                DMA("sp", out_d[c_ * 128:(c_ + 1) * 128, :], VR[:, c_, :].bitcast(F32) if False else XB[0][:, :], r=["o"], w=["o"]) if False else None
        cut(2)

        arena(BIG_BASE + 8 * 2064 * 2 + 16 * 1024 * 2)
        MG = alloc("MG", [128, 8, 2048], BF16)
        SGA = [alloc("SGA%d" % i, [128, 512], BF16) for i in range(2)]
        P2B_END = cur[0]
        assert P2B_END <= WB_BASE, P2B_END
        assert cur[0] <= WB_BASE, cur[0]
        arena(WB_BASE)
        WP = alloc("WP", [128, 8, 512], BF16)
        WGB = alloc("WGB", [128, 8, 1024], BF16)
        WBP = alloc("WBP", [128, 4, 1024], BF16)
        PLW = alloc("PLW", [128, 4, 128], BF16)
        w_bp_v = w_bp.rearrange("(kc p) n -> p kc n", p=128)
        for kc in range(8):
            DMA("pool", WP[:, kc, :], w_in_v[:, kc, 3072:3584], w=["WB"])
            DMA("pool", WGB[:, kc, :], w_in_v[:, kc, 4608:5632], w=["WB"])
        for kc in range(4):
            DMA("pool", WBP[:, kc, :], w_bp_v[:, kc, :], w=["WB"])
            DMA("pool", PLW[:, kc, :], pool_w[kc], w=["WB"])
        it = 0
        for t4 in range(4):
            for dc in range(8):
                bx, by = 2 * (it % 4), 2 * (it % 4) + 1
                sga = SGA[it % 2]; sgk = "SGA%d" % (it % 2)
                for kc in range(8):
                    mm(bank(bx), WBR[:, kc, dc * 128:(dc + 1) * 128], VR[:, 4 * t4:4 * t4 + 4, kc * 128:(kc + 1) * 128],
                       kc == 0, kc == 7, ["W"] + ["VR%d" % (4 * t4 + q) for q in range(4)], ["ps%d" % bx])
                for kc in range(8):
                    mm(bank(by), WGA[:, kc, dc * 128:(dc + 1) * 128], UT[:, kc, 8 + t4 * 512:8 + (t4 + 1) * 512],
                       kc == 0, kc == 7, ["W", "UTall"], ["ps%d" % by])
                act(sga[:, :], bank(by), AF.Sigmoid, ["ps%d" % by], [sgk])
                tt("dve", MG[:, dc, t4 * 512:(t4 + 1) * 512], bank(bx), sga[:, :], ALU.mult, ["ps%d" % bx, sgk], ["MG"])
                it += 1
        R.barrier()
        cut(3)

        arena(W_BASE)
        WO = alloc("WO", [128, 8, 1024], BF16)
        TMPM = [alloc("TMPM%d" % i, [128, 512], F32) for i in range(1)]
        w_out_v = w_out.rearrange("(kc p) n -> p kc n", p=128)
        for kc in range(8):
            DMA("pool", WO[:, kc, :], w_out_v[:, kc, :], w=["W"])
        arena(BIG_BASE + 8 * 2064 * 2)
        PT = alloc("PT", [128, 4, 528], F32)
        TA = alloc("TA", [128, 4, 528], F32)
        TB = alloc("TB", [128, 3, 528], F32)
        DT = alloc("DT", [128, 4, 512], BF16)
        PM = alloc("PM", [128, 4, 512], BF16)
        assert cur[0] <= BIG_BASE + 8 * 2064 * 2 + 16 * 1024 * 2, cur[0]
        def p2b_P(t4):
            for g in range(4):
                for kc in range(8):
                    mm(bank(g), WP[:, kc, g * 128:(g + 1) * 128], UT[:, kc, t4 * 512:t4 * 512 + 512], kc == 0, kc == 7,
                       ["WB", "UTall"], ["ps%d" % g])
            for g in range(4):
                for kc in range(8):
                    mm(PS[:, 2048 + g * 16:2048 + (g + 1) * 16], WP[:, kc, g * 128:(g + 1) * 128],
                       UT[:, kc, t4 * 512 + 512:t4 * 512 + 528], kc == 0, kc == 7, ["WB", "UTall"], ["ps4"])
            act(PT[:, :, 0:512], PS[:, 0:2048].rearrange("p (g x) -> p g x", x=512), AF.Copy, bkeys(0, 4), ["PT"])
            act(PT[:, :, 512:528], PS[:, 2048:2112].rearrange("p (g x) -> p g x", x=16), AF.Copy, ["ps4"], ["PT"])
            tt("dve", TA[:, 0:4, 1:528], PT[:, 0:4, 0:527], PT[:, 0:4, 1:528], ALU.add, ["PT"], ["TA"])
            tt("dve", TB[:, 0:3, 2:527], TA[:, 1:4, 1:526], TA[:, 1:4, 3:528], ALU.add, ["TA"], ["TB"])
            tt("dve", TA[:, 2:4, 4:525], TB[:, 1:3, 2:523], TB[:, 1:3, 6:527], ALU.add, ["TB", "TA"], ["TA2"])
            tt("dve", TB[:, 2:3, 8:520], TA[:, 3:4, 4:516], TA[:, 3:4, 12:524], ALU.add, ["TA2", "TB"], ["TB2"])
            srcs = [TA[:, 0, 8:520], TB[:, 0, 8:520], TA[:, 2, 8:520], TB[:, 2, 8:520]]
            if t4 == 0 or t4 == 3:
                o_, w_ = _CT["EDGEL" if t4 == 0 else "EDGER"]
                for g in range(4):
                    sl = srcs[g][:, 0:8] if t4 == 0 else srcs[g][:, 504:512]
                    tt("dve", sl, sl, CT[:, o_ + g * 8:o_ + g * 8 + 8], ALU.mult, ["TA", "TB", "TA2", "TB2", "CT"], ["TA", "TB", "TA2", "TB2"])
            for g, wdw in enumerate((2, 4, 8, 16)):
                stt("dve", DT[:, g, :], srcs[g], 1.0 / wdw, PT[:, g, 8:520], ALU.mult, ALU.subtract,
                    ["TA", "TB", "TA2", "TB2", "PT"], ["DT"])

        def p2b_pre(t4):
            for g in range(4):
                mm(bank(5 + (g % 2)), PLW[:, g, :], DT[:, g, :], True, True, ["WB", "DT"], ["ps%d" % (5 + g % 2)])
                act(PM[:, g, :], bank(5 + (g % 2)), AF.Identity, ["ps%d" % (5 + g % 2), "PSC"], ["PM"], scale=PSC[:, g:g + 1])

        def p2b_dc(t4):
            for dc in range(8):
                bx, by = (0, 1) if dc % 2 == 0 else (2, 3)
                sga = SGA[dc % 2]; sgk = "SGA%d" % (dc % 2)
                tm = TMPM[0]; tmk = "TMPM0"
                for g in range(4):
                    mm(bank(bx), WBP[:, g, dc * 128:(dc + 1) * 128], PM[:, g, :], g == 0, g == 3, ["WB", "PM"], ["ps%d" % bx])
                for kc in range(8):
                    mm(bank(by), WGB[:, kc, dc * 128:(dc + 1) * 128], UT[:, kc, 8 + t4 * 512:8 + (t4 + 1) * 512],
                       kc == 0, kc == 7, ["WB", "UTall"], ["ps%d" % by])
                act(sga[:, :], bank(by), AF.Sigmoid, ["ps%d" % by], [sgk])
                tt("dve", tm[:, :], bank(bx), sga[:, :], ALU.mult, ["ps%d" % bx, sgk], [tmk])
                mgs = MG[:, dc, t4 * 512:(t4 + 1) * 512]
                tt("dve", mgs, mgs, tm[:, :], ALU.add, ["MG", tmk], ["MG"])

        p2b_P(0)
        for t4 in range(4):
            p2b_pre(t4)
            if t4 + 1 < 4:
                p2b_P(t4 + 1)
            p2b_dc(t4)
        R.barrier()
        cut(4)

        def bcast_rows(BC, gcol0, lng, lnb):
            DMA("sp", BC[:, 1, :], lng.partition_broadcast(128).rearrange("p o n -> p (o n)"), w=["BC"])
            DMA("sp", BC[:, 2, :], lnb.partition_broadcast(128).rearrange("p o n -> p (o n)"), w=["BC"])
            for kc in range(8):
                ts("dve", BC[:, 0, kc * 128:(kc + 1) * 128], ctc("IDENT"), MODT[:, gcol0 + kc, 0:1], None, ALU.mult, None,
                   ["CT", "MODT"], ["BC"])
            for kc in range(8):
                mm(PS[:, (6 + kc // 4) * 512 + (kc % 4) * 128:(6 + kc // 4) * 512 + (kc % 4 + 1) * 128],
                   ctc("ONES"), BC[:, 0, kc * 128:(kc + 1) * 128], True, True, ["CT", "BC"], ["ps%d" % (6 + kc // 4)])
            act(BC[:, 0, :], bank(6, 2), AF.Copy, ["ps6", "ps7"], ["BC"])

        def epilogue(pb, xb, xbk, BC, Z, outb, outk, zk="Z", folded=True):
            if folded:
                stt("dve", Z[:, :], xb[:, :], ALPHA, bank(pb, 2), ALU.mult, ALU.add, [xbk, "ps%d" % pb, "ps%d" % (pb + 1)], [zk])
            else:
                tt("dve", Z[:, :], bank(pb, 2), BC[:, 0, :], ALU.mult, ["ps%d" % pb, "ps%d" % (pb + 1), "BC"], [zk])
                stt("dve", Z[:, :], xb[:, :], ALPHA, Z[:, :], ALU.mult, ALU.add, [xbk, zk], [zk])
            mean, rstd, k = ln_stats(Z[:, :], 128, [zk])
            ts("dve", Z[:, :], Z[:, :], mean, rstd, ALU.subtract, ALU.mult, [zk, k], [zk])
            tt("dve", Z[:, :], Z[:, :], BC[:, 1, :], ALU.mult, [zk, "BC"], [zk])
            tt("dve", outb[:, :], Z[:, :], BC[:, 2, :], ALU.add, [zk, "BC"], [outk])

        arena(BIG_BASE)
        BC = alloc("BC", [128, 3, 1024], F32)
        Z = [alloc("Z%d" % i, [128, 1024], F32) for i in range(2)]
        XC = [alloc("XC%d" % i, [128, 1024], F32) for i in range(4)]
        XO = [alloc("XO%d" % i, [128, 1024], F32) for i in range(2)]
        assert cur[0] <= WB_BASE, cur[0]
        arena(WB_BASE)
        WD = alloc("WD", [128, 22, 1024], BF16)
        w_down_v = w_down.rearrange("(kc p) n -> p kc n", p=128)
        bcast_rows(BC, 16, ln1g, ln1b)
        for kc in range(8):
            tt("dve", WO[:, kc, :], WO[:, kc, :], BC[:, 0, :], ALU.mult, ["W", "BC"], ["W"])
        x1_target = out_d if STAGE == 2 else x1d
        order = [0, NCH - 1] + list(range(1, NCH - 1))
        for i0 in range(3):
            c0 = order[i0]
            DMA("sp", XC[i0 % 4][:, :], xs[c0 * 128:(c0 + 1) * 128, :], w=["XC%d" % (i0 % 4)])
        for it_, c in enumerate(order):
            xc = XC[it_ % 4]; xck = "XC%d" % (it_ % 4)
            xo = XO[it_ % 2]; xok = "XO%d" % (it_ % 2)
            if it_ + 3 < NCH:
                cn = order[it_ + 3]
                DMA("sp", XC[(it_ + 3) % 4][:, :], xs[cn * 128:(cn + 1) * 128, :], w=["XC%d" % ((it_ + 3) % 4)])
            if STAGE >= 3 and it_ < 11:
                for kc in (2 * it_, 2 * it_ + 1):
                    DMA("pool", WD[:, kc, :], w_down_v[:, kc, :], w=["WD%d" % kc])
            pb = 2 * (it_ % 2)
            for nb in range(2):
                for kc in range(8):
                    mm(bank(pb + nb), MG[:, kc, c * 128:(c + 1) * 128], WO[:, kc, nb * 512:(nb + 1) * 512], kc == 0, kc == 7,
                       ["W", "MG"], ["ps%d" % (pb + nb)])
            epilogue(pb, xc, xck, BC, Z[it_ % 2], xo, xok, zk="Z%d" % (it_ % 2))
            DMA("sp", x1_target[c * 128:(c + 1) * 128, :], xo[:, :], r=[xok], w=["x1d%d" % c])
            if c == 0:
                DMA("sp", ffin.ap()[0:64, :], xo[0:64, :], r=[xok], w=["ffin"])
            if c == NCH - 1:
                DMA("sp", ffin.ap()[64:128, :], xo[64:128, :], r=[xok], w=["ffin"])
                if STAGE >= 3:
                    R.add("pool", lambda e: e.collective_compute("AllGather", ALU.bypass, replica_groups=GROUPS,
                                                                 ins=[ffin.ap().opt()], outs=[ffout.ap().opt()]),
                          ["ffin"], ["ffout"], kind="cc")
        R.barrier(keep=["ffout"] + ["WD%d" % kc for kc in range(22)])

        if STAGE >= 3:
            arena(W_BASE)
            U2 = alloc("U2", [128, 8, 1152], BF16)
            GT = alloc("GT", [128, 22, 1024], BF16)
            WUP = [alloc("WUP%d" % i, [128, 8, 256], BF16) for i in range(2)]
            HA = [alloc("HA%d" % i, [128, 3, 18, 64], BF16) for i in range(1)]
            HBP = alloc("HBP", [128, 18, 66], BF16)
            ACC = alloc("ACC", [128, 1024], F32)
            ACCB = ACC[:, 0:512].bitcast(BF16)
            GA = [alloc("GA%d" % i, [128, 1024], F32) for i in range(1)]
            DG = [alloc("DG%d" % i, [128, 1, 9, 128], BF16) for i in range(1)]
            BC2 = alloc("BC2", [128, 3, 1024], F32)
            Z2 = alloc("Z2", [128, 1024], F32)
            X2 = [alloc("X2%d" % i, [128, 1024], F32) for i in range(2)]
            XN2 = [alloc("XN2%d" % i, [128, 1024], BF16) for i in range(1)]
            XO2 = [alloc("XO2%d" % i, [128, 1024], F32) for i in range(2)]
            HX = alloc("HX", [128, 1024], F32)
            X2U = [alloc("X2U%d" % i, [128, 1024], F32) for i in range(1)]
            assert cur[0] <= WB_BASE, cur[0]
            bcast_rows(BC2, 40, ln2g, ln2b)
            for i in range(1):
                R.add("pool", lambda e, t=HA[i]: e.memset(t[:, :, :, :], 0.0), [], ["HA%d" % i])
                R.add("pool", lambda e: e.memset(HBP[:, :, :], 0.0), [], ["HBP"])
            ci = 0
            pi_glob = 0
            u2ctr = [0]

            def u2_buf(hf, j):
                if hf == 0:
                    bufs = [(X2U[0], "X2U0"), (X2[0], "X20"), (X2[1], "X21")]
                else:
                    bufs = [(X2U[0], "X2U0"), (HX, "HX")]
                return bufs[j % len(bufs)]

            def u2_load(hf, j):
                x2, x2k = u2_buf(hf, j)
                row0 = 16 * hf + 2 * j
                if row0 == 0:
                    cp("pool", x2[0:64, :], HX[0:64, :], ["HX"], [x2k])
                    DMA("sp", x2[64:128, :], x1d[0:64, :], r=["x1d0"], w=[x2k])
                elif row0 == 32:
                    DMA("sp", x2[0:64, :], x1d[31 * 64:32 * 64, :], r=["x1d15"], w=[x2k])
                    DMA("sp", x2[64:128, :], hxd.ap()[64:128, :], r=["hxd"], w=[x2k])
                else:
                    t0 = (row0 - 1) * 64
                    DMA("sp", x2[:, :], x1d[t0:t0 + 128, :], r=["x1d%d" % (t0 // 128), "x1d%d" % ((t0 + 127) // 128)], w=[x2k])

            def u2_chunk(hf, j, part=0, load=True):
                x2, x2k = u2_buf(hf, j)
                row0 = 16 * hf + 2 * j
                if load and part != 2:
                    u2_load(hf, j)
                par_ = j % 2
                xnb, xnk_ = (XN2[0], "XN20") if par_ == 0 else (ACCB, "ACC")
                ln_to_T(x2[:, :], 128, [x2k], xnb, xnk_,
                        lambda kc, j=j: U2[:, kc, j * 128:(j + 1) * 128], "U2",
                        lambda kc: MODT[:, 24 + kc, 0:1], lambda kc: OPSC2[:, kc:kc + 1], 6 + par_, part=part)
                if part == 1:
                    return
                if row0 == 0:
                    ts("dve", U2[:, :, 0:64], U2[:, :, 0:64], ctc("HM", 0, 1), None, ALU.mult, None, ["U2", "CT"], ["U2"])
                if row0 == 32:
                    ts("dve", U2[:, :, 1088:1152], U2[:, :, 1088:1152], ctc("HM", 1, 2), None, ALU.mult, None, ["U2", "CT"], ["U2"])

            for hf in range(2):
                if hf == 0:
                    for j in [1, 2, 3, 4, 5, 6, 7, 8]:
                        u2_chunk(0, j)
                    R.add("dve", lambda e: e.memset(HX[:, :], 0.0), [], ["HX"])
                    ghs = [(XO2[0], "XO20"), (XO2[1], "XO21"), (Z2, "Z"), (GA[0], "GA0")]
                    for r_ in range(4):
                        gh, ghk = ghs[r_]
                        DMA("sp", gh[0:64, :], ffout.ap()[r_ * 128 + 64:r_ * 128 + 128, :], r=["ffout"], w=[ghk])
                        DMA("sp", gh[64:128, :], ffout.ap()[r_ * 128:r_ * 128 + 64, :], r=["ffout"], w=[ghk])
                    for r_ in range(4):
                        gh, ghk = ghs[r_]
                        stt("dve", HX[:, :], gh[:, :], ctc("OH", r_, r_ + 1), HX[:, :], ALU.mult, ALU.add, [ghk, "CT", "HX"], ["HX"])
                    DMA("sp", hxd.ap()[64:128, :], HX[64:128, :], r=["HX"], w=["hxd"])
                    u2_chunk(0, 0)
                for pi in range(22):
                    wu = WUP[pi_glob % 2]; wuk = "WUP%d" % (pi_glob % 2)
                    dg = DG[0]; dgk = "DG0"
                    ha = HA[0]; hak = "HA0"
                    ga = GA[0]; gak = "GA0"
                    DMA("pool", wu[:, :, 0:128], w_up_v[:, :, pi * 128:(pi + 1) * 128], w=[wuk])
                    DMA("pool", wu[:, :, 128:256], w_up_v[:, :, 2816 + pi * 128:2816 + (pi + 1) * 128], w=[wuk])
                    for t in range(9):
                        act(dg[:, 0, t, :], IDB[:, :], AF.Copy, ["IDB", "CW"], [dgk], scale=CW[:, pi, t:t + 1])
                    for ab in range(2):
                        for nb, (n0, n1) in enumerate(((0, 512), (512, 1024), (1024, 1152))):
                            bb = 3 * ab + nb
                            for kc in range(8):
                                mm(PS[:, bb * 512:bb * 512 + (n1 - n0)], wu[:, kc, ab * 128:(ab + 1) * 128], U2[:, kc, n0:n1],
                                   kc == 0, kc == 7, [wuk, "U2"], ["ps%d" % bb])
                    pa = PS[:, 0:1152].rearrange("p (r c) -> p r c", c=64)
                    pb_ = PS[:, 1536:2688].rearrange("p (r c) -> p r c", c=64)
                    act(ha[:, 1, :, :], pa, AF.Copy, bkeys(0, 3), [hak])
                    act(ha[:, 0, :, 1:64], pa[:, :, 0:63], AF.Copy, bkeys(0, 3), [hak])
                    act(ha[:, 2, :, 0:63], pa[:, :, 1:64], AF.Copy, bkeys(0, 3), [hak])
                    act(HBP[:, :, 1:65], pb_, AF.Copy, bkeys(3, 3), ["HBP"])
                    for blk in range(2):
                        bb = 6 + blk
                        for t in range(9):
                            dr, dc_ = t // 3 - 1, t % 3 - 1
                            mm(bank(bb), dg[:, 0, t, :], ha[:, dc_ + 1, 1 + 8 * blk + dr:9 + 8 * blk + dr, :],
                               t == 0, t == 8, [dgk, hak], ["ps%d" % bb])
                    chb = 22 + pi
                    accv = ACC[:, :].rearrange("p (r c) -> p r c", c=64)
                    for t in range(9):
                        dr, dc_ = t // 3 - 1, t % 3 - 1
                        win = HBP[:, 1 + dr:17 + dr, 1 + dc_:65 + dc_]
                        if t == 0:
                            act(accv, win, AF.Identity, ["HBP", "CW", "CB"], ["ACC"], bias=CB[:, chb:chb + 1], scale=CW[:, chb, 0:1])
                        else:
                            stt("dve", accv, win, CW[:, chb, t:t + 1], accv, ALU.mult, ALU.add, ["HBP", "CW", "ACC"], ["ACC"])
                    act(ga[:, :], bank(6, 2), AF.Gelu_apprx_tanh, ["ps6", "ps7"], [gak], bias=CB[:, pi:pi + 1])
                    tt("dve", GT[:, pi, :], ACC[:, :], ga[:, :], ALU.mult, ["ACC", gak], ["GT"])
                    pi_glob += 1
                if hf == 0:
                    u2_load(1, 0)
                DMA("sp", X2[ci % 2][:, :], x1d[hf * 8 * 128:(hf * 8 + 1) * 128, :], r=["x1d%d" % (hf * 8)], w=["X2%d" % (ci % 2)])
                for c in range(8):
                    gc = hf * 8 + c
                    xc = X2[ci % 2]; xck = "X2%d" % (ci % 2)
                    xo = XO2[ci % 2]; xok = "XO2%d" % (ci % 2)
                    ci += 1
                    if c + 1 < 8:
                        DMA("sp", X2[ci % 2][:, :], x1d[(gc + 1) * 128:(gc + 2) * 128, :], r=["x1d%d" % (gc + 1)], w=["X2%d" % (ci % 2)])
                    pb = 2 + 2 * (c % 2)
                    if hf == 0:
                        u2_load(1, c + 1)
                        u2_chunk(1, c, part=1, load=False)
                    for nb in range(2):
                        for kc in range(22):
                            mm(bank(pb + nb), GT[:, kc, c * 128:(c + 1) * 128], WD[:, kc, nb * 512:(nb + 1) * 512], kc == 0, kc == 21,
                               ["WD%d" % kc, "GT"], ["ps%d" % (pb + nb)])
                    if hf == 0:
                        u2_chunk(1, c, part=2)
                    epilogue(pb, xc, xck, BC2, Z2, xo, xok, folded=False)
                    DMA("sp", out_d[gc * 128:(gc + 1) * 128, :], xo[:, :], r=[xok], w=["out%d" % gc])
                if hf == 0:
                    u2_chunk(1, 8, load=False)

    except _Stop:
        pass

    if STAGE == 1:
        DMA("sp", out_d[0:128, 0:1024], SST[:, :, :, :].rearrange("p a b c -> p (a b c)"), r=["SST"], w=["o"])
        DMA("sp", out_d[128:256, 0:1024], SCTX[:, :, :, :].rearrange("p a b c -> p (a b c)"), r=["SCTXf", "SCTXb"], w=["o"])
        DMA("sp", out_d[256:384, 0:96], MODT[:, :, :].rearrange("p a b -> p (a b)"), r=["MODT"], w=["o"])
        DMA("sp", out_d[384:512, 0:16], LG[:, :], r=["LG"], w=["o"])

    with (nc.semaphore("s_pe") as s_pe, nc.semaphore("s_act") as s_act, nc.semaphore("s_dve") as s_dve,
          nc.semaphore("s_pool") as s_pool, nc.semaphore("s_sp") as s_sp, nc.semaphore("s_cc") as s_cc):
        import contextlib
        with contextlib.ExitStack() as es:
            dsems = [es.enter_context(nc.semaphore("s_d%d" % i)) for i in range(NDS)]
            with nc.Block() as block:
                R.emit(nc, block, dict(pe=s_pe, act=s_act, dve=s_dve, pool=s_pool, sp=s_sp), dsems, s_cc)
    return nc


_NC_CACHE = {}


def kernel(x, c, ctx, c_ctx, w_ada, b_ada, w_in, ret_decay_logit, pool_w, pool_scale,
           w_branch_ret, w_branch_pool, w_out, ln1_g, ln1_b, w_up, conv_w, conv_b, w_down, ln2_g, ln2_b):
    f = lambda a: np.ascontiguousarray(np.asarray(a, dtype=np.float32))
    x = f(x); ctx = f(ctx); c = f(c); c_ctx = f(c_ctx)
    if "nc" not in _NC_CACHE:
        _NC_CACHE["nc"] = build()
    nc = _NC_CACHE["nc"]
    shared = dict(
        w_ada=f(w_ada[0]), bada=f(b_ada[0].reshape(48, 128).T), w_in=f(w_in[0]), lgt=f(ret_decay_logit[0].reshape(1, 16)),
        pool_w=f(pool_w[0]), psc=f(pool_scale[0].reshape(4, 128).T), w_br=f(w_branch_ret[0]), w_bp=f(w_branch_pool[0]),
        w_out=f(w_out[0]), ln1g=f(ln1_g[0].reshape(1, D)), ln1b=f(ln1_b[0].reshape(1, D)),
        ln2g=f(ln2_g[0].reshape(1, D)), ln2b=f(ln2_b[0].reshape(1, D)), w_up=f(w_up[0]),
        cwf=f(conv_w[0].reshape(9, 44, 128).transpose(2, 1, 0).reshape(128, 44 * 9)),
        cbf=f(conv_b[0].reshape(44, 128).T), w_down=f(w_down[0]),
    )
    in_maps = []
    for core in range(NCORE):
        b, s = core // 4, core % 4
        xh = np.zeros((16, D), np.float32)
        if s > 0:
            xh[0:8] = x[b, s * SEG - 8:s * SEG]
        if s < 3:
            xh[8:16] = x[b, (s + 1) * SEG:(s + 1) * SEG + 8]
        cfm = np.concatenate([c[b].reshape(8, 128).T, c_ctx.reshape(8, 128).T], axis=1)
        m = dict(shared)
        m.update(xs=f(x[b, s * SEG:(s + 1) * SEG]), xh=xh, ctxs=f(ctx[b]), cfm=f(cfm), ct=_const_table(core))
        in_maps.append(m)
    res = run_bass_kernel_spmd(nc, in_maps, core_ids=list(range(NCORE)))
    out = np.zeros((2, 8192, D), np.float32)
    for core in range(NCORE):
        b, s = core // 4, core % 4
        out[b, s * SEG:(s + 1) * SEG] = np.asarray(res.results[core]["out"])
    return out
```
